# Optimizing a Trainium2 kernel written in Bass

```python
import jax, jax.numpy as jnp
from jax import lax
import numpy as np

D_MODEL = 2048
BATCH = 4
SEQ = 4096
DEPTH = 1
DEC_BATCH = 8
DEC_SEQ = 16
PAST_LEN = 4096

CHUNK = 64
N_META = 16
CONV_WIDTH = 1024
SHORTCONV_K = 3
N_GDN_HEADS = 8
GDN_DK = 128
GDN_DV = 128
GDN_QK = N_GDN_HEADS * GDN_DK
GDN_V = N_GDN_HEADS * GDN_DV
GDN_CONV_K = 4
QKV_WIDTH = 2 * GDN_QK + GDN_V
MIX_WIDTH = CONV_WIDTH + GDN_V
D_FF = 5632
FFN_CONV_K = 3
EPS = 1e-6
SPLITS = list(np.cumsum([CONV_WIDTH, CONV_WIDTH, CONV_WIDTH, QKV_WIDTH, GDN_V, N_GDN_HEADS]))
IN_COLS = 3 * CONV_WIDTH + QKV_WIDTH + GDN_V + 2 * N_GDN_HEADS

kernel_name = "hymba_conv_gdn_streaming_step"


def rms_norm(x, g):
    xf = x.astype(jnp.float32)
    y = xf * lax.rsqrt(jnp.mean(xf * xf, -1, keepdims=True) + EPS)
    return y * g.astype(jnp.float32)


def l2norm(x):
    return x * lax.rsqrt(jnp.sum(x * x, -1, keepdims=True) + EPS)


def causal_dwconv(x, hist, w):
    K = w.shape[0]
    T = x.shape[1]
    xp = jnp.concatenate([hist.astype(x.dtype), x], axis=1)
    y = sum(xp[:, j:j + T] * w[j] for j in range(K))
    return y, xp[:, -(K - 1):]


def gated_delta_chunked(q, k, v, g, beta, S0):
    Bn, T, H, dk = q.shape
    dv = v.shape[-1]
    n = T // CHUNK

    def to_chunks(a):
        return a.reshape(Bn, n, CHUNK, H, *a.shape[3:]).swapaxes(2, 3)

    qc, kc, vc, gc, bc = map(to_chunks, (q, k, v, g, beta))
    gcum = jnp.cumsum(gc, axis=-1)
    idx = jnp.arange(CHUNK)
    lower_incl = idx[:, None] >= idx[None, :]
    strict = idx[:, None] > idx[None, :]
    diff = gcum[..., :, None] - gcum[..., None, :]
    decay_mat = jnp.exp(jnp.where(lower_incl, diff, -jnp.inf))
    kb = kc * bc[..., None]
    M = jnp.einsum('bnhid,bnhjd->bnhij', kb, kc) * jnp.where(strict, decay_mat, 0.0)
    A = jnp.eye(CHUNK, dtype=jnp.float32) + M
    rhs = jnp.concatenate([vc * bc[..., None], kb * jnp.exp(gcum)[..., None]], axis=-1)
    sol = lax.linalg.triangular_solve(A, rhs, left_side=True, lower=True, unit_diagonal=True)
    u, w = sol[..., :dv], sol[..., dv:]
    attn_in = jnp.einsum('bnhid,bnhjd->bnhij', qc, kc) * decay_mat

    def step(S, xs):
        qn, kn, un, wn, gn, an = xs
        v_new = un - jnp.einsum('bhld,bhde->bhle', wn, S)
        o = (jnp.einsum('bhld,bhde->bhle', qn * jnp.exp(gn)[..., None], S)
             + jnp.einsum('bhij,bhje->bhie', an, v_new))
        g_last = gn[..., -1:]
        S = (S * jnp.exp(g_last)[..., None]
             + jnp.einsum('bhld,bhle->bhde', kn * jnp.exp(g_last - gn)[..., None], v_new))
        return S, o

    xs = tuple(jnp.moveaxis(a, 1, 0) for a in (qc, kc, u, w, gcum, attn_in))
    S, o = lax.scan(step, S0, xs)
    o = jnp.moveaxis(o, 0, 1).swapaxes(2, 3).reshape(Bn, T, H, dv)
    return o, S


def gdn_run(q, k, v, g, beta, S0, front):
    T = q.shape[1]
    back = (-(T + front)) % CHUNK

    def pad(a):
        return jnp.pad(a, [(0, 0), (front, back)] + [(0, 0)] * (a.ndim - 2))

    o, S = gated_delta_chunked(pad(q), pad(k), pad(v), pad(g), pad(beta), S0)
    return o[:, front:front + T], S


def layer_forward(x, hist_a, hist_qkv, S0, hist_ffn, front,
                  g_pre_mix, w_in, w_conv_a, g_norm_a, w_conv_gdn, a_log, dt_bias,
                  g_norm_gdn, w_out, g_post_mix, g_pre_ffn, w_up, w_conv_ffn, w_down, g_post_ffn):
    Bn, T, _ = x.shape
    dt = x.dtype
    h = rms_norm(x, g_pre_mix).astype(dt)
    proj = h @ w_in
    a_h, a_c, a_b, qkv, z, b_logit, a_logit = jnp.split(proj, SPLITS, axis=-1)
    conv_a, new_hist_a = causal_dwconv(a_c * a_h, hist_a, w_conv_a)
    y_a = rms_norm(a_b * conv_a, g_norm_a)
    qkv_c, new_hist_qkv = causal_dwconv(qkv, hist_qkv, w_conv_gdn)
    qkv_c = jax.nn.silu(qkv_c.astype(jnp.float32))
    q, k, v = jnp.split(qkv_c, [GDN_QK, 2 * GDN_QK], axis=-1)
    q = l2norm(q.reshape(Bn, T, N_GDN_HEADS, GDN_DK)) * (GDN_DK ** -0.5)
    k = l2norm(k.reshape(Bn, T, N_GDN_HEADS, GDN_DK))
    v = v.reshape(Bn, T, N_GDN_HEADS, GDN_DV)
    beta = jax.nn.sigmoid(b_logit.astype(jnp.float32))
    g = -jnp.exp(a_log.astype(jnp.float32)) * jax.nn.softplus(
        a_logit.astype(jnp.float32) + dt_bias.astype(jnp.float32))
    o, S = gdn_run(q, k, v, g, beta, S0.astype(jnp.float32), front)
    zf = z.astype(jnp.float32).reshape(Bn, T, N_GDN_HEADS, GDN_DV)
    y_b = (rms_norm(o, g_norm_gdn) * jax.nn.silu(zf)).reshape(Bn, T, GDN_V)
    mix = jnp.concatenate([y_a, y_b], axis=-1).astype(dt) @ w_out
    x = x + rms_norm(mix, g_post_mix).astype(dt)
    h2 = rms_norm(x, g_pre_ffn).astype(dt)
    up_g, up_v = jnp.split(h2 @ w_up, [D_FF], axis=-1)
    up_gc, new_hist_ffn = causal_dwconv(up_g, hist_ffn, w_conv_ffn)
    f = (jax.nn.silu(up_gc) * up_v) @ w_down
    x = x + rms_norm(f, g_post_ffn).astype(dt)
    return x, (new_hist_a, new_hist_qkv, S.astype(dt), new_hist_ffn)


def run_stack(x, states, front, weights):
    new = []
    for l in range(DEPTH):
        x, st = layer_forward(x, *states[l], front, *(w[l] for w in weights))
        new.append(st)
    stacked = tuple(jnp.stack([s[i] for s in new]) for i in range(4))
    return x, stacked


def setup_inputs(seed: int = 0) -> dict:
    key = jax.random.key(seed)
    ks = jax.random.split(key, 24)
    f32 = jnp.float32
    nrm = lambda k, s, sc: jax.random.normal(k, s, f32) * sc
    gain = lambda k, n: 1.0 + 0.05 * jax.random.normal(k, (DEPTH, n), f32)
    dt0 = jnp.exp(jax.random.uniform(ks[20], (DEPTH, N_GDN_HEADS), f32, np.log(1e-3), np.log(1e-1)))
    return {
        "x_prompt": nrm(ks[0], (BATCH, SEQ, D_MODEL), 1.0),
        "x_sample": nrm(ks[1], (DEC_BATCH, DEC_SEQ, D_MODEL), 1.0),
        "state_conv_a": nrm(ks[2], (DEPTH, DEC_BATCH, SHORTCONV_K - 1, CONV_WIDTH), 1.0),
        "state_gdn_conv": nrm(ks[3], (DEPTH, DEC_BATCH, GDN_CONV_K - 1, QKV_WIDTH), 1.0),
        "state_gdn": nrm(ks[4], (DEPTH, DEC_BATCH, N_GDN_HEADS, GDN_DK, GDN_DV), GDN_DK ** -0.5),
        "state_ffn_conv": nrm(ks[5], (DEPTH, DEC_BATCH, FFN_CONV_K - 1, D_FF), 1.0),
        "meta_tokens": nrm(ks[6], (N_META, D_MODEL), 1.0),
        "g_pre_mix": gain(ks[7], D_MODEL),
        "w_in": nrm(ks[8], (DEPTH, D_MODEL, IN_COLS), D_MODEL ** -0.5),
        "w_conv_a": nrm(ks[9], (DEPTH, SHORTCONV_K, CONV_WIDTH), SHORTCONV_K ** -0.5),
        "g_norm_a": gain(ks[10], CONV_WIDTH),
        "w_conv_gdn": nrm(ks[11], (DEPTH, GDN_CONV_K, QKV_WIDTH), GDN_CONV_K ** -0.5),
        "a_log": jnp.log(jax.random.uniform(ks[12], (DEPTH, N_GDN_HEADS), f32, 1.0, 16.0)),
        "dt_bias": jnp.log(jnp.expm1(dt0)),
        "g_norm_gdn": gain(ks[13], GDN_DV),
        "w_out": nrm(ks[14], (DEPTH, MIX_WIDTH, D_MODEL), MIX_WIDTH ** -0.5),
        "g_post_mix": gain(ks[15], D_MODEL),
        "g_pre_ffn": gain(ks[16], D_MODEL),
        "w_up": nrm(ks[17], (DEPTH, D_MODEL, 2 * D_FF), D_MODEL ** -0.5),
        "w_conv_ffn": nrm(ks[18], (DEPTH, FFN_CONV_K, D_FF), FFN_CONV_K ** -0.5),
        "w_down": nrm(ks[19], (DEPTH, D_FF, D_MODEL), D_FF ** -0.5),
        "g_post_ffn": gain(ks[21], D_MODEL),
    }


def reference(x_prompt, x_sample, state_conv_a, state_gdn_conv, state_gdn, state_ffn_conv,
              meta_tokens, g_pre_mix, w_in, w_conv_a, g_norm_a, w_conv_gdn, a_log, dt_bias,
              g_norm_gdn, w_out, g_post_mix, g_pre_ffn, w_up, w_conv_ffn, w_down, g_post_ffn):
    weights = (g_pre_mix, w_in, w_conv_a, g_norm_a, w_conv_gdn, a_log, dt_bias, g_norm_gdn,
               w_out, g_post_mix, g_pre_ffn, w_up, w_conv_ffn, w_down, g_post_ffn)
    dt = x_prompt.dtype
    meta = jnp.broadcast_to(meta_tokens.astype(dt)[None], (BATCH, N_META, D_MODEL))
    xp = jnp.concatenate([meta, x_prompt], axis=1)
    zero_states = [(jnp.zeros((BATCH, SHORTCONV_K - 1, CONV_WIDTH), dt),
                    jnp.zeros((BATCH, GDN_CONV_K - 1, QKV_WIDTH), dt),
                    jnp.zeros((BATCH, N_GDN_HEADS, GDN_DK, GDN_DV), jnp.float32),
                    jnp.zeros((BATCH, FFN_CONV_K - 1, D_FF), dt)) for _ in range(DEPTH)]
    yp, (nca_p, ngc_p, ngd_p, nfc_p) = run_stack(xp, zero_states, CHUNK - N_META, weights)
    y_prompt = yp[:, N_META:]
    samp_states = [(state_conv_a[l], state_gdn_conv[l], state_gdn[l], state_ffn_conv[l])
                   for l in range(DEPTH)]
    y_sample, (nca_s, ngc_s, ngd_s, nfc_s) = run_stack(x_sample, samp_states, 0, weights)
    return (y_prompt, y_sample, nca_p, ngc_p, ngd_p, nfc_p, nca_s, ngc_s, ngd_s, nfc_s)
```

```python
import numpy as np
import concourse.bass as bass
import concourse.mybir as mybir
from concourse.bass_utils import run_bass_kernel_spmd

F32 = mybir.dt.float32
BF16 = mybir.dt.bfloat16
ALU = mybir.AluOpType
AF = mybir.ActivationFunctionType

D_MODEL = 2048
SEQ = 4096
N_META = 16
CW = 1024
NH = 8
DK = 128
QKV = 3072
D_FF = 5632
IN_COLS = 7184
EPS = 1e-6
NPRE = 2048
NMAIN = 2112
NSAMP = 64
NCHUNK_ALL = (NPRE + NMAIN + NSAMP) // 64
NTMAX = 576
KD = D_MODEL // 128
KF = D_FF // 128


class Eng:
    def __init__(self, name, h, sem, is_pe=False):
        self.name, self.h, self.sem = name, h, sem
        self.cnt = 0
        self.seen = {}
        self.is_pe = is_pe


class DSem:
    def __init__(self, sem, name):
        self.sem, self.name, self.cnt = sem, name, 0


class T:
    def __init__(self, name="", psum=False):
        self.name = name
        self.w = None
        self.r = []
        self.psum = psum


class FW:
    def __init__(self, nc):
        self.nc = nc
        self.stack = []
        self.pe = self._eng("pe", nc.tensor, True)
        self.dve = self._eng("dve", nc.vector)
        self.act = self._eng("act", nc.scalar)
        self.pool = self._eng("pool", nc.gpsimd)
        self.sp = self._eng("sp", nc.sync)
        self.engs = [self.pe, self.dve, self.act, self.pool, self.sp]
        self.dsems = []
        self.ninst = 0

    def sem(self, name):
        g = self.nc.semaphore(name)
        s = g.__enter__()
        self.stack.append(g)
        return s

    def _eng(self, name, h, is_pe=False):
        return Eng(name, h, self.sem("s_" + name), is_pe)

    def dsem(self, name):
        d = DSem(self.sem("d_" + name), name)
        self.dsems.append(d)
        return d

    def _wait(self, eng, e, c):
        if eng.seen.get(e, 0) < c:
            eng.h.wait_ge(e.sem, c)
            eng.seen[e] = c
            self.ninst += 1

    def _deps(self, eng, reads, writes):
        deps = {}

        def add(p):
            if p is None:
                return
            e, c = p
            if eng.is_pe and e is eng:
                return
            if deps.get(e, 0) < c:
                deps[e] = c
        for t in reads:
            add(t.w)
            if t.psum:
                for r in t.r:
                    if r[0] is not eng:
                        add(r)
        for t in writes:
            add(t.w)
            for r in t.r:
                add(r)
        for e, c in deps.items():
            self._wait(eng, e, c)

    def _mark(self, me, reads, writes):
        for t in reads:
            t.r.append(me)
            if len(t.r) > 24:
                best = {}
                for (e, c) in t.r:
                    if best.get(e, 0) < c:
                        best[e] = c
                t.r = list(best.items())
        for t in writes:
            t.w = me
            t.r = []

    def op(self, eng, emit, reads=(), writes=()):
        self._deps(eng, reads, writes)
        ins = emit(eng.h)
        eng.cnt += 1
        ins.then_inc(eng.sem, 1)
        self.ninst += 1
        self._mark((eng, eng.cnt), reads, writes)
        return ins

    def dma(self, q, dsem, out, in_, reads=(), writes=(), waw=True, **kw):
        self._deps(q, reads, writes if waw else ())
        ins = q.h.dma_start(out=out, in_=in_, **kw)
        dsem.cnt += 16
        ins.then_inc(dsem.sem, 16)
        self.ninst += 1
        self._mark((dsem, dsem.cnt), reads, writes)

    def barrier(self):
        for a in self.engs:
            for b in self.engs:
                if a is not b and b.cnt:
                    self._wait(a, b, b.cnt)
            for d in self.dsems:
                if d.cnt:
                    self._wait(a, d, d.cnt)

    def finish(self):
        self.barrier()


def nsplit(n):
    out = []
    o = 0
    while o < n:
        m = min(512, n - o)
        out.append((o, m))
        o += m
    return out


def blocks_of(n):
    out = []
    o = 0
    while o < n:
        m = min(128, n - o)
        out.append((o, m))
        o += m
    return out


class _Stop(Exception):
    pass


class Buf:
    def __init__(self, a, name="", psum=False):
        self.a = a
        self.t = T(name, psum)


def run_interleaved(gens):
    gens = [g for g in gens if g is not None]
    while gens:
        for g in list(gens):
            try:
                next(g)
            except StopIteration:
                gens.remove(g)


def build_program(stage=99):
    def chk(n):
        if stage == n:
            raise _Stop()

    nc = bass.Bass("TRN2", target_bir_lowering=False)
    D = {}

    def din(name, shape):
        D[name] = nc.dram_tensor(name, list(shape), F32, kind="ExternalInput").ap()

    def dout(name, shape):
        D[name] = nc.dram_tensor(name, list(shape), F32, kind="ExternalOutput").ap()

    din("xpre", (NPRE, D_MODEL)); din("xmain", (NMAIN, D_MODEL)); din("xsamp", (NSAMP, D_MODEL))
    din("tmask", (NCHUNK_ALL, 64))
    din("s_conv_a", (2, CW)); din("s_gdn_conv", (3, QKV)); din("s_gdn", (NH, DK, DK)); din("s_ffn_conv", (2, D_FF))
    din("g_pre_mix", (1, D_MODEL)); din("w_in", (D_MODEL, IN_COLS)); din("w_conv_a", (3, CW))
    din("g_norm_a", (1, CW)); din("w_conv_gdn", (4, QKV)); din("a_log", (1, NH)); din("dt_bias", (1, NH))
    din("g_norm_gdn", (1, DK)); din("w_out", (D_MODEL, D_MODEL)); din("g_post_mix", (1, D_MODEL))
    din("g_pre_ffn", (1, D_MODEL)); din("w_up", (D_MODEL, 2 * D_FF)); din("w_conv_ffn", (3, D_FF))
    din("w_down", (D_FF, D_MODEL)); din("g_post_ffn", (1, D_MODEL))
    dout("y", (NMAIN - 64, D_MODEL)); dout("ys", (16, D_MODEL))
    for p in ("p", "s"):
        dout(p + "_nca", (2, CW)); dout(p + "_ngc", (3, QKV)); dout(p + "_ngd", (NH, DK, DK)); dout(p + "_nfc", (2, D_FF))
    wi_b = nc.dram_tensor("wi_b", [D_MODEL, IN_COLS], BF16, kind="Internal").ap()
    wo_b = nc.dram_tensor("wo_b", [D_MODEL, D_MODEL], BF16, kind="Internal").ap()
    wu_b = nc.dram_tensor("wu_b", [D_MODEL, 2 * D_FF], BF16, kind="Internal").ap()
    wd_b = nc.dram_tensor("wd_b", [D_FF, D_MODEL], BF16, kind="Internal").ap()
    x1s = nc.dram_tensor("x1s", [640, D_MODEL], F32, kind="Internal").ap()

    fw = FW(nc)
    pe, dve, act, pool, sp = fw.pe, fw.dve, fw.act, fw.pool, fw.sp

    def sb(name, shape, dt=F32):
        return nc.alloc_sbuf_tensor(name, list(shape), dt)

    ACC = [nc.alloc_psum_tensor(f"acc{i}", [128, 1024], F32) for i in range(4)]
    TB = [T(f"bank{i}", psum=True) for i in range(8)]
    acc_ctr = [0]

    class Acc:
        def __init__(self, i):
            self.i = i
            self.a = ACC[i]
            self.t0 = TB[2 * i]
            self.t1 = TB[2 * i + 1]
            self.tt = [self.t0, self.t1]

        def ts(self, width):
            return [self.t0] if width <= 512 else self.tt

    acc_pool = [0, 1, 2, 3]

    def next_acc():
        i = acc_pool[acc_ctr[0] % len(acc_pool)]
        acc_ctr[0] += 1
        return Acc(i)

    ident_f = sb("ident_f", [128, 128]); ident_b = sb("ident_b", [128, 128], BF16)
    ones_b = sb("ones_b", [128, 128], BF16); ones_f = sb("ones_f", [64, 128])
    U8 = sb("U8", [64, 8, 64]); SU8 = sb("SU8", [64, 8, 64]); L8 = sb("L8", [64, 8, 64]); I8 = sb("I8", [64, 8, 64])
    eps_c = sb("eps_c", [128, 1]); one_c = sb("one_c", [128, 1])
    Tc = T("consts")
    fw.op(pool, lambda e: e.memset(ident_f[:], 0.0), writes=[Tc])
    fw.op(pool, lambda e: e.affine_select(out=ident_f[:], in_=ident_f[:], pattern=[[-1, 128]], compare_op=ALU.not_equal,
                                          fill=1.0, base=0, channel_multiplier=1), reads=[Tc], writes=[Tc])
    fw.op(pool, lambda e: e.tensor_copy(ident_b[:], ident_f[:]), reads=[Tc], writes=[Tc])
    fw.op(pool, lambda e: e.memset(ones_b[:], 1.0), writes=[Tc])
    fw.op(pool, lambda e: e.memset(ones_f[:], 1.0), writes=[Tc])
    fw.op(pool, lambda e: e.memset(eps_c[:], EPS), writes=[Tc])
    fw.op(pool, lambda e: e.memset(one_c[:], 1.0), writes=[Tc])
    mhalf_c = sb("mhalf_c", [128, 1])
    fw.op(pool, lambda e: e.memset(mhalf_c[:], -0.5), writes=[Tc])
    e5 = sb("e5", [128, 8], BF16)
    fw.op(pool, lambda e: e.memset(e5[:], 0.0), writes=[Tc])
    fw.op(pool, lambda e: e.memset(e5[:, 0:1], 1.0), reads=[Tc], writes=[Tc])
    for (m, pat, cm, cmp_) in ((U8, [[0, 8], [1, 64]], -1, ALU.is_ge), (SU8, [[0, 8], [1, 64]], -1, ALU.is_gt),
                               (L8, [[0, 8], [-1, 64]], 1, ALU.is_gt), (I8, [[0, 8], [-1, 64]], 1, ALU.is_equal)):
        fw.op(pool, lambda e: e.memset(m[:], 1.0), reads=[Tc], writes=[Tc])
        fw.op(pool, lambda e: e.affine_select(out=m[:], in_=m[:], pattern=pat, compare_op=cmp_, fill=0.0, base=0,
                                              channel_multiplier=cm), reads=[Tc], writes=[Tc])
    NEGU = sb("NEGU", [64, 8, 64], BF16); NEGL = sb("NEGL", [64, 8, 64], BF16)
    fw.op(dve, lambda e: e.tensor_scalar(NEGU[:], SU8[:], 30000.0, -30000.0, ALU.mult, ALU.add), reads=[Tc], writes=[Tc])
    fw.op(dve, lambda e: e.tensor_scalar(NEGL[:], L8[:], 30000.0, -30000.0, ALU.mult, ALU.add), reads=[Tc], writes=[Tc])
    Umat = U8[:, 0, :]
    Lmat = L8[:, 0, :]
    ULb = sb("ULb", [64, 2, 64], BF16)
    fw.op(pool, lambda e: e.tensor_copy(ULb[:, 0, :], Umat), reads=[Tc], writes=[Tc])
    fw.op(pool, lambda e: e.tensor_copy(ULb[:, 1, :], Lmat), reads=[Tc], writes=[Tc])
    Umat_b = ULb[:, 0, :]
    Lmat_b = ULb[:, 1, :]

    Twi_kv, Twi_rest, Two, Twu, Twd = T("wi_kv"), T("wi_rest"), T("wo"), T("wu"), T("wd")

    cast_sems = {}
    cast_q = []

    def cast(dst, src, tr, rows, c0, c1, rstep=256, defer=False):
        if tr.name not in cast_sems:
            cast_sems[tr.name] = fw.dsem("cast_" + tr.name)
        for r0 in range(0, rows, rstep):
            item = (dst[r0:r0 + rstep, c0:c1], src[r0:r0 + rstep, c0:c1], tr)
            if defer:
                cast_q.append(item)
            else:
                cast_emit(item)

    def cast_emit(item):
        d_, s_, tr = item
        fw.dma(pool, cast_sems[tr.name], d_, s_, writes=[tr], waw=False)

    def cast_pump(n):
        for _ in range(n):
            if cast_q:
                cast_emit(cast_q.pop(0))

    d_par = fw.dsem("par")
    Tp = T("params")
    xin = sb("xin", [128, D_MODEL]); Txin = T("xin")
    tok = sb("tok", [128, D_MODEL]); Ttok = T("tok")
    pst = xin[:, 0:512].rearrange("p (q c) -> p q c", q=4); Tpst = Txin
    stT = xin[:, 512:576]
    par = sb("par", [128, 4, 128])
    tm = sb("tm", [64, NCHUNK_ALL])
    alog_r = sb("alog_r", [128, 8]); dtb_r = sb("dtb_r", [128, 8]); nA_r = sb("nA_r", [128, 8])
    fw.op(pool, lambda e: e.memset(xin[:, 0:576], 0.0), writes=[Tpst])

    d_pst = fw.dsem("pst")

    def pl(dst, src):
        fw.dma(sp, d_pst, dst, src, reads=[Tpst], writes=[Tpst])
    r128 = lambda ap1d: ap1d.rearrange("(k p) -> k p", p=128)
    pl(pst[0:16, 0, :], r128(D["g_pre_mix"][0]))
    pl(pst[16:32, 0, :], r128(D["g_pre_ffn"][0]))
    pl(pst[32:40, 0, :], r128(D["g_norm_a"][0]))
    pl(pst[40:41, 0, :], r128(D["g_norm_gdn"][0]))
    pl(pst[41:65, 0, :], D["w_conv_a"].rearrange("j (c p) -> (j c) p", p=128))
    pl(pst[65:81, 0, :], r128(D["g_post_mix"][0]))
    pl(pst[81:97, 0, :], r128(D["g_post_ffn"][0]))
    pl(pst[0:96, 1, :], D["w_conv_gdn"].rearrange("j (c p) -> (j c) p", p=128))
    wcf_rows = D["w_conv_ffn"].rearrange("j (c p) -> (j c) p", p=128)
    pl(pst[0:128, 2, :], wcf_rows[0:128, :])
    pl(pst[0:4, 3, :], wcf_rows[128:132, :])
    pl(stT[0:NCHUNK_ALL, :], D["tmask"])
    fw.dma(sp, d_par, alog_r[:], D["a_log"][0].partition_broadcast(128), writes=[Tp])
    fw.dma(sp, d_par, dtb_r[:], D["dt_bias"][0].partition_broadcast(128), writes=[Tp])
    cast(wi_b, D["w_in"], Twi_kv, D_MODEL, 4096, IN_COLS)
    a_ = next_acc()
    for q in range(4):
        fw.op(pe, lambda e: e.transpose(a_.a[:, q * 128:(q + 1) * 128], pst[:, q, :], ident_f[:, :]), reads=[Tpst, Tc], writes=[a_.t0])
    fw.op(dve, lambda e: e.tensor_copy(par[:].rearrange("p q c -> p (q c)"), a_.a[:, 0:512]), reads=[a_.t0], writes=[Tp])
    fw.op(pe, lambda e: e.transpose(a_.a[:64, 512:512 + NCHUNK_ALL], stT[:NCHUNK_ALL, :], ident_f[:NCHUNK_ALL, :NCHUNK_ALL]),
          reads=[Tpst, Tc], writes=[a_.t1])
    fw.op(dve, lambda e: e.tensor_copy(tm[:], a_.a[:64, 512:512 + NCHUNK_ALL]), reads=[a_.t1], writes=[Tp])
    gpm = par[:, 0, 0:16]; gpf = par[:, 0, 16:32]; gna = par[:, 0, 32:40]; gng = par[:, 0, 40:41]
    gqm = par[:, 0, 65:81]; gqf = par[:, 0, 81:97]

    def wca(c, j):
        return par[:, 0, 41 + j * 8 + c:42 + j * 8 + c]

    def wcg(c, j):
        return par[:, 1, j * 24 + c:j * 24 + c + 1]

    def wcf(c, j):
        r = j * KF + c
        return par[:, 2, r:r + 1] if r < 128 else par[:, 3, r - 128:r - 127]
    fw.op(act, lambda e: e.activation(nA_r[:], alog_r[:], AF.Exp), reads=[Tp], writes=[Tp])
    fw.op(dve, lambda e: e.tensor_scalar(nA_r[:], nA_r[:], -1.0, None, ALU.mult), reads=[Tp], writes=[Tp])

    cast(wi_b, D["w_in"], Twi_rest, D_MODEL, 0, 4096, rstep=128, defer=True)
    cast(wo_b, D["w_out"], Two, D_MODEL, 0, D_MODEL, rstep=256, defer=True)
    cast(wu_b, D["w_up"], Twu, D_MODEL, 0, 2 * D_FF, rstep=64, defer=True)
    cast(wd_b, D["w_down"], Twd, D_FF, 0, D_MODEL, rstep=256, defer=True)

    xsb = sb("xsb", [128, D_MODEL], BF16); Txsb = T("xsb")
    junk = xsb; Tjunk = Txsb
    hT = sb("hT", [128, KD, NTMAX], BF16); ThT = T("hT")
    yT = sb("yT", [128, KD, NTMAX], BF16); TyT = T("yT")
    class St:
        pass

    def mk_state(sfx):
        st = St()
        st.S = sb("S" + sfx, [128, NH, DK]); st.TS = T("S" + sfx)
        st.Sbf = sb("Sbf" + sfx, [128, NH, DK], BF16); st.TSbf = T("Sbf" + sfx)
        st.hA = sb("histA" + sfx, [128, 8, 2]); st.ThA = T("hA" + sfx)
        st.hG = sb("histG" + sfx, [128, 24, 3]); st.ThG = T("hG" + sfx)
        st.hF = sb("histF" + sfx, [128, KF, 2]); st.ThF = T("hF" + sfx)
        return st
    ST = [mk_state(""), mk_state("_s")]
    dSs = tok[:, 0:1024].rearrange("p (h d) -> p h d", h=NH); TdSs = Ttok

    class Seg:
        def __init__(self, t0, n, nreal, sidx, xsrc, row0, out_ap=None, out_row0=0, tok_lo=0, tok_hi=0):
            self.t0, self.n, self.nreal, self.sidx, self.xsrc, self.row0 = t0, n, nreal, sidx, xsrc, row0
            self.out_ap, self.out_row0, self.tok_lo, self.tok_hi = out_ap, out_row0, tok_lo, tok_hi
    wl = sb("wl", [128, KD, 16], BF16); Twl = T("wl")
    smt = sb("smt", [128, 160])

    def small(c0, w, name, parts=128):
        return Buf(smt[:parts, c0:c0 + w], name)
    n_ss = small(0, 1, "n_ss"); n_rs = small(1, 1, "n_rs"); t_ssc = small(2, 4, "t_ssc"); t_st = small(6, 1, "t_st"); t_rs = small(7, 1, "t_rs")
    sm_beta = [small(8 + 64 * i, 8, f"beta{i}", 64) for i in range(2)]
    sm_gg = [small(16 + 64 * i, 8, f"gg{i}", 64) for i in range(2)]
    sm_t8 = [small(24 + 64 * i, 8, f"t8{i}", 64) for i in range(2)]
    sm_egc = [small(32 + 64 * i, 8, f"egc{i}", 64) for i in range(2)]
    sm_edl = [small(40 + 64 * i, 8, f"edl{i}", 64) for i in range(2)]
    sm_cb = [small(48 + 64 * i, 8, f"cb{i}", 64) for i in range(2)]
    sm_egl = [small(56 + 64 * i, 8, f"egl{i}", 128) for i in range(2)]
    sm_gc = [small(64 + 64 * i, 8, f"gc{i}", 64) for i in range(2)]
    smb = sb("smb", [64, 2, 2, 16], BF16)
    sm_r = sb("sm_r", [64, 2, 16])
    sm_split = [Buf(smb[:, i], f"split{i}") for i in range(2)]
    sm_res = [Buf(sm_r[:, i], f"res{i}") for i in range(2)]
    NU = 3
    wun = [sb(f"wu{i}", [128, 16, 256], BF16) for i in range(NU)]
    Twun = [T(f"wun{i}") for i in range(NU)]
    d_wun = [fw.dsem(f"wun{i}") for i in range(NU)]
    d_x = fw.dsem("x"); d_o = fw.dsem("o"); d_st = fw.dsem("st"); d_x1 = fw.dsem("x1"); d_x1l = fw.dsem("x1l")
    unit_ctr = [0]

    UN = 25200
    UNI = sb("UNI", [128, UN])
    upos = [0]

    def carve(nelem_f32):
        a = upos[0]
        upos[0] += nelem_f32
        assert upos[0] <= UN, upos[0]
        return UNI[:, a:a + nelem_f32]

    h3 = "p (h t) -> p h t"

    def cf(n, name, parts=128, h=None):
        a = carve(n)[:parts]
        if h:
            a = a.rearrange(h3, h=h)
        return Buf(a, name)

    def cb16(n_bf, name, parts=128, h=None):
        a = carve(n_bf // 2).bitcast(BF16)[:parts]
        if h:
            a = a.rearrange(h3, h=h)
        return Buf(a, name)

    qnT = cb16(NH * NTMAX, "qnT", h=NH); knT = cb16(NH * NTMAX, "knT", h=NH)
    vT = cb16(NH * NTMAX, "vT", h=NH); szT = cb16(NH * NTMAX, "szT", h=NH)
    pre = [cf(NTMAX + 8, f"pre{i}") for i in range(2)]
    cvb = [cf(NTMAX, f"cvb{i}") for i in range(2)]
    qsb = [cf(NTMAX, f"qs{i}") for i in range(2)]; ahs = cf(NTMAX, "ahs")
    sqbb = [cb16(NTMAX, f"sqb{i}") for i in range(2)]
    sqb = sqbb[0]
    alias0 = upos[0]
    yraw = cf(8 * NTMAX, "yraw", h=8)
    upos[0] = alias0
    def cf2(name):
        a = carve(512).bitcast(BF16)[:64].rearrange("p (s h t) -> p s h t", s=2, h=8)
        return Buf(a, name)
    rhsG = cf2("rhsG"); rhsL = cf2("rhsL"); rhsB = cf2("rhsB")
    EG = cb16(512, "EG", 128, 8)
    DTi = cb16(512, "DTi", 64, 8); DTs = cb16(512, "DTs", 64, 8); Dm = cb16(512, "Dm", 64, 8)
    otmp = cf(512, "otmp", 128, 8); rso = cf(512, "rso", 128, 8)
    kbT = cb16(512, "kbT", 128, 8)
    Pm = [cb16(512, f"P{i}", 64, 8) for i in range(2)]
    PTm = [cb16(512, f"PT{i}", 64, 8) for i in range(2)]
    TT = cb16(512, "TT", 64, 8); TTc = cb16(512, "TTc", 64, 8)
    vnew = cb16(1024, "vnew", 64, 8); vnews = cb16(1024, "vnews", 64, 8)
    osq = cb16(512, "osq", 128, 8)
    qgT = [cb16(512, f"qgT{i}", 128, 8) for i in range(2)]
    ktk = [cb16(1024, f"ktk{i}", 64, 8) for i in range(2)]
    vtk = [cb16(1024, f"vtk{i}", 64, 8) for i in range(2)]
    TTb = [cb16(512, f"TTb{i}", 64, 8) for i in range(2)]
    attnT = [cb16(512, f"attnT{i}", 64, 8) for i in range(2)]
    negwT = [cb16(512, f"negwT{i}", 128, 8) for i in range(2)]
    mix_end = upos[0]
    upos[0] = 0
    fT = Buf(carve(KD * NTMAX).rearrange("p (m t) -> p m t", m=KD), "fT")
    actT = Buf(carve(KF * NTMAX // 2).bitcast(BF16).rearrange("p (k t) -> p k t", k=KF), "actT")
    pre2 = [cf(NTMAX + 8, f"pre2{i}") for i in range(1)]
    cv2 = [cf(NTMAX, f"cv2{i}") for i in range(1)]
    sqb2 = [cb16(NTMAX, f"sqb2{i}") for i in range(2)]
    assert max(mix_end, upos[0]) <= UN, (mix_end, upos[0])

    def load_unit(src_ap, nk, tr_src):
        i = unit_ctr[0] % NU
        unit_ctr[0] += 1
        fw.dma(sp, d_wun[i], wun[i][:, 0:nk, :], src_ap.rearrange("(k p) c -> p k c", p=128), reads=[tr_src], writes=[Twun[i]])
        return wun[i], Twun[i]

    def mm_chunk(A, ub, tub, j, src, tsrc, Nt, nk, koff=0, first=True, last=True):
        for (o, n) in nsplit(Nt):
            for k in range(nk):
                fw.op(pe, lambda e: e.matmul(A.a[:, o:o + n], ub[:, k, j * 128:(j + 1) * 128], src[:, koff + k, o:o + n],
                                             start=(first and k == 0), stop=(last and k == nk - 1)),
                      reads=[tub, tsrc], writes=[A.t0 if o < 512 else A.t1])

    def rsqrt_cols(dst, src, scale, rd, wr):
        P = dst.shape[0]
        if len(dst.shape) == 2 and dst.shape[1] == 1:
            fw.op(act, lambda e: e.activation(dst, src, AF.Identity, bias=eps_c[:P], scale=scale), reads=rd + [Tc], writes=wr)
            fw.op(pool, lambda e: e.tensor_tensor(dst, dst, mhalf_c[:P], ALU.pow), reads=wr + [Tc], writes=wr)
            return
        fw.op(act, lambda e: e.activation(dst, src, AF.Sqrt, bias=eps_c[:P], scale=scale), reads=rd + [Tc], writes=wr)
        fw.op(dve, lambda e: e.reciprocal(dst, dst), reads=wr, writes=wr)

    def rsqrt_big(dst, src, scale, rd, wr):
        P = dst.shape[0]
        fw.op(act, lambda e: e.activation(dst, src, AF.Ln, bias=eps_c[:P], scale=scale), reads=rd + [Tc], writes=wr)
        fw.op(act, lambda e: e.activation(dst, dst, AF.Exp, scale=-0.5), reads=wr, writes=wr)

    def norm_block_to_T(src_tile, tsrc, nb, gvec, dstT, tdst, t0):
        ss = n_ss.a[:nb]
        rs = n_rs.a[:nb]
        fw.op(act, lambda e: e.activation(junk[:nb], src_tile[:nb], AF.Square, accum_out=ss), reads=[tsrc], writes=[Tjunk, n_ss.t])
        rsqrt_cols(rs, ss, 1.0 / D_MODEL, [n_ss.t], [n_rs.t])
        fw.op(act, lambda e: e.activation(xsb[:nb], src_tile[:nb], AF.Copy, scale=rs), reads=[tsrc, n_rs.t], writes=[Txsb])
        for g in range(4):
            A = next_acc()
            accb = A.a[:].bitcast(BF16)[:, 0:512].rearrange("p (j t) -> p j t", j=4)
            for j in range(4):
                k = g * 4 + j
                fw.op(pe, lambda e: e.transpose(accb[:, j, :nb], xsb[:nb, k * 128:(k + 1) * 128], ident_b[:nb, :nb]),
                      reads=[Txsb, Tc], writes=[A.t0])
            fw.op(dve, lambda e: e.tensor_tensor(dstT[:, g * 4:(g + 1) * 4, t0:t0 + nb], accb[:, :, :nb],
                                                 gvec[:, g * 4:(g + 1) * 4, None].broadcast_to([128, 4, nb]), ALU.mult),
                  reads=[A.t0, Tp], writes=[tdst])

    def build_hT_gen(segs):
        for sg in segs:
            for (o, nb) in blocks_of(sg.n):
                fw.dma(sp, d_x, xin[:nb], sg.xsrc[sg.row0 + o:sg.row0 + o + nb, :], writes=[Txin])
                norm_block_to_T(xin, Txin, nb, gpm, hT, ThT, sg.t0 + o)
                yield

    def build_hT(segs):
        for _ in build_hT_gen(segs):
            pass

    def conv_segs_gen(cv, pr, K, wfn, widx, segs, hist_of, fill):
        K1 = K - 1
        base = 0
        bases = []
        for sg in segs:
            hb, th = hist_of(ST[sg.sidx])
            fw.op(pool, lambda e: e.tensor_copy(pr.a[:, base:base + K1], hb[:, widx, :]), reads=[th], writes=[pr.t])
            fill(pr.a[:, base + K1:base + K1 + sg.n], sg.t0, sg.n)
            fw.op(pool, lambda e: e.tensor_copy(hb[:, widx, :], pr.a[:, base + sg.nreal:base + sg.nreal + K1]), reads=[pr.t], writes=[th])
            bases.append(base)
            base += K1 + sg.n
        yield
        for sg, base in zip(segs, bases):
            d = cv.a[:, sg.t0:sg.t0 + sg.n]
            fw.op(dve, lambda e: e.tensor_scalar(d, pr.a[:, base:base + sg.n], wfn(widx, 0), None, ALU.mult), reads=[pr.t, Tp], writes=[cv.t])
            for j in range(1, K):
                fw.op(dve, lambda e: e.scalar_tensor_tensor(d, pr.a[:, base + j:base + j + sg.n], wfn(widx, j), d, ALU.mult, ALU.add),
                      reads=[pr.t, cv.t, Tp], writes=[cv.t])
        yield

    def conv_segs(*args):
        for _ in conv_segs_gen(*args):
            pass

    def to_token_major(srcT, o, nb, dst_tile, tdst):
        for g in range(4):
            A = next_acc()
            a3 = A.a[:, 0:512].rearrange("p (j t) -> p j t", j=4)
            for j in range(4):
                m = g * 4 + j
                fw.op(pe, lambda e: e.transpose(a3[:nb, j, :], srcT.a[:, m, o:o + nb], ident_f[:, :]),
                      reads=[srcT.t, Tc], writes=[A.t0])
            fw.op(act, lambda e: e.activation(junk[:nb, 0:512], A.a[:nb, 0:512], AF.Square, accum_out=t_ssc.a[:nb, g:g + 1]),
                  reads=[A.t0], writes=[Tjunk, t_ssc.t])
            fw.op(dve, lambda e: e.tensor_copy(dst_tile[:nb, g * 512:(g + 1) * 512], A.a[:nb, 0:512]), reads=[A.t0], writes=[tdst])

    def bc(ap2, n):
        return ap2[:, :, None].broadcast_to([ap2.shape[0], ap2.shape[1], n])

    s1ctr = [0]

    def s1_acc():
        i = s1ctr[0] % 2
        s1ctr[0] += 1
        return Acc(i)

    front_done = set()
    WARM = 0

    def s1_front(t0, gci, pb):
        c64 = slice(t0, t0 + 64)
        beta, gg, t8 = sm_beta[pb], sm_gg[pb], sm_t8[pb]
        AL = s1_acc()
        for k in range(KD):
            fw.op(pe, lambda e: e.matmul(AL.a[:64, 0:16], hT[:, k, c64], wl[:, k, :], start=(k == 0), stop=(k == KD - 1)),
                  reads=[ThT, Twl], writes=[AL.t0])
        mcol = tm[:, gci:gci + 1]
        fw.op(act, lambda e: e.activation(beta.a, AL.a[:64, 0:8], AF.Exp, scale=-1.0), reads=[AL.t0], writes=[beta.t])
        fw.op(dve, lambda e: e.tensor_tensor(t8.a, AL.a[:64, 8:16], dtb_r[:64], ALU.add), reads=[AL.t0, Tp], writes=[t8.t])
        fw.op(act, lambda e: e.activation(t8.a, t8.a, AF.Exp), reads=[t8.t], writes=[t8.t])
        fw.op(dve, lambda e: e.tensor_scalar(beta.a, beta.a, 1.0, None, ALU.add), reads=[beta.t], writes=[beta.t])
        fw.op(act, lambda e: e.activation(t8.a, t8.a, AF.Ln, bias=one_c[:64]), reads=[t8.t, Tc], writes=[t8.t])
        fw.op(dve, lambda e: e.reciprocal(beta.a, beta.a), reads=[beta.t], writes=[beta.t])
        fw.op(dve, lambda e: e.tensor_scalar(beta.a, beta.a, mcol, None, ALU.mult), reads=[beta.t, Tp], writes=[beta.t])
        fw.op(dve, lambda e: e.scalar_tensor_tensor(gg.a, t8.a, mcol, nA_r[:64], ALU.mult, ALU.mult), reads=[t8.t, Tp], writes=[gg.t])
        sp_, rs_ = sm_split[pb], sm_res[pb]
        fw.op(dve, lambda e: e.tensor_copy(sp_.a[:, 0, 0:8], gg.a), reads=[gg.t], writes=[sp_.t])
        fw.op(dve, lambda e: e.tensor_copy(sp_.a[:, 0, 8:16], beta.a), reads=[beta.t, sp_.t], writes=[sp_.t])
        fw.op(dve, lambda e: e.tensor_tensor(rs_.a[:, 0:8], gg.a, sp_.a[:, 0, 0:8], ALU.subtract), reads=[gg.t, sp_.t], writes=[rs_.t])
        fw.op(dve, lambda e: e.tensor_tensor(rs_.a[:, 8:16], beta.a, sp_.a[:, 0, 8:16], ALU.subtract), reads=[beta.t, sp_.t, rs_.t], writes=[rs_.t])
        fw.op(dve, lambda e: e.tensor_copy(sp_.a[:, 1, :], rs_.a), reads=[rs_.t, sp_.t], writes=[sp_.t])
        front_done.add(gci)

    def gdn_stage1(t0, gci, full, pb, nxt):
        c64 = slice(t0, t0 + 64)
        beta, gg, t8, egc, edl, cb_, egl, gcs = sm_beta[pb], sm_gg[pb], sm_t8[pb], sm_egc[pb], sm_edl[pb], sm_cb[pb], sm_egl[pb], sm_gc[pb]
        if gci not in front_done:
            s1_front(t0, gci, pb)
        sp_ = sm_split[pb]
        A0 = s1_acc()
        g2 = sp_.a[:, :, 0:8]
        b2 = sp_.a[:, :, 8:16]

        def bc4(m8, v2):
            return (m8[:, None, :, :].broadcast_to([64, 2, 8, 64]), v2[:, :, :, None].broadcast_to([64, 2, 8, 64]))
        fw.op(dve, lambda e: e.tensor_tensor(rhsG.a, *bc4(U8, g2), ALU.mult), reads=[sp_.t, Tc], writes=[rhsG.t])
        fw.op(dve, lambda e: e.tensor_tensor(rhsL.a, *bc4(L8, g2), ALU.mult), reads=[sp_.t, Tc], writes=[rhsL.t])
        fw.op(dve, lambda e: e.tensor_tensor(rhsB.a, *bc4(I8, b2), ALU.mult), reads=[sp_.t, Tc], writes=[rhsB.t])
        yield
        A1 = s1_acc()
        fl = "p h t -> p (h t)"
        fl4 = "p h t -> p (h t)"

        def mm2(out, lhsT, rb, neg, tw):
            for part in range(2):
                fw.op(pe, lambda e: e.matmul(out, lhsT, rb.a[:, part].rearrange(fl4), start=(part == 0), stop=(part == 1 and neg is None)),
                      reads=[rb.t, Tc], writes=[tw])
            if neg is not None:
                fw.op(pe, lambda e: e.matmul(out, ident_b[:64, :64], neg[:].rearrange(fl4), start=False, stop=True), reads=[Tc], writes=[tw])
        fw.op(pe, lambda e: e.matmul(A0.a[:64, 16:24], Umat, gg.a, start=True, stop=True), reads=[gg.t, Tc], writes=[A0.t0])
        fw.op(pe, lambda e: e.matmul(A0.a[:, 24:32], ones_f[:, :], gg.a, start=True, stop=True), reads=[gg.t, Tc], writes=[A0.t0])
        mm2(A1.a[:64, 0:512], Lmat_b, rhsG, NEGU, A1.t0)
        mm2(A1.a[:64, 512:1024], Umat_b, rhsL, NEGL, A1.t1)
        mm2(A0.a[:, 512:1024], ones_b[:64, :], rhsB, None, A0.t1)
        yield
        fw.op(act, lambda e: e.activation(egc.a, A0.a[:64, 16:24], AF.Exp), reads=[A0.t0], writes=[egc.t])
        fw.op(act, lambda e: e.copy(gcs.a, A0.a[:64, 16:24]), reads=[A0.t0], writes=[gcs.t])
        fw.op(act, lambda e: e.activation(egl.a, A0.a[:, 24:32], AF.Exp), reads=[A0.t0], writes=[egl.t])
        fw.op(dve, lambda e: e.tensor_tensor(edl.a, A0.a[:64, 24:32], gcs.a, ALU.subtract), reads=[A0.t0, gcs.t], writes=[edl.t])
        fw.op(act, lambda e: e.activation(edl.a, edl.a, AF.Exp), reads=[edl.t], writes=[edl.t])
        fw.op(dve, lambda e: e.tensor_tensor(cb_.a, beta.a, egc.a, ALU.mult), reads=[beta.t, egc.t], writes=[cb_.t])
        if full:
            mm2(A0.a[:, 0:512], ones_b[:64, :], rhsG, None, A0.t0)
        fw.op(act, lambda e: e.activation(DTs.a, A1.a[:64, 0:512].rearrange(h3, h=8), AF.Exp), reads=[A1.t0], writes=[DTs.t])
        fw.op(dve, lambda e: e.tensor_tensor(kbT.a, knT.a[:, :, c64], A0.a[:, 512:1024].rearrange(h3, h=8), ALU.mult),
              reads=[knT.t, A0.t1], writes=[kbT.t])
        fw.op(act, lambda e: e.activation(Dm.a, A1.a[:64, 512:1024].rearrange(h3, h=8), AF.Exp), reads=[A1.t1], writes=[Dm.t])
        if full:
            fw.op(act, lambda e: e.activation(EG.a, A0.a[:, 0:512].rearrange(h3, h=8), AF.Exp), reads=[A0.t0], writes=[EG.t])
            fw.op(pool, lambda e: e.tensor_tensor(DTi.a, DTs.a, I8[:], ALU.add), reads=[DTs.t, Tc], writes=[DTi.t])
            fw.op(pool, lambda e: e.tensor_tensor(qgT[pb].a, qnT.a[:, :, c64], EG.a, ALU.mult), reads=[qnT.t, EG.t], writes=[qgT[pb].t])
        yield
        B2 = s1_acc()
        b2b = B2.a[:].bitcast(BF16)
        ktok = b2b[:64, 0:1024].rearrange("p (h d) -> p h d", h=8)
        vtok = b2b[:64, 1024:2048].rearrange("p (h d) -> p h d", h=8)
        for h in range(NH):
            fw.op(pe, lambda e: e.transpose(ktok[:, h, :], knT.a[:, h, c64], ident_b[:, :]), reads=[knT.t, Tc], writes=[B2.t0])
        for h in range(NH):
            fw.op(pe, lambda e: e.transpose(vtok[:, h, :], vT.a[:, h, c64], ident_b[:, :]), reads=[vT.t, Tc], writes=[B2.t1])
        fw.op(act, lambda e: e.copy(ktk[pb].a, ktok), reads=[B2.t0], writes=[ktk[pb].t])
        fw.op(act, lambda e: e.copy(vtk[pb].a, vtok), reads=[B2.t1], writes=[vtk[pb].t])
        yield
        B0 = s1_acc()
        for h in range(NH):
            hs = slice(h * 64, (h + 1) * 64)
            fw.op(pe, lambda e: e.matmul(B0.a[:64, hs], knT.a[:, h, c64], kbT.a[:, h, :], start=True, stop=True), reads=[knT.t, kbT.t], writes=[B0.t0])
        for h in range(NH):
            fw.op(pe, lambda e: e.matmul(B0.a[:64, 512 + h * 64:512 + (h + 1) * 64], kbT.a[:, h, :], knT.a[:, h, c64], start=True, stop=True),
                  reads=[knT.t, kbT.t], writes=[B0.t1])
        fw.op(dve, lambda e: e.tensor_tensor(PTm[0].a, B0.a[:64, 0:512].rearrange(h3, h=8), DTs.a, ALU.mult), reads=[B0.t0, DTs.t], writes=[PTm[0].t])
        fw.op(dve, lambda e: e.tensor_tensor(Pm[0].a, B0.a[:64, 512:1024].rearrange(h3, h=8), Dm.a, ALU.mult), reads=[B0.t1, Dm.t], writes=[Pm[0].t])
        fw.op(dve, lambda e: e.tensor_tensor(TT.a, I8[:], PTm[0].a, ALU.subtract), reads=[PTm[0].t, Tc], writes=[TT.t])
        yield
        if full:
            B1 = s1_acc()
            for h in range(NH):
                hs = slice(h * 64, (h + 1) * 64)
                fw.op(pe, lambda e: e.matmul(B1.a[:64, hs], knT.a[:, h, c64], qnT.a[:, h, c64], start=True, stop=True), reads=[knT.t, qnT.t], writes=[B1.t0])
            fw.op(dve, lambda e: e.tensor_tensor(attnT[pb].a, B1.a[:64, 0:512].rearrange(h3, h=8), DTi.a, ALU.mult),
                  reads=[B1.t0, DTi.t], writes=[attnT[pb].t])
        if nxt is not None and full:
            s1_front(nxt[0], nxt[1], 1 - pb)
        cur = 0
        for lvl in range(5):
            nxt = 1 - cur
            C0 = s1_acc()
            for h in range(NH):
                hs = slice(h * 64, (h + 1) * 64)
                fw.op(pe, lambda e: e.matmul(C0.a[:64, hs], PTm[cur].a[:, h, :], Pm[cur].a[:, h, :], start=True, stop=True),
                      reads=[PTm[cur].t, Pm[cur].t], writes=[C0.t0])
            if lvl < 4:
                for h in range(NH):
                    fw.op(pe, lambda e: e.matmul(C0.a[:64, 512 + h * 64:512 + (h + 1) * 64], Pm[cur].a[:, h, :], PTm[cur].a[:, h, :], start=True, stop=True),
                          reads=[PTm[cur].t, Pm[cur].t], writes=[C0.t1])
            C1 = s1_acc()
            for _ in range(WARM):
                fw.op(pe, lambda e: e.matmul(C1.a[:, 512:1024], ones_b[:, :], hT[:, 0, 0:512], start=True, stop=True), reads=[ThT, Tc], writes=[C1.t1])
            fw.op(act, lambda e: e.copy(Pm[nxt].a, C0.a[:64, 0:512].rearrange(h3, h=8)), reads=[C0.t0], writes=[Pm[nxt].t])
            if lvl < 4:
                fw.op(dve, lambda e: e.tensor_copy(PTm[nxt].a, C0.a[:64, 512:1024].rearrange(h3, h=8)), reads=[C0.t1], writes=[PTm[nxt].t])
            yield
            for h in range(NH):
                hs = slice(h * 64, (h + 1) * 64)
                fw.op(pe, lambda e: e.matmul(C1.a[:64, hs], Pm[nxt].a[:, h, :], TT.a[:, h, :], start=True, stop=True),
                      reads=[Pm[nxt].t, TT.t], writes=[C1.t0])
            fw.op(dve, lambda e: e.tensor_tensor(TT.a, TT.a, C1.a[:64, 0:512].rearrange(h3, h=8), ALU.add), reads=[C1.t0, TT.t], writes=[TT.t])
            cur = nxt
        yield
        fw.op(dve, lambda e: e.tensor_tensor(TTc.a, TT.a, bc(cb_.a, 64), ALU.mult), reads=[TT.t, cb_.t], writes=[TTc.t])
        fw.op(dve, lambda e: e.tensor_tensor(TTb[pb].a, TT.a, bc(beta.a, 64), ALU.mult), reads=[TT.t, beta.t], writes=[TTb[pb].t])
        D0 = s1_acc()
        for h in range(NH):
            hs = slice(h * 64, (h + 1) * 64)
            fw.op(pe, lambda e: e.matmul(D0.a[:, hs], ktk[pb].a[:, h, :], TTc.a[:, h, :], start=True, stop=True), reads=[ktk[pb].t, TTc.t], writes=[D0.t0])
        fw.op(act, lambda e: e.mul(negwT[pb].a, D0.a[:, 0:512].rearrange(h3, h=8), -1.0), reads=[D0.t0], writes=[negwT[pb].t])
        yield

    def gdn_stage2(t0, full, pb, st):
        c64 = slice(t0, t0 + 64)
        S, Sbf, TS, TSbf = st.S, st.Sbf, st.TS, st.TSbf
        edl, egl = sm_edl[pb], sm_egl[pb]
        fw.op(pool, lambda e: e.tensor_tensor(S[:], S[:], bc(egl.a, 128), ALU.mult), reads=[egl.t, TS], writes=[TS])
        D1 = Acc(2)
        for h in range(NH):
            es = slice(h * 128, (h + 1) * 128)
            tt_ = D1.t0 if h < 4 else D1.t1
            fw.op(pe, lambda e: e.matmul(D1.a[:64, es], TTb[pb].a[:, h, :], vtk[pb].a[:, h, :], start=True, stop=False), reads=[TTb[pb].t, vtk[pb].t], writes=[tt_])
            fw.op(pe, lambda e: e.matmul(D1.a[:64, es], negwT[pb].a[:, h, :], Sbf[:, h, :], start=False, stop=True), reads=[negwT[pb].t, TSbf], writes=[tt_])
        yield
        d13 = D1.a[:64, :].rearrange("p (h d) -> p h d", h=8)
        if full:
            fw.op(act, lambda e: e.copy(vnew.a, d13), reads=D1.tt, writes=[vnew.t])
        fw.op(dve, lambda e: e.tensor_tensor(vnews.a, d13, bc(edl.a, 128), ALU.mult), reads=D1.tt + [edl.t], writes=[vnews.t])
        yield
        D3 = Acc(2)
        for h in range(NH):
            es = slice(h * 128, (h + 1) * 128)
            fw.op(pe, lambda e: e.matmul(D3.a[:, es], ktk[pb].a[:, h, :], vnews.a[:, h, :], start=True, stop=True), reads=[ktk[pb].t, vnews.t],
                  writes=[D3.t0 if h < 4 else D3.t1])
        if full:
            D2 = Acc(3)
            for h in range(NH):
                hs = slice(h * 64, (h + 1) * 64)
                fw.op(pe, lambda e: e.matmul(D2.a[:, hs], Sbf[:, h, :], qgT[pb].a[:, h, :], start=True, stop=False), reads=[TSbf, qgT[pb].t], writes=[D2.t0])
                fw.op(pe, lambda e: e.matmul(D2.a[:, hs], vnew.a[:, h, :], attnT[pb].a[:, h, :], start=False, stop=True), reads=[vnew.t, attnT[pb].t], writes=[D2.t0])
        yield
        d33 = D3.a[:, :].rearrange("p (h d) -> p h d", h=8)
        fw.op(dve, lambda e: e.tensor_tensor(Sbf[:], S[:], d33, ALU.add), reads=[TS] + D3.tt, writes=[TSbf])
        fw.op(act, lambda e: e.copy(dSs[:], d33), reads=D3.tt, writes=[TdSs])
        fw.op(pool, lambda e: e.tensor_tensor(S[:], S[:], dSs[:], ALU.add), reads=[TS, TdSs], writes=[TS])
        yield
        if full:
            o3 = D2.a[:, 0:512].rearrange(h3, h=8)
            fw.op(act, lambda e: e.activation(osq.a, o3, AF.Square), reads=[D2.t0], writes=[osq.t])
            E0 = Acc(3)
            fw.op(pe, lambda e: e.matmul(E0.a[:, 512:1024], ones_b[:, :], osq.a.rearrange("p h t -> p (h t)"), start=True, stop=True),
                  reads=[osq.t, Tc], writes=[E0.t1])
            yield
            rsqrt_big(rso.a, E0.a[:, 512:1024].rearrange(h3, h=8), 1.0 / DK, [E0.t1], [rso.t])
            fw.op(dve, lambda e: e.scalar_tensor_tensor(otmp.a, o3, gng, rso.a, ALU.mult, ALU.mult), reads=[D2.t0, Tp, rso.t], writes=[otmp.t])
            fw.op(pool, lambda e: e.tensor_tensor(yT[:, 8:16, c64], otmp.a, szT.a[:, :, c64], ALU.mult), reads=[otmp.t, szT.t], writes=[TyT])
            yield

    def gdn_chunks(chunks, full, tail_gen=None):
        fw.barrier()
        prev = None
        for ci, (t0, gci, sidx) in enumerate(chunks):
            pb = ci % 2
            cast_pump(3)
            nxt = (chunks[ci + 1][0], chunks[ci + 1][1]) if ci + 1 < len(chunks) else None
            extra = None
            if tail_gen is not None and ci == len(chunks) - 1:
                acc_pool[:] = [3]
                extra = tail_gen
            run_interleaved([gdn_stage1(t0, gci, full, pb, nxt), prev, extra])
            if extra is not None:
                acc_pool[:] = [0, 1, 2, 3]
            prev = gdn_stage2(t0, full, pb, ST[sidx])
        run_interleaved([prev])

    def gdn_heads(Nt, full, segs, pre_hook=None):
        cnt = [0]

        def bufs():
            i = cnt[0] % 2
            cnt[0] += 1
            return pre[i], cvb[i], qsb[i], sqbb[i]

        def proj(ub, tub, j):
            A = next_acc()
            mm_chunk(A, ub, tub, j, hT, ThT, Nt, KD)
            return A

        def conv_gen(A, ci, pr, cv):
            yield from conv_segs_gen(cv, pr, 4, wcg, ci, segs, lambda st: (st.hG, st.ThG),
                                     lambda dst, t0, n: fw.op(act, lambda e: e.copy(dst, A.a[:, t0:t0 + n]), reads=A.ts(Nt), writes=[pr.t]))

        def norm_gen(A, ci, pr, cv, qs, sq, out):
            yield from conv_gen(A, ci, pr, cv)
            fw.op(act, lambda e: e.activation(qs.a[:, :Nt], cv.a[:, :Nt], AF.Silu), reads=[cv.t], writes=[qs.t])
            fw.op(act, lambda e: e.activation(sq.a[:, :Nt], qs.a[:, :Nt], AF.Square), reads=[qs.t], writes=[sq.t])
            A2 = next_acc()
            for (o, n) in nsplit(Nt):
                fw.op(pe, lambda e: e.matmul(A2.a[:, o:o + n], ones_b[:, :], sq.a[:, o:o + n], start=True, stop=True),
                      reads=[sq.t, Tc], writes=[A2.t0 if o < 512 else A2.t1])
            out.append(A2)
            yield

        def norm_p2(A2, cv, qs, dst, h, sc):
            rsqrt_big(cv.a[:, :Nt], A2.a[:, :Nt], 1.0, A2.ts(Nt), [cv.t])
            fw.op(dve, lambda e: e.scalar_tensor_tensor(dst.a[:, h, :Nt], qs.a[:, :Nt], sc, cv.a[:, :Nt], ALU.mult, ALU.mult),
                  reads=[qs.t, cv.t], writes=[dst.t])

        seq = []
        for hp in range(4):
            if full:
                for j in range(2):
                    seq.append([("q", hp, j), ("k", hp, j)])
            else:
                seq.append([("k", hp, 0), ("k", hp, 1)])
            for j in range(2):
                seq.append([("v", hp, j)] + ([("z", hp, j)] if full else []))
        ubase = {"q": 12, "k": 16, "v": 20, "z": 24}
        units = {}

        def stage_a(grp):
            res = []
            for (kind, hp, j) in grp:
                if (kind, hp) not in units:
                    units[(kind, hp)] = load_unit(wi_b[:, (ubase[kind] + hp) * 256:(ubase[kind] + hp + 1) * 256], KD, Twi_rest if kind == "q" else Twi_kv)
                ub, tub = units[(kind, hp)]
                res.append(proj(ub, tub, j))
            return res

        nxt = stage_a(seq[0])
        if pre_hook is not None:
            pre_hook()
        for gi, grp in enumerate(seq):
            accs_ = nxt
            if grp[0][0] != "v" and gi + 1 < len(seq):
                pass
            if grp[0][0] in ("q", "k"):
                gens = []
                todo = []
                for (kind, hp, j), A in zip(grp, accs_):
                    h = hp * 2 + j
                    pr, cv, qs, sq = bufs()
                    cidx0, dst, sc = (0, qnT, DK ** -0.5) if kind == "q" else (8, knT, 1.0)
                    out = []
                    gens.append(norm_gen(A, cidx0 + h, pr, cv, qs, sq, out))
                    todo.append((out, cv, qs, dst, h, sc))
                run_interleaved(gens)
                if gi + 1 < len(seq):
                    nxt = stage_a(seq[gi + 1])
                for (out, cv, qs, dst, h, sc) in todo:
                    norm_p2(out[0], cv, qs, dst, h, sc)
            else:
                (kind, hp, j) = grp[0]
                h = hp * 2 + j
                pr, cv, qs, sq = bufs()
                A = accs_[0]
                run_interleaved([conv_gen(A, 16 + h, pr, cv)])
                fw.op(act, lambda e: e.activation(vT.a[:, h, :Nt], cv.a[:, :Nt], AF.Silu), reads=[cv.t], writes=[vT.t])
                if full:
                    Az = accs_[1]
                    fw.op(act, lambda e: e.activation(szT.a[:, h, :Nt], Az.a[:, :Nt], AF.Silu), reads=Az.ts(Nt), writes=[szT.t])
                if gi + 1 < len(seq):
                    nxt = stage_a(seq[gi + 1])

    def group_a(Nt, segs):
        fw.barrier()
        ssA = Acc(3)
        cnt = 0
        for cp in range(4):
            u_h, t_h = load_unit(wi_b[:, cp * 256:(cp + 1) * 256], KD, Twi_rest)
            u_c, t_c = load_unit(wi_b[:, (4 + cp) * 256:(5 + cp) * 256], KD, Twi_rest)
            u_b, t_b = load_unit(wi_b[:, (8 + cp) * 256:(9 + cp) * 256], KD, Twi_rest)
            for j in range(2):
                c = cp * 2 + j
                a_h, a_c, a_b = Acc(0), Acc(1), Acc(2)
                mm_chunk(a_h, u_h, t_h, j, hT, ThT, Nt, KD)
                mm_chunk(a_c, u_c, t_c, j, hT, ThT, Nt, KD)
                mm_chunk(a_b, u_b, t_b, j, hT, ThT, Nt, KD)
                pr = pre[cnt % 2]; cv = cvb[0]; cnt += 1
                fw.op(act, lambda e: e.copy(ahs.a[:, :Nt], a_h.a[:, :Nt]), reads=a_h.ts(Nt), writes=[ahs.t])
                conv_segs(cv, pr, 3, wca, c, segs, lambda st: (st.hA, st.ThA),
                          lambda dst, t0, n: fw.op(dve, lambda e: e.tensor_tensor(dst, a_c.a[:, t0:t0 + n], ahs.a[:, t0:t0 + n], ALU.mult),
                                                   reads=a_c.ts(Nt) + [ahs.t], writes=[pr.t]))
                fw.op(dve, lambda e: e.tensor_tensor(yraw.a[:, c, :Nt], a_b.a[:, :Nt], cv.a[:, :Nt], ALU.mult), reads=a_b.ts(Nt) + [cv.t], writes=[yraw.t])
                fw.op(act, lambda e: e.activation(sqb.a[:, :Nt], yraw.a[:, c, :Nt], AF.Square), reads=[yraw.t], writes=[sqb.t])
                for (o, n) in nsplit(Nt):
                    fw.op(pe, lambda e: e.matmul(ssA.a[:, o:o + n], ones_b[:, :], sqb.a[:, o:o + n], start=(c == 0), stop=(c == 7)),
                          reads=[sqb.t, Tc], writes=[ssA.t0 if o < 512 else ssA.t1])
        acc_ctr[0] = 0

        def tail():
            rsqrt_cols(ahs.a[:, :Nt], ssA.a[:, :Nt], 1.0 / CW, ssA.ts(Nt), [ahs.t])
            for c in range(8):
                fw.op(dve, lambda e: e.scalar_tensor_tensor(yT[:, c, :Nt], yraw.a[:, c, :Nt], gna[:, c:c + 1], ahs.a[:, :Nt], ALU.mult, ALU.mult),
                      reads=[yraw.t, ahs.t, Tp], writes=[TyT])
        return tail

    Tx1 = T("x1s")

    def out_and_ffn(Nt, segs):
        fw.barrier()
        blks = []
        for sg in segs:
            for (o, nb) in blocks_of(sg.n):
                blks.append((sg.t0 + o, nb))
        acc_pool[:] = [0, 1, 2]
        SSP = Acc(3)

        def proj_evac(A, m, gvec):
            sq = sqb2[m % 2]
            fw.op(act, lambda e: e.activation(sq.a[:, :Nt], A.a[:, :Nt], AF.Square), reads=A.ts(Nt), writes=[sq.t])
            fw.op(act, lambda e: e.activation(fT.a[:, m, :Nt], A.a[:, :Nt], AF.Copy, scale=gvec[:, m:m + 1]), reads=A.ts(Nt) + [Tp], writes=[fT.t])
            for bi, (o, nb) in enumerate(blks):
                if m == 0 and bi == 0:
                    fw.op(pe, lambda e: e.matmul(SSP.a[:nb, 512:517], sq.a[:, o:o + nb], e5[:, 0:5], start=True, stop=False),
                          reads=[sq.t, Tc], writes=[SSP.t1])
                else:
                    fw.op(pe, lambda e: e.matmul(SSP.a[:nb, 512 + bi:513 + bi], sq.a[:, o:o + nb], ones_b[:, 0:1], start=False,
                                                 stop=(m == KD - 1 and bi == len(blks) - 1)),
                          reads=[sq.t, Tc], writes=[SSP.t1])

        def token_major_resid(bi, o, nb, res=xin, tres=Txin):
            rsqrt_cols(t_rs.a[:nb], SSP.a[:nb, 512 + bi:513 + bi], 1.0 / D_MODEL, [SSP.t1], [t_rs.t])
            for g in range(4):
                A = next_acc()
                a3 = A.a[:, 0:512].rearrange("p (j t) -> p j t", j=4)
                for j in range(4):
                    m = g * 4 + j
                    fw.op(pe, lambda e: e.transpose(a3[:nb, j, :], fT.a[:, m, o:o + nb], ident_f[:, :]), reads=[fT.t, Tc], writes=[A.t0])
                fw.op(dve, lambda e: e.scalar_tensor_tensor(tok[:nb, g * 512:(g + 1) * 512], A.a[:nb, 0:512], t_rs.a[:nb],
                                                            res[:nb, g * 512:(g + 1) * 512], ALU.mult, ALU.add),
                      reads=[A.t0, t_rs.t, tres], writes=[Ttok])

        for up in range(8):
            ub, tub = load_unit(wo_b[:, up * 256:(up + 1) * 256], KD, Two)
            for j in range(2):
                m = up * 2 + j
                A = next_acc()
                mm_chunk(A, ub, tub, j, yT, TyT, Nt, KD)
                proj_evac(A, m, gqm)
        bsrc = []
        for sg in segs:
            for (o, nb) in blocks_of(sg.n):
                bsrc.append((sg, o))
        for bi, (o, nb) in enumerate(blks):
            sg, so = bsrc[bi]
            fw.dma(sp, d_x, xin[:nb], sg.xsrc[sg.row0 + so:sg.row0 + so + nb, :], writes=[Txin])
            token_major_resid(bi, o, nb)
            fw.dma(pool, d_x1, x1s[o:o + nb, :], tok[:nb], reads=[Ttok], writes=[Tx1])
            norm_block_to_T(tok, Ttok, nb, gpf, hT, ThT, o)
        cnt = 0
        for up in range(KF // 2):
            ug, tug = load_unit(wu_b[:, up * 256:(up + 1) * 256], KD, Twu)
            uv, tuv = load_unit(wu_b[:, D_FF + up * 256:D_FF + (up + 1) * 256], KD, Twu)
            for j in range(2):
                idx = up * 2 + j
                ag = next_acc(); av = next_acc()
                mm_chunk(ag, ug, tug, j, hT, ThT, Nt, KD)
                mm_chunk(av, uv, tuv, j, hT, ThT, Nt, KD)
                pr = pre2[0]; cv = cv2[0]; cnt += 1
                conv_segs(cv, pr, 3, wcf, idx, segs, lambda st: (st.hF, st.ThF),
                          lambda dst, t0, n: fw.op(act, lambda e: e.copy(dst, ag.a[:, t0:t0 + n]), reads=ag.ts(Nt), writes=[pr.t]))
                fw.op(act, lambda e: e.activation(cv.a[:, :Nt], cv.a[:, :Nt], AF.Silu), reads=[cv.t], writes=[cv.t])
                fw.op(dve, lambda e: e.tensor_tensor(actT.a[:, idx, :Nt], av.a[:, :Nt], cv.a[:, :Nt], ALU.mult), reads=av.ts(Nt) + [cv.t], writes=[actT.t])
        kparts = ((0, 16), (16, 16), (32, 12))
        for mp in range(8):
            a_m = [next_acc(), next_acc()]
            for pi, (k0, nk) in enumerate(kparts):
                ub, tub = load_unit(wd_b[k0 * 128:(k0 + nk) * 128, mp * 256:(mp + 1) * 256], nk, Twd)
                for j in range(2):
                    mm_chunk(a_m[j], ub, tub, j, actT.a, actT.t, Nt, nk, koff=k0, first=(pi == 0), last=(pi == 2))
            for j in range(2):
                proj_evac(a_m[j], mp * 2 + j, gqf)
        def final_gen():
            for bi, (o, nb) in enumerate(blks):
                fw.dma(sp, d_x1l, tok[:nb], x1s[o:o + nb, :], reads=[Tx1], writes=[Ttok])
                token_major_resid(bi, o, nb, tok, Ttok)
                sg, so = bsrc[bi]
                lo = max(0, sg.tok_lo - so)
                hi = min(nb, sg.tok_hi - so)
                if hi > lo:
                    r0 = sg.out_row0 + so + lo
                    fw.dma(pool, d_o, sg.out_ap[r0:r0 + hi - lo, :], tok[lo:hi], reads=[Ttok])
                yield
            acc_pool[:] = [0, 1, 2, 3]
        return final_gen()

    hT_prebuilt = [False]

    def tile_prefix(row0, Nt, gci0, next_segs):
        chk(1)
        segs = [Seg(0, Nt, Nt, 0, D["xpre"], row0)]
        if not hT_prebuilt[0]:
            build_hT(segs)
        hT_prebuilt[0] = False
        gdn_heads(Nt, False, segs)
        chk(3)
        gdn_chunks([(ci * 64, gci0 + ci, 0) for ci in range(Nt // 64)], False, build_hT_gen(next_segs))
        hT_prebuilt[0] = True
        chk(5)

    pending_final = [None]

    def tile_full(Nt, segs, chunks):
        chk(6)
        if hT_prebuilt[0]:
            hT_prebuilt[0] = False
        else:
            run_interleaved([pending_final[0], build_hT_gen(segs)])
        pending_final[0] = None
        ga_tail = group_a(Nt, segs)
        chk(7)
        gdn_heads(Nt, True, segs, ga_tail)
        chk(8)
        gdn_chunks(chunks, True)
        chk(11)
        pending_final[0] = out_and_ffn(Nt, segs)
        chk(10)

    stO = tok[:, 1024:1200]; TstO = Ttok
    stI = tok[:88, 1280:1536].rearrange("p (a c) -> p a c", a=2); TstI = Ttok

    def hist_specs(st):
        return ((st.hA, st.ThA, 2, 8, 0), (st.hG, st.ThG, 3, 24, 16), (st.hF, st.ThF, 2, KF, 88))

    def store_states(prefix, st):
        for (hb, th, K1, C, c0) in hist_specs(st):
            fw.op(pool, lambda e: e.tensor_copy(stO[:, c0:c0 + K1 * C].rearrange("p (t c) -> p c t", t=K1), hb[:]), reads=[th, TstO], writes=[TstO])
        A = next_acc()
        fw.op(pe, lambda e: e.transpose(A.a[:88, 0:128], stO[:, 0:88], ident_f[:, :]), reads=[TstO, Tc], writes=[A.t0])
        fw.op(pe, lambda e: e.transpose(A.a[:88, 128:256], stO[:, 88:176], ident_f[:, :]), reads=[TstO, Tc], writes=[A.t0])
        fw.op(dve, lambda e: e.tensor_copy(stI[:].rearrange("p a c -> p (a c)"), A.a[:88, 0:256]), reads=[A.t0, TstI], writes=[TstI])
        fw.dma(pool, d_st, D[prefix + "_nca"].rearrange("t (c p) -> (t c) p", p=128), stI[0:16, 0, :], reads=[TstI])
        fw.dma(pool, d_st, D[prefix + "_ngc"].rearrange("t (c p) -> (t c) p", p=128), stI[16:88, 0, :], reads=[TstI])
        fw.dma(pool, d_st, D[prefix + "_nfc"].rearrange("t (c p) -> (t c) p", p=128), stI[0:88, 1, :], reads=[TstI])
        fw.dma(pool, d_st, D[prefix + "_ngd"].rearrange("h d e -> d h e"), st.S[:], reads=[st.TS])
        fw.barrier()

    def load_states(st):
        d_ld = fw.dsem("ld")
        fw.dma(sp, d_ld, stI[0:16, 0, :], D["s_conv_a"].rearrange("t (c p) -> (t c) p", p=128), reads=[TstI], writes=[TstI])
        fw.dma(sp, d_ld, stI[16:88, 0, :], D["s_gdn_conv"].rearrange("t (c p) -> (t c) p", p=128), reads=[TstI], writes=[TstI])
        fw.dma(sp, d_ld, stI[0:88, 1, :], D["s_ffn_conv"].rearrange("t (c p) -> (t c) p", p=128), reads=[TstI], writes=[TstI])
        fw.dma(sp, fw.dsem("ldS"), st.S[:], D["s_gdn"].rearrange("h d e -> d h e"), reads=[st.TS], writes=[st.TS])
        A = next_acc()
        fw.op(pe, lambda e: e.transpose(A.a[:, 0:88], stI[:, 0, :], ident_f[:88, :88]), reads=[TstI, Tc], writes=[A.t0])
        fw.op(pe, lambda e: e.transpose(A.a[:, 88:176], stI[:, 1, :], ident_f[:88, :88]), reads=[TstI, Tc], writes=[A.t0])
        for (hb, th, K1, C, c0) in hist_specs(st):
            fw.op(dve, lambda e: e.tensor_copy(hb[:], A.a[:, c0:c0 + K1 * C].rearrange("p (t c) -> p c t", t=K1)), reads=[A.t0, th], writes=[th])
        fw.op(act, lambda e: e.copy(st.Sbf[:], st.S[:]), reads=[st.TS], writes=[st.TSbf])

    try:
        fw.dma(sp, fw.dsem("wl"), wl[:], wi_b[:, 7168:7184].rearrange("(k p) c -> p k c", p=128), reads=[Twi_kv], writes=[Twl])
        load_states(ST[1])
        p0 = ST[0]
        fw.op(pool, lambda e: e.memset(p0.hA[:], 0.0), writes=[p0.ThA])
        fw.op(pool, lambda e: e.memset(p0.hG[:], 0.0), writes=[p0.ThG])
        fw.op(pool, lambda e: e.memset(p0.hF[:], 0.0), writes=[p0.ThF])
        fw.op(pool, lambda e: e.memset(p0.S[:], 0.0), writes=[p0.TS])
        fw.op(pool, lambda e: e.memset(p0.Sbf[:], 0.0), writes=[p0.TSbf])
        gci = 0
        npre_t = NPRE // 512
        main0_segs = [Seg(0, 576, 576, 0, D["xmain"], 0, D["y"], -64, 64, 576)]
        for i in range(npre_t):
            nsegs = [Seg(0, 512, 512, 0, D["xpre"], (i + 1) * 512)] if i + 1 < npre_t else main0_segs
            tile_prefix(i * 512, 512, gci, nsegs)
            gci += 8
        cast_pump(10000)
        row = 0
        for ti, n_main in enumerate((576, 512, 512, 512)):
            segs = [Seg(0, n_main, n_main, 0, D["xmain"], row, D["y"], row - 64, max(0, 64 - row), n_main)]
            chunks = [(ci * 64, gci + ci, 0) for ci in range(n_main // 64)]
            Nt = n_main
            if ti == 3:
                segs.append(Seg(n_main, NSAMP, 16, 1, D["xsamp"], 0, D["ys"], 0, 0, 16))
                chunks.append((n_main, NCHUNK_ALL - 1, 1))
                Nt = n_main + NSAMP
            tile_full(Nt, segs, chunks)
            gci += n_main // 64
            row += n_main
        run_interleaved([pending_final[0]])
        fw.barrier()
        store_states("p", ST[0])
        store_states("s", ST[1])
    except _Stop:
        pass
    fw.finish()
    return nc, fw


_CACHE = {}


def kernel(x_prompt, x_sample, state_conv_a, state_gdn_conv, state_gdn, state_ffn_conv, meta_tokens,
           g_pre_mix, w_in, w_conv_a, g_norm_a, w_conv_gdn, a_log, dt_bias, g_norm_gdn, w_out, g_post_mix,
           g_pre_ffn, w_up, w_conv_ffn, w_down, g_post_ffn):
    f = lambda a: np.ascontiguousarray(np.asarray(a, dtype=np.float32))
    x_prompt, x_sample, meta = f(x_prompt), f(x_sample), f(meta_tokens)
    if "nc" not in _CACHE:
        _CACHE["nc"] = build_program()[0]
    nc = _CACHE["nc"]
    shared = {"g_pre_mix": f(g_pre_mix), "w_in": f(w_in[0]), "w_conv_a": f(w_conv_a[0]), "g_norm_a": f(g_norm_a),
              "w_conv_gdn": f(w_conv_gdn[0]), "a_log": f(a_log), "dt_bias": f(dt_bias), "g_norm_gdn": f(g_norm_gdn),
              "w_out": f(w_out[0]), "g_post_mix": f(g_post_mix), "g_pre_ffn": f(g_pre_ffn), "w_up": f(w_up[0]),
              "w_conv_ffn": f(w_conv_ffn[0]), "w_down": f(w_down[0]), "g_post_ffn": f(g_post_ffn)}
    in_maps = []
    zeros48 = np.zeros((48, D_MODEL), np.float32)
    for c in range(8):
        b, half = c // 2, c % 2
        full = np.concatenate([zeros48, meta, x_prompt[b]], axis=0)
        tmask = np.ones((NCHUNK_ALL, 64), np.float32)
        if half == 0:
            xpre = np.zeros((NPRE, D_MODEL), np.float32)
            xmain = full[0:NMAIN]
            tmask[32, :48] = 0.0
        else:
            xpre = full[0:NPRE]
            xmain = full[NPRE:NPRE + NMAIN]
            tmask[0, :48] = 0.0
        xsamp = np.concatenate([x_sample[c], np.zeros((48, D_MODEL), np.float32)], axis=0)
        tmask[65, 16:] = 0.0
        m = dict(shared)
        m.update({"xpre": np.ascontiguousarray(xpre), "xmain": np.ascontiguousarray(xmain), "xsamp": xsamp, "tmask": tmask,
                  "s_conv_a": f(state_conv_a[0, c]), "s_gdn_conv": f(state_gdn_conv[0, c]), "s_gdn": f(state_gdn[0, c]),
                  "s_ffn_conv": f(state_ffn_conv[0, c])})
        in_maps.append(m)
    res = run_bass_kernel_spmd(nc, in_maps, core_ids=list(range(8)))
    R = res.results
    y_prompt = np.stack([np.concatenate([R[2 * b]["y"], R[2 * b + 1]["y"]], axis=0) for b in range(4)])
    y_sample = np.stack([R[c]["ys"] for c in range(8)])

    def st(key, cores):
        return np.stack([R[c][key] for c in cores])[None]
    pc = [1, 3, 5, 7]
    sc = list(range(8))
    return (y_prompt, y_sample,
            st("p_nca", pc), st("p_ngc", pc), st("p_ngd", pc), st("p_nfc", pc),
            st("s_nca", sc), st("s_ngc", sc), st("s_ngd", sc), st("s_nfc", sc))
```

```python
import numpy as np
import concourse.bass as bass
import concourse.mybir as mybir
from concourse.bass_utils import run_bass_kernel_spmd

F32 = mybir.dt.float32
BF16 = mybir.dt.bfloat16
ALU = mybir.AluOpType
AF = mybir.ActivationFunctionType

D_MODEL = 2048
SEQ = 4096
N_META = 16
CW = 1024
NH = 8
DK = 128
QKV = 3072
D_FF = 5632
IN_COLS = 7184
EPS = 1e-6
NPRE = 2048
NMAIN = 2112
NSAMP = 64
NCHUNK_ALL = (NPRE + NMAIN + NSAMP) // 64
NTMAX = 576
KD = D_MODEL // 128
KF = D_FF // 128


class Eng:
    def __init__(self, name, h, sem, is_pe=False):
        self.name, self.h, self.sem = name, h, sem
        self.cnt = 0
        self.seen = {}
        self.is_pe = is_pe


class DSem:
    def __init__(self, sem, name):
        self.sem, self.name, self.cnt = sem, name, 0


class T:
    def __init__(self, name="", psum=False):
        self.name = name
        self.w = None
        self.r = []
        self.psum = psum


class FW:
    def __init__(self, nc):
        self.nc = nc
        self.stack = []
        self.pe = self._eng("pe", nc.tensor, True)
        self.dve = self._eng("dve", nc.vector)
        self.act = self._eng("act", nc.scalar)
        self.pool = self._eng("pool", nc.gpsimd)
        self.sp = self._eng("sp", nc.sync)
        self.engs = [self.pe, self.dve, self.act, self.pool, self.sp]
        self.dsems = []
        self.ninst = 0

    def sem(self, name):
        g = self.nc.semaphore(name)
        s = g.__enter__()
        self.stack.append(g)
        return s

    def _eng(self, name, h, is_pe=False):
        return Eng(name, h, self.sem("s_" + name), is_pe)

    def dsem(self, name):
        d = DSem(self.sem("d_" + name), name)
        self.dsems.append(d)
        return d

    def _wait(self, eng, e, c):
        if eng.seen.get(e, 0) < c:
            eng.h.wait_ge(e.sem, c)
            eng.seen[e] = c
            self.ninst += 1

    def _deps(self, eng, reads, writes):
        deps = {}

        def add(p):
            if p is None:
                return
            e, c = p
            if eng.is_pe and e is eng:
                return
            if deps.get(e, 0) < c:
                deps[e] = c
        for t in reads:
            add(t.w)
            if t.psum:
                for r in t.r:
                    if r[0] is not eng:
                        add(r)
        for t in writes:
            add(t.w)
            for r in t.r:
                add(r)
        for e, c in deps.items():
            self._wait(eng, e, c)

    def _mark(self, me, reads, writes):
        for t in reads:
            t.r.append(me)
            if len(t.r) > 24:
                best = {}
                for (e, c) in t.r:
                    if best.get(e, 0) < c:
                        best[e] = c
                t.r = list(best.items())
        for t in writes:
            t.w = me
            t.r = []

    def op(self, eng, emit, reads=(), writes=()):
        self._deps(eng, reads, writes)
        ins = emit(eng.h)
        eng.cnt += 1
        ins.then_inc(eng.sem, 1)
        self.ninst += 1
        self._mark((eng, eng.cnt), reads, writes)
        return ins

    def dma(self, q, dsem, out, in_, reads=(), writes=(), waw=True, **kw):
        self._deps(q, reads, writes if waw else ())
        ins = q.h.dma_start(out=out, in_=in_, **kw)
        dsem.cnt += 16
        ins.then_inc(dsem.sem, 16)
        self.ninst += 1
        self._mark((dsem, dsem.cnt), reads, writes)

    def barrier(self):
        for a in self.engs:
            for b in self.engs:
                if a is not b and b.cnt:
                    self._wait(a, b, b.cnt)
            for d in self.dsems:
                if d.cnt:
                    self._wait(a, d, d.cnt)

    def finish(self):
        self.barrier()


def nsplit(n):
    out = []
    o = 0
    while o < n:
        m = min(512, n - o)
        out.append((o, m))
        o += m
    return out


def blocks_of(n):
    out = []
    o = 0
    while o < n:
        m = min(128, n - o)
        out.append((o, m))
        o += m
    return out


class _Stop(Exception):
    pass


class Buf:
    def __init__(self, a, name="", psum=False):
        self.a = a
        self.t = T(name, psum)


def run_interleaved(gens):
    gens = [g for g in gens if g is not None]
    while gens:
        for g in list(gens):
            try:
                next(g)
            except StopIteration:
                gens.remove(g)


def build_program(stage=99):
    def chk(n):
        if stage == n:
            raise _Stop()

    nc = bass.Bass("TRN2", target_bir_lowering=False)
    D = {}

    def din(name, shape):
        D[name] = nc.dram_tensor(name, list(shape), F32, kind="ExternalInput").ap()

    def dout(name, shape):
        D[name] = nc.dram_tensor(name, list(shape), F32, kind="ExternalOutput").ap()

    din("xpre", (NPRE, D_MODEL)); din("xmain", (NMAIN, D_MODEL)); din("xsamp", (NSAMP, D_MODEL))
    din("tmask", (NCHUNK_ALL, 64))
    din("s_conv_a", (2, CW)); din("s_gdn_conv", (3, QKV)); din("s_gdn", (NH, DK, DK)); din("s_ffn_conv", (2, D_FF))
    din("g_pre_mix", (1, D_MODEL)); din("w_in", (D_MODEL, IN_COLS)); din("w_conv_a", (3, CW))
    din("g_norm_a", (1, CW)); din("w_conv_gdn", (4, QKV)); din("a_log", (1, NH)); din("dt_bias", (1, NH))
    din("g_norm_gdn", (1, DK)); din("w_out", (D_MODEL, D_MODEL)); din("g_post_mix", (1, D_MODEL))
    din("g_pre_ffn", (1, D_MODEL)); din("w_up", (D_MODEL, 2 * D_FF)); din("w_conv_ffn", (3, D_FF))
    din("w_down", (D_FF, D_MODEL)); din("g_post_ffn", (1, D_MODEL))
    dout("y", (NMAIN - 64, D_MODEL)); dout("ys", (16, D_MODEL))
    for p in ("p", "s"):
        dout(p + "_nca", (2, CW)); dout(p + "_ngc", (3, QKV)); dout(p + "_ngd", (NH, DK, DK)); dout(p + "_nfc", (2, D_FF))
    wi_b = nc.dram_tensor("wi_b", [D_MODEL, IN_COLS], BF16, kind="Internal").ap()
    wo_b = nc.dram_tensor("wo_b", [D_MODEL, D_MODEL], BF16, kind="Internal").ap()
    wu_b = nc.dram_tensor("wu_b", [D_MODEL, 2 * D_FF], BF16, kind="Internal").ap()
    wd_b = nc.dram_tensor("wd_b", [D_FF, D_MODEL], BF16, kind="Internal").ap()
    x1s = nc.dram_tensor("x1s", [640, D_MODEL], F32, kind="Internal").ap()

    fw = FW(nc)
    pe, dve, act, pool, sp = fw.pe, fw.dve, fw.act, fw.pool, fw.sp

    def sb(name, shape, dt=F32):
        return nc.alloc_sbuf_tensor(name, list(shape), dt)

    ACC = [nc.alloc_psum_tensor(f"acc{i}", [128, 1024], F32) for i in range(4)]
    TB = [T(f"bank{i}", psum=True) for i in range(8)]
    acc_ctr = [0]

    class Acc:
        def __init__(self, i):
            self.i = i
            self.a = ACC[i]
            self.t0 = TB[2 * i]
            self.t1 = TB[2 * i + 1]
            self.tt = [self.t0, self.t1]

        def ts(self, width):
            return [self.t0] if width <= 512 else self.tt

    acc_pool = [0, 1, 2, 3]

    def next_acc():
        i = acc_pool[acc_ctr[0] % len(acc_pool)]
        acc_ctr[0] += 1
        return Acc(i)

    ident_f = sb("ident_f", [128, 128]); ident_b = sb("ident_b", [128, 128], BF16)
    ones_b = sb("ones_b", [128, 128], BF16); ones_f = sb("ones_f", [64, 128])
    U8 = sb("U8", [64, 8, 64]); SU8 = sb("SU8", [64, 8, 64]); L8 = sb("L8", [64, 8, 64]); I8 = sb("I8", [64, 8, 64])
    eps_c = sb("eps_c", [128, 1]); one_c = sb("one_c", [128, 1])
    Tc = T("consts")
    fw.op(pool, lambda e: e.memset(ident_f[:], 0.0), writes=[Tc])
    fw.op(pool, lambda e: e.affine_select(out=ident_f[:], in_=ident_f[:], pattern=[[-1, 128]], compare_op=ALU.not_equal,
                                          fill=1.0, base=0, channel_multiplier=1), reads=[Tc], writes=[Tc])
    fw.op(pool, lambda e: e.tensor_copy(ident_b[:], ident_f[:]), reads=[Tc], writes=[Tc])
    fw.op(pool, lambda e: e.memset(ones_b[:], 1.0), writes=[Tc])
    fw.op(pool, lambda e: e.memset(ones_f[:], 1.0), writes=[Tc])
    fw.op(pool, lambda e: e.memset(eps_c[:], EPS), writes=[Tc])
    fw.op(pool, lambda e: e.memset(one_c[:], 1.0), writes=[Tc])
    mhalf_c = sb("mhalf_c", [128, 1])
    fw.op(pool, lambda e: e.memset(mhalf_c[:], -0.5), writes=[Tc])
    e5 = sb("e5", [128, 8], BF16)
    fw.op(pool, lambda e: e.memset(e5[:], 0.0), writes=[Tc])
    fw.op(pool, lambda e: e.memset(e5[:, 0:1], 1.0), reads=[Tc], writes=[Tc])
    for (m, pat, cm, cmp_) in ((U8, [[0, 8], [1, 64]], -1, ALU.is_ge), (SU8, [[0, 8], [1, 64]], -1, ALU.is_gt),
                               (L8, [[0, 8], [-1, 64]], 1, ALU.is_gt), (I8, [[0, 8], [-1, 64]], 1, ALU.is_equal)):
        fw.op(pool, lambda e: e.memset(m[:], 1.0), reads=[Tc], writes=[Tc])
        fw.op(pool, lambda e: e.affine_select(out=m[:], in_=m[:], pattern=pat, compare_op=cmp_, fill=0.0, base=0,
                                              channel_multiplier=cm), reads=[Tc], writes=[Tc])
    NEGU = sb("NEGU", [64, 8, 64], BF16); NEGL = sb("NEGL", [64, 8, 64], BF16)
    fw.op(dve, lambda e: e.tensor_scalar(NEGU[:], SU8[:], 30000.0, -30000.0, ALU.mult, ALU.add), reads=[Tc], writes=[Tc])
    fw.op(dve, lambda e: e.tensor_scalar(NEGL[:], L8[:], 30000.0, -30000.0, ALU.mult, ALU.add), reads=[Tc], writes=[Tc])
    Umat = U8[:, 0, :]
    Lmat = L8[:, 0, :]
    ULb = sb("ULb", [64, 2, 64], BF16)
    fw.op(pool, lambda e: e.tensor_copy(ULb[:, 0, :], Umat), reads=[Tc], writes=[Tc])
    fw.op(pool, lambda e: e.tensor_copy(ULb[:, 1, :], Lmat), reads=[Tc], writes=[Tc])
    Umat_b = ULb[:, 0, :]
    Lmat_b = ULb[:, 1, :]

    Twi_kv, Twi_rest, Two, Twu, Twd = T("wi_kv"), T("wi_rest"), T("wo"), T("wu"), T("wd")

    cast_sems = {}
    cast_q = []

    def cast(dst, src, tr, rows, c0, c1, rstep=256, defer=False):
        if tr.name not in cast_sems:
            cast_sems[tr.name] = fw.dsem("cast_" + tr.name)
        for r0 in range(0, rows, rstep):
            item = (dst[r0:r0 + rstep, c0:c1], src[r0:r0 + rstep, c0:c1], tr)
            if defer:
                cast_q.append(item)
            else:
                cast_emit(item)

    def cast_emit(item):
        d_, s_, tr = item
        fw.dma(pool, cast_sems[tr.name], d_, s_, writes=[tr], waw=False)

    def cast_pump(n):
        for _ in range(n):
            if cast_q:
                cast_emit(cast_q.pop(0))

    d_par = fw.dsem("par")
    Tp = T("params")
    xin = sb("xin", [128, D_MODEL]); Txin = T("xin")
    tok = sb("tok", [128, D_MODEL]); Ttok = T("tok")
    pst = xin[:, 0:512].rearrange("p (q c) -> p q c", q=4); Tpst = Txin
    stT = xin[:, 512:576]
    par = sb("par", [128, 4, 128])
    tm = sb("tm", [64, NCHUNK_ALL])
    alog_r = sb("alog_r", [128, 8]); dtb_r = sb("dtb_r", [128, 8]); nA_r = sb("nA_r", [128, 8])
    fw.op(pool, lambda e: e.memset(xin[:, 0:576], 0.0), writes=[Tpst])

    d_pst = fw.dsem("pst")

    def pl(dst, src):
        fw.dma(sp, d_pst, dst, src, reads=[Tpst], writes=[Tpst])
    r128 = lambda ap1d: ap1d.rearrange("(k p) -> k p", p=128)
    pl(pst[0:16, 0, :], r128(D["g_pre_mix"][0]))
    pl(pst[16:32, 0, :], r128(D["g_pre_ffn"][0]))
    pl(pst[32:40, 0, :], r128(D["g_norm_a"][0]))
    pl(pst[40:41, 0, :], r128(D["g_norm_gdn"][0]))
    pl(pst[41:65, 0, :], D["w_conv_a"].rearrange("j (c p) -> (j c) p", p=128))
    pl(pst[65:81, 0, :], r128(D["g_post_mix"][0]))
    pl(pst[81:97, 0, :], r128(D["g_post_ffn"][0]))
    pl(pst[0:96, 1, :], D["w_conv_gdn"].rearrange("j (c p) -> (j c) p", p=128))
    wcf_rows = D["w_conv_ffn"].rearrange("j (c p) -> (j c) p", p=128)
    pl(pst[0:128, 2, :], wcf_rows[0:128, :])
    pl(pst[0:4, 3, :], wcf_rows[128:132, :])
    pl(stT[0:NCHUNK_ALL, :], D["tmask"])
    fw.dma(sp, d_par, alog_r[:], D["a_log"][0].partition_broadcast(128), writes=[Tp])
    fw.dma(sp, d_par, dtb_r[:], D["dt_bias"][0].partition_broadcast(128), writes=[Tp])
    cast(wi_b, D["w_in"], Twi_kv, D_MODEL, 4096, IN_COLS)
    a_ = next_acc()
    for q in range(4):
        fw.op(pe, lambda e: e.transpose(a_.a[:, q * 128:(q + 1) * 128], pst[:, q, :], ident_f[:, :]), reads=[Tpst, Tc], writes=[a_.t0])
    fw.op(dve, lambda e: e.tensor_copy(par[:].rearrange("p q c -> p (q c)"), a_.a[:, 0:512]), reads=[a_.t0], writes=[Tp])
    fw.op(pe, lambda e: e.transpose(a_.a[:64, 512:512 + NCHUNK_ALL], stT[:NCHUNK_ALL, :], ident_f[:NCHUNK_ALL, :NCHUNK_ALL]),
          reads=[Tpst, Tc], writes=[a_.t1])
    fw.op(dve, lambda e: e.tensor_copy(tm[:], a_.a[:64, 512:512 + NCHUNK_ALL]), reads=[a_.t1], writes=[Tp])
    gpm = par[:, 0, 0:16]; gpf = par[:, 0, 16:32]; gna = par[:, 0, 32:40]; gng = par[:, 0, 40:41]
    gqm = par[:, 0, 65:81]; gqf = par[:, 0, 81:97]

    def wca(c, j):
        return par[:, 0, 41 + j * 8 + c:42 + j * 8 + c]

    def wcg(c, j):
        return par[:, 1, j * 24 + c:j * 24 + c + 1]

    def wcf(c, j):
        r = j * KF + c
        return par[:, 2, r:r + 1] if r < 128 else par[:, 3, r - 128:r - 127]
    fw.op(act, lambda e: e.activation(nA_r[:], alog_r[:], AF.Exp), reads=[Tp], writes=[Tp])
    fw.op(dve, lambda e: e.tensor_scalar(nA_r[:], nA_r[:], -1.0, None, ALU.mult), reads=[Tp], writes=[Tp])

    cast(wi_b, D["w_in"], Twi_rest, D_MODEL, 0, 4096, rstep=128, defer=True)
    cast(wo_b, D["w_out"], Two, D_MODEL, 0, D_MODEL, rstep=256, defer=True)
    cast(wu_b, D["w_up"], Twu, D_MODEL, 0, 2 * D_FF, rstep=64, defer=True)
    cast(wd_b, D["w_down"], Twd, D_FF, 0, D_MODEL, rstep=256, defer=True)

    xsb = sb("xsb", [128, D_MODEL], BF16); Txsb = T("xsb")
    junk = xsb; Tjunk = Txsb
    hT = sb("hT", [128, KD, NTMAX], BF16); ThT = T("hT")
    yT = sb("yT", [128, KD, NTMAX], BF16); TyT = T("yT")
    class St:
        pass

    def mk_state(sfx):
        st = St()
        st.S = sb("S" + sfx, [128, NH, DK]); st.TS = T("S" + sfx)
        st.Sbf = sb("Sbf" + sfx, [128, NH, DK], BF16); st.TSbf = T("Sbf" + sfx)
        st.hA = sb("histA" + sfx, [128, 8, 2]); st.ThA = T("hA" + sfx)
        st.hG = sb("histG" + sfx, [128, 24, 3]); st.ThG = T("hG" + sfx)
        st.hF = sb("histF" + sfx, [128, KF, 2]); st.ThF = T("hF" + sfx)
        return st
    ST = [mk_state(""), mk_state("_s")]
    dSs = tok[:, 0:1024].rearrange("p (h d) -> p h d", h=NH); TdSs = Ttok

    class Seg:
        def __init__(self, t0, n, nreal, sidx, xsrc, row0, out_ap=None, out_row0=0, tok_lo=0, tok_hi=0):
            self.t0, self.n, self.nreal, self.sidx, self.xsrc, self.row0 = t0, n, nreal, sidx, xsrc, row0
            self.out_ap, self.out_row0, self.tok_lo, self.tok_hi = out_ap, out_row0, tok_lo, tok_hi
    wl = sb("wl", [128, KD, 16], BF16); Twl = T("wl")
    smt = sb("smt", [128, 160])

    def small(c0, w, name, parts=128):
        return Buf(smt[:parts, c0:c0 + w], name)
    n_ss = small(0, 1, "n_ss"); n_rs = small(1, 1, "n_rs"); t_ssc = small(2, 4, "t_ssc"); t_st = small(6, 1, "t_st"); t_rs = small(7, 1, "t_rs")
    sm_beta = [small(8 + 64 * i, 8, f"beta{i}", 64) for i in range(2)]
    sm_gg = [small(16 + 64 * i, 8, f"gg{i}", 64) for i in range(2)]
    sm_t8 = [small(24 + 64 * i, 8, f"t8{i}", 64) for i in range(2)]
    sm_egc = [small(32 + 64 * i, 8, f"egc{i}", 64) for i in range(2)]
    sm_edl = [small(40 + 64 * i, 8, f"edl{i}", 64) for i in range(2)]
    sm_cb = [small(48 + 64 * i, 8, f"cb{i}", 64) for i in range(2)]
    sm_egl = [small(56 + 64 * i, 8, f"egl{i}", 128) for i in range(2)]
    sm_gc = [small(64 + 64 * i, 8, f"gc{i}", 64) for i in range(2)]
    smb = sb("smb", [64, 2, 2, 16], BF16)
    sm_r = sb("sm_r", [64, 2, 16])
    sm_split = [Buf(smb[:, i], f"split{i}") for i in range(2)]
    sm_res = [Buf(sm_r[:, i], f"res{i}") for i in range(2)]
    NU = 3
    wun = [sb(f"wu{i}", [128, 16, 256], BF16) for i in range(NU)]
    Twun = [T(f"wun{i}") for i in range(NU)]
    d_wun = [fw.dsem(f"wun{i}") for i in range(NU)]
    d_x = fw.dsem("x"); d_o = fw.dsem("o"); d_st = fw.dsem("st"); d_x1 = fw.dsem("x1"); d_x1l = fw.dsem("x1l")
    unit_ctr = [0]

    UN = 25200
    UNI = sb("UNI", [128, UN])
    upos = [0]

    def carve(nelem_f32):
        a = upos[0]
        upos[0] += nelem_f32
        assert upos[0] <= UN, upos[0]
        return UNI[:, a:a + nelem_f32]

    h3 = "p (h t) -> p h t"

    def cf(n, name, parts=128, h=None):
        a = carve(n)[:parts]
        if h:
            a = a.rearrange(h3, h=h)
        return Buf(a, name)

    def cb16(n_bf, name, parts=128, h=None):
        a = carve(n_bf // 2).bitcast(BF16)[:parts]
        if h:
            a = a.rearrange(h3, h=h)
        return Buf(a, name)

    qnT = cb16(NH * NTMAX, "qnT", h=NH); knT = cb16(NH * NTMAX, "knT", h=NH)
    vT = cb16(NH * NTMAX, "vT", h=NH); szT = cb16(NH * NTMAX, "szT", h=NH)
    pre = [cf(NTMAX + 8, f"pre{i}") for i in range(2)]
    cvb = [cf(NTMAX, f"cvb{i}") for i in range(2)]
    qsb = [cf(NTMAX, f"qs{i}") for i in range(2)]; ahs = cf(NTMAX, "ahs")
    sqbb = [cb16(NTMAX, f"sqb{i}") for i in range(2)]
    sqb = sqbb[0]
    alias0 = upos[0]
    yraw = cf(8 * NTMAX, "yraw", h=8)
    upos[0] = alias0
    def cf2(name):
        a = carve(512).bitcast(BF16)[:64].rearrange("p (s h t) -> p s h t", s=2, h=8)
        return Buf(a, name)
    rhsG = cf2("rhsG"); rhsL = cf2("rhsL"); rhsB = cf2("rhsB")
    EG = cb16(512, "EG", 128, 8)
    DTi = cb16(512, "DTi", 64, 8); DTs = cb16(512, "DTs", 64, 8); Dm = cb16(512, "Dm", 64, 8)
    otmp = cf(512, "otmp", 128, 8); rso = cf(512, "rso", 128, 8)
    kbT = cb16(512, "kbT", 128, 8)
    Pm = [cb16(512, f"P{i}", 64, 8) for i in range(2)]
    PTm = [cb16(512, f"PT{i}", 64, 8) for i in range(2)]
    TT = cb16(512, "TT", 64, 8); TTc = cb16(512, "TTc", 64, 8)
    vnew = cb16(1024, "vnew", 64, 8); vnews = cb16(1024, "vnews", 64, 8)
    osq = cb16(512, "osq", 128, 8)
    qgT = [cb16(512, f"qgT{i}", 128, 8) for i in range(2)]
    ktk = [cb16(1024, f"ktk{i}", 64, 8) for i in range(2)]
    vtk = [cb16(1024, f"vtk{i}", 64, 8) for i in range(2)]
    TTb = [cb16(512, f"TTb{i}", 64, 8) for i in range(2)]
    attnT = [cb16(512, f"attnT{i}", 64, 8) for i in range(2)]
    negwT = [cb16(512, f"negwT{i}", 128, 8) for i in range(2)]
    mix_end = upos[0]
    upos[0] = 0
    fT = Buf(carve(KD * NTMAX).rearrange("p (m t) -> p m t", m=KD), "fT")
    actT = Buf(carve(KF * NTMAX // 2).bitcast(BF16).rearrange("p (k t) -> p k t", k=KF), "actT")
    pre2 = [cf(NTMAX + 8, f"pre2{i}") for i in range(1)]
    cv2 = [cf(NTMAX, f"cv2{i}") for i in range(1)]
    sqb2 = [cb16(NTMAX, f"sqb2{i}") for i in range(2)]
    assert max(mix_end, upos[0]) <= UN, (mix_end, upos[0])

    def load_unit(src_ap, nk, tr_src):
        i = unit_ctr[0] % NU
        unit_ctr[0] += 1
        fw.dma(sp, d_wun[i], wun[i][:, 0:nk, :], src_ap.rearrange("(k p) c -> p k c", p=128), reads=[tr_src], writes=[Twun[i]])
        return wun[i], Twun[i]

    def mm_chunk(A, ub, tub, j, src, tsrc, Nt, nk, koff=0, first=True, last=True):
        for (o, n) in nsplit(Nt):
            for k in range(nk):
                fw.op(pe, lambda e: e.matmul(A.a[:, o:o + n], ub[:, k, j * 128:(j + 1) * 128], src[:, koff + k, o:o + n],
                                             start=(first and k == 0), stop=(last and k == nk - 1)),
                      reads=[tub, tsrc], writes=[A.t0 if o < 512 else A.t1])

    def rsqrt_cols(dst, src, scale, rd, wr):
        P = dst.shape[0]
        if len(dst.shape) == 2 and dst.shape[1] == 1:
            fw.op(act, lambda e: e.activation(dst, src, AF.Identity, bias=eps_c[:P], scale=scale), reads=rd + [Tc], writes=wr)
            fw.op(pool, lambda e: e.tensor_tensor(dst, dst, mhalf_c[:P], ALU.pow), reads=wr + [Tc], writes=wr)
            return
        fw.op(act, lambda e: e.activation(dst, src, AF.Sqrt, bias=eps_c[:P], scale=scale), reads=rd + [Tc], writes=wr)
        fw.op(dve, lambda e: e.reciprocal(dst, dst), reads=wr, writes=wr)

    def rsqrt_big(dst, src, scale, rd, wr):
        P = dst.shape[0]
        fw.op(act, lambda e: e.activation(dst, src, AF.Ln, bias=eps_c[:P], scale=scale), reads=rd + [Tc], writes=wr)
        fw.op(act, lambda e: e.activation(dst, dst, AF.Exp, scale=-0.5), reads=wr, writes=wr)

    def norm_block_to_T(src_tile, tsrc, nb, gvec, dstT, tdst, t0):
        ss = n_ss.a[:nb]
        rs = n_rs.a[:nb]
        fw.op(act, lambda e: e.activation(junk[:nb], src_tile[:nb], AF.Square, accum_out=ss), reads=[tsrc], writes=[Tjunk, n_ss.t])
        rsqrt_cols(rs, ss, 1.0 / D_MODEL, [n_ss.t], [n_rs.t])
        fw.op(act, lambda e: e.activation(xsb[:nb], src_tile[:nb], AF.Copy, scale=rs), reads=[tsrc, n_rs.t], writes=[Txsb])
        for g in range(4):
            A = next_acc()
            accb = A.a[:].bitcast(BF16)[:, 0:512].rearrange("p (j t) -> p j t", j=4)
            for j in range(4):
                k = g * 4 + j
                fw.op(pe, lambda e: e.transpose(accb[:, j, :nb], xsb[:nb, k * 128:(k + 1) * 128], ident_b[:nb, :nb]),
                      reads=[Txsb, Tc], writes=[A.t0])
            fw.op(dve, lambda e: e.tensor_tensor(dstT[:, g * 4:(g + 1) * 4, t0:t0 + nb], accb[:, :, :nb],
                                                 gvec[:, g * 4:(g + 1) * 4, None].broadcast_to([128, 4, nb]), ALU.mult),
                  reads=[A.t0, Tp], writes=[tdst])

    def build_hT_gen(segs):
        for sg in segs:
            for (o, nb) in blocks_of(sg.n):
                fw.dma(sp, d_x, xin[:nb], sg.xsrc[sg.row0 + o:sg.row0 + o + nb, :], writes=[Txin])
                norm_block_to_T(xin, Txin, nb, gpm, hT, ThT, sg.t0 + o)
                yield

    def build_hT(segs):
        for _ in build_hT_gen(segs):
            pass

    def conv_segs_gen(cv, pr, K, wfn, widx, segs, hist_of, fill):
        K1 = K - 1
        base = 0
        bases = []
        for sg in segs:
            hb, th = hist_of(ST[sg.sidx])
            fw.op(pool, lambda e: e.tensor_copy(pr.a[:, base:base + K1], hb[:, widx, :]), reads=[th], writes=[pr.t])
            fill(pr.a[:, base + K1:base + K1 + sg.n], sg.t0, sg.n)
            fw.op(pool, lambda e: e.tensor_copy(hb[:, widx, :], pr.a[:, base + sg.nreal:base + sg.nreal + K1]), reads=[pr.t], writes=[th])
            bases.append(base)
            base += K1 + sg.n
        yield
        for sg, base in zip(segs, bases):
            d = cv.a[:, sg.t0:sg.t0 + sg.n]
            fw.op(dve, lambda e: e.tensor_scalar(d, pr.a[:, base:base + sg.n], wfn(widx, 0), None, ALU.mult), reads=[pr.t, Tp], writes=[cv.t])
            for j in range(1, K):
                fw.op(dve, lambda e: e.scalar_tensor_tensor(d, pr.a[:, base + j:base + j + sg.n], wfn(widx, j), d, ALU.mult, ALU.add),
                      reads=[pr.t, cv.t, Tp], writes=[cv.t])
        yield

    def conv_segs(*args):
        for _ in conv_segs_gen(*args):
            pass

    def to_token_major(srcT, o, nb, dst_tile, tdst):
        for g in range(4):
            A = next_acc()
            a3 = A.a[:, 0:512].rearrange("p (j t) -> p j t", j=4)
            for j in range(4):
                m = g * 4 + j
                fw.op(pe, lambda e: e.transpose(a3[:nb, j, :], srcT.a[:, m, o:o + nb], ident_f[:, :]),
                      reads=[srcT.t, Tc], writes=[A.t0])
            fw.op(act, lambda e: e.activation(junk[:nb, 0:512], A.a[:nb, 0:512], AF.Square, accum_out=t_ssc.a[:nb, g:g + 1]),
                  reads=[A.t0], writes=[Tjunk, t_ssc.t])
            fw.op(dve, lambda e: e.tensor_copy(dst_tile[:nb, g * 512:(g + 1) * 512], A.a[:nb, 0:512]), reads=[A.t0], writes=[tdst])

    def bc(ap2, n):
        return ap2[:, :, None].broadcast_to([ap2.shape[0], ap2.shape[1], n])

    s1ctr = [0]

    def s1_acc():
        i = s1ctr[0] % 2
        s1ctr[0] += 1
        return Acc(i)

    front_done = set()
    WARM = 0

    def s1_front(t0, gci, pb):
        c64 = slice(t0, t0 + 64)
        beta, gg, t8 = sm_beta[pb], sm_gg[pb], sm_t8[pb]
        AL = s1_acc()
        for k in range(KD):
            fw.op(pe, lambda e: e.matmul(AL.a[:64, 0:16], hT[:, k, c64], wl[:, k, :], start=(k == 0), stop=(k == KD - 1)),
                  reads=[ThT, Twl], writes=[AL.t0])
        mcol = tm[:, gci:gci + 1]
        fw.op(act, lambda e: e.activation(beta.a, AL.a[:64, 0:8], AF.Exp, scale=-1.0), reads=[AL.t0], writes=[beta.t])
        fw.op(dve, lambda e: e.tensor_tensor(t8.a, AL.a[:64, 8:16], dtb_r[:64], ALU.add), reads=[AL.t0, Tp], writes=[t8.t])
        fw.op(act, lambda e: e.activation(t8.a, t8.a, AF.Exp), reads=[t8.t], writes=[t8.t])
        fw.op(dve, lambda e: e.tensor_scalar(beta.a, beta.a, 1.0, None, ALU.add), reads=[beta.t], writes=[beta.t])
        fw.op(act, lambda e: e.activation(t8.a, t8.a, AF.Ln, bias=one_c[:64]), reads=[t8.t, Tc], writes=[t8.t])
        fw.op(dve, lambda e: e.reciprocal(beta.a, beta.a), reads=[beta.t], writes=[beta.t])
        fw.op(dve, lambda e: e.tensor_scalar(beta.a, beta.a, mcol, None, ALU.mult), reads=[beta.t, Tp], writes=[beta.t])
        fw.op(dve, lambda e: e.scalar_tensor_tensor(gg.a, t8.a, mcol, nA_r[:64], ALU.mult, ALU.mult), reads=[t8.t, Tp], writes=[gg.t])
        sp_, rs_ = sm_split[pb], sm_res[pb]
        fw.op(dve, lambda e: e.tensor_copy(sp_.a[:, 0, 0:8], gg.a), reads=[gg.t], writes=[sp_.t])
        fw.op(dve, lambda e: e.tensor_copy(sp_.a[:, 0, 8:16], beta.a), reads=[beta.t, sp_.t], writes=[sp_.t])
        fw.op(dve, lambda e: e.tensor_tensor(rs_.a[:, 0:8], gg.a, sp_.a[:, 0, 0:8], ALU.subtract), reads=[gg.t, sp_.t], writes=[rs_.t])
        fw.op(dve, lambda e: e.tensor_tensor(rs_.a[:, 8:16], beta.a, sp_.a[:, 0, 8:16], ALU.subtract), reads=[beta.t, sp_.t, rs_.t], writes=[rs_.t])
        fw.op(dve, lambda e: e.tensor_copy(sp_.a[:, 1, :], rs_.a), reads=[rs_.t, sp_.t], writes=[sp_.t])
        front_done.add(gci)

    def gdn_stage1(t0, gci, full, pb, nxt):
        c64 = slice(t0, t0 + 64)
        beta, gg, t8, egc, edl, cb_, egl, gcs = sm_beta[pb], sm_gg[pb], sm_t8[pb], sm_egc[pb], sm_edl[pb], sm_cb[pb], sm_egl[pb], sm_gc[pb]
        if gci not in front_done:
            s1_front(t0, gci, pb)
        sp_ = sm_split[pb]
        A0 = s1_acc()
        g2 = sp_.a[:, :, 0:8]
        b2 = sp_.a[:, :, 8:16]

        def bc4(m8, v2):
            return (m8[:, None, :, :].broadcast_to([64, 2, 8, 64]), v2[:, :, :, None].broadcast_to([64, 2, 8, 64]))
        fw.op(dve, lambda e: e.tensor_tensor(rhsG.a, *bc4(U8, g2), ALU.mult), reads=[sp_.t, Tc], writes=[rhsG.t])
        fw.op(dve, lambda e: e.tensor_tensor(rhsL.a, *bc4(L8, g2), ALU.mult), reads=[sp_.t, Tc], writes=[rhsL.t])
        fw.op(dve, lambda e: e.tensor_tensor(rhsB.a, *bc4(I8, b2), ALU.mult), reads=[sp_.t, Tc], writes=[rhsB.t])
        yield
        A1 = s1_acc()
        fl = "p h t -> p (h t)"
        fl4 = "p h t -> p (h t)"

        def mm2(out, lhsT, rb, neg, tw):
            for part in range(2):
                fw.op(pe, lambda e: e.matmul(out, lhsT, rb.a[:, part].rearrange(fl4), start=(part == 0), stop=(part == 1 and neg is None)),
                      reads=[rb.t, Tc], writes=[tw])
            if neg is not None:
                fw.op(pe, lambda e: e.matmul(out, ident_b[:64, :64], neg[:].rearrange(fl4), start=False, stop=True), reads=[Tc], writes=[tw])
        fw.op(pe, lambda e: e.matmul(A0.a[:64, 16:24], Umat, gg.a, start=True, stop=True), reads=[gg.t, Tc], writes=[A0.t0])
        fw.op(pe, lambda e: e.matmul(A0.a[:, 24:32], ones_f[:, :], gg.a, start=True, stop=True), reads=[gg.t, Tc], writes=[A0.t0])
        mm2(A1.a[:64, 0:512], Lmat_b, rhsG, NEGU, A1.t0)
        mm2(A1.a[:64, 512:1024], Umat_b, rhsL, NEGL, A1.t1)
        mm2(A0.a[:, 512:1024], ones_b[:64, :], rhsB, None, A0.t1)
        yield
        fw.op(act, lambda e: e.activation(egc.a, A0.a[:64, 16:24], AF.Exp), reads=[A0.t0], writes=[egc.t])
        fw.op(act, lambda e: e.copy(gcs.a, A0.a[:64, 16:24]), reads=[A0.t0], writes=[gcs.t])
        fw.op(act, lambda e: e.activation(egl.a, A0.a[:, 24:32], AF.Exp), reads=[A0.t0], writes=[egl.t])
        fw.op(dve, lambda e: e.tensor_tensor(edl.a, A0.a[:64, 24:32], gcs.a, ALU.subtract), reads=[A0.t0, gcs.t], writes=[edl.t])
        fw.op(act, lambda e: e.activation(edl.a, edl.a, AF.Exp), reads=[edl.t], writes=[edl.t])
        fw.op(dve, lambda e: e.tensor_tensor(cb_.a, beta.a, egc.a, ALU.mult), reads=[beta.t, egc.t], writes=[cb_.t])
        if full:
            mm2(A0.a[:, 0:512], ones_b[:64, :], rhsG, None, A0.t0)
        fw.op(act, lambda e: e.activation(DTs.a, A1.a[:64, 0:512].rearrange(h3, h=8), AF.Exp), reads=[A1.t0], writes=[DTs.t])
        fw.op(dve, lambda e: e.tensor_tensor(kbT.a, knT.a[:, :, c64], A0.a[:, 512:1024].rearrange(h3, h=8), ALU.mult),
              reads=[knT.t, A0.t1], writes=[kbT.t])
        fw.op(act, lambda e: e.activation(Dm.a, A1.a[:64, 512:1024].rearrange(h3, h=8), AF.Exp), reads=[A1.t1], writes=[Dm.t])
        if full:
            fw.op(act, lambda e: e.activation(EG.a, A0.a[:, 0:512].rearrange(h3, h=8), AF.Exp), reads=[A0.t0], writes=[EG.t])
            fw.op(dve, lambda e: e.tensor_tensor(DTi.a, DTs.a, I8[:], ALU.add), reads=[DTs.t, Tc], writes=[DTi.t])
            fw.op(dve, lambda e: e.tensor_tensor(qgT[pb].a, qnT.a[:, :, c64], EG.a, ALU.mult), reads=[qnT.t, EG.t], writes=[qgT[pb].t])
        yield
        B2 = s1_acc()
        b2b = B2.a[:].bitcast(BF16)
        ktok = b2b[:64, 0:1024].rearrange("p (h d) -> p h d", h=8)
        vtok = b2b[:64, 1024:2048].rearrange("p (h d) -> p h d", h=8)
        for h in range(NH):
            fw.op(pe, lambda e: e.transpose(ktok[:, h, :], knT.a[:, h, c64], ident_b[:, :]), reads=[knT.t, Tc], writes=[B2.t0])
        for h in range(NH):
            fw.op(pe, lambda e: e.transpose(vtok[:, h, :], vT.a[:, h, c64], ident_b[:, :]), reads=[vT.t, Tc], writes=[B2.t1])
        fw.op(act, lambda e: e.copy(ktk[pb].a, ktok), reads=[B2.t0], writes=[ktk[pb].t])
        fw.op(act, lambda e: e.copy(vtk[pb].a, vtok), reads=[B2.t1], writes=[vtk[pb].t])
        yield
        B0 = s1_acc()
        for h in range(NH):
            hs = slice(h * 64, (h + 1) * 64)
            fw.op(pe, lambda e: e.matmul(B0.a[:64, hs], knT.a[:, h, c64], kbT.a[:, h, :], start=True, stop=True), reads=[knT.t, kbT.t], writes=[B0.t0])
        for h in range(NH):
            fw.op(pe, lambda e: e.matmul(B0.a[:64, 512 + h * 64:512 + (h + 1) * 64], kbT.a[:, h, :], knT.a[:, h, c64], start=True, stop=True),
                  reads=[knT.t, kbT.t], writes=[B0.t1])
        fw.op(dve, lambda e: e.tensor_tensor(PTm[0].a, B0.a[:64, 0:512].rearrange(h3, h=8), DTs.a, ALU.mult), reads=[B0.t0, DTs.t], writes=[PTm[0].t])
        fw.op(dve, lambda e: e.tensor_tensor(Pm[0].a, B0.a[:64, 512:1024].rearrange(h3, h=8), Dm.a, ALU.mult), reads=[B0.t1, Dm.t], writes=[Pm[0].t])
        fw.op(dve, lambda e: e.tensor_tensor(TT.a, I8[:], PTm[0].a, ALU.subtract), reads=[PTm[0].t, Tc], writes=[TT.t])
        yield
        if full:
            B1 = s1_acc()
            for h in range(NH):
                hs = slice(h * 64, (h + 1) * 64)
                fw.op(pe, lambda e: e.matmul(B1.a[:64, hs], knT.a[:, h, c64], qnT.a[:, h, c64], start=True, stop=True), reads=[knT.t, qnT.t], writes=[B1.t0])
            fw.op(dve, lambda e: e.tensor_tensor(attnT[pb].a, B1.a[:64, 0:512].rearrange(h3, h=8), DTi.a, ALU.mult),
                  reads=[B1.t0, DTi.t], writes=[attnT[pb].t])
        if nxt is not None and full:
            s1_front(nxt[0], nxt[1], 1 - pb)
        cur = 0
        for lvl in range(5):
            nxt = 1 - cur
            C0 = s1_acc()
            for h in range(NH):
                hs = slice(h * 64, (h + 1) * 64)
                fw.op(pe, lambda e: e.matmul(C0.a[:64, hs], PTm[cur].a[:, h, :], Pm[cur].a[:, h, :], start=True, stop=True),
                      reads=[PTm[cur].t, Pm[cur].t], writes=[C0.t0])
            if lvl < 4:
                for h in range(NH):
                    fw.op(pe, lambda e: e.matmul(C0.a[:64, 512 + h * 64:512 + (h + 1) * 64], Pm[cur].a[:, h, :], PTm[cur].a[:, h, :], start=True, stop=True),
                          reads=[PTm[cur].t, Pm[cur].t], writes=[C0.t1])
            C1 = s1_acc()
            for _ in range(WARM):
                fw.op(pe, lambda e: e.matmul(C1.a[:, 512:1024], ones_b[:, :], hT[:, 0, 0:512], start=True, stop=True), reads=[ThT, Tc], writes=[C1.t1])
            fw.op(act, lambda e: e.copy(Pm[nxt].a, C0.a[:64, 0:512].rearrange(h3, h=8)), reads=[C0.t0], writes=[Pm[nxt].t])
            if lvl < 4:
                fw.op(dve, lambda e: e.tensor_copy(PTm[nxt].a, C0.a[:64, 512:1024].rearrange(h3, h=8)), reads=[C0.t1], writes=[PTm[nxt].t])
            yield
            for h in range(NH):
                hs = slice(h * 64, (h + 1) * 64)
                fw.op(pe, lambda e: e.matmul(C1.a[:64, hs], Pm[nxt].a[:, h, :], TT.a[:, h, :], start=True, stop=True),
                      reads=[Pm[nxt].t, TT.t], writes=[C1.t0])
            fw.op(dve, lambda e: e.tensor_tensor(TT.a, TT.a, C1.a[:64, 0:512].rearrange(h3, h=8), ALU.add), reads=[C1.t0, TT.t], writes=[TT.t])
            cur = nxt
        yield
        fw.op(dve, lambda e: e.tensor_tensor(TTc.a, TT.a, bc(cb_.a, 64), ALU.mult), reads=[TT.t, cb_.t], writes=[TTc.t])
        fw.op(dve, lambda e: e.tensor_tensor(TTb[pb].a, TT.a, bc(beta.a, 64), ALU.mult), reads=[TT.t, beta.t], writes=[TTb[pb].t])
        D0 = s1_acc()
        for h in range(NH):
            hs = slice(h * 64, (h + 1) * 64)
            fw.op(pe, lambda e: e.matmul(D0.a[:, hs], ktk[pb].a[:, h, :], TTc.a[:, h, :], start=True, stop=True), reads=[ktk[pb].t, TTc.t], writes=[D0.t0])
        fw.op(act, lambda e: e.mul(negwT[pb].a, D0.a[:, 0:512].rearrange(h3, h=8), -1.0), reads=[D0.t0], writes=[negwT[pb].t])
        yield

    def gdn_stage2(t0, full, pb, st):
        c64 = slice(t0, t0 + 64)
        S, Sbf, TS, TSbf = st.S, st.Sbf, st.TS, st.TSbf
        edl, egl = sm_edl[pb], sm_egl[pb]
        D1 = Acc(2)
        for h in range(NH):
            es = slice(h * 128, (h + 1) * 128)
            tt_ = D1.t0 if h < 4 else D1.t1
            fw.op(pe, lambda e: e.matmul(D1.a[:64, es], TTb[pb].a[:, h, :], vtk[pb].a[:, h, :], start=True, stop=False), reads=[TTb[pb].t, vtk[pb].t], writes=[tt_])
            fw.op(pe, lambda e: e.matmul(D1.a[:64, es], negwT[pb].a[:, h, :], Sbf[:, h, :], start=False, stop=True), reads=[negwT[pb].t, TSbf], writes=[tt_])
        yield
        fw.op(pool, lambda e: e.tensor_tensor(S[:], S[:], bc(egl.a, 128), ALU.mult), reads=[egl.t, TS], writes=[TS])
        d13 = D1.a[:64, :].rearrange("p (h d) -> p h d", h=8)
        if full:
            fw.op(act, lambda e: e.copy(vnew.a, d13), reads=D1.tt, writes=[vnew.t])
        fw.op(dve, lambda e: e.tensor_tensor(vnews.a, d13, bc(edl.a, 128), ALU.mult), reads=D1.tt + [edl.t], writes=[vnews.t])
        yield
        D3 = Acc(2)
        for h in range(NH):
            es = slice(h * 128, (h + 1) * 128)
            fw.op(pe, lambda e: e.matmul(D3.a[:, es], ktk[pb].a[:, h, :], vnews.a[:, h, :], start=True, stop=True), reads=[ktk[pb].t, vnews.t],
                  writes=[D3.t0 if h < 4 else D3.t1])
        if full:
            D2 = Acc(3)
            for h in range(NH):
                hs = slice(h * 64, (h + 1) * 64)
                fw.op(pe, lambda e: e.matmul(D2.a[:, hs], Sbf[:, h, :], qgT[pb].a[:, h, :], start=True, stop=False), reads=[TSbf, qgT[pb].t], writes=[D2.t0])
                fw.op(pe, lambda e: e.matmul(D2.a[:, hs], vnew.a[:, h, :], attnT[pb].a[:, h, :], start=False, stop=True), reads=[vnew.t, attnT[pb].t], writes=[D2.t0])
        yield
        d33 = D3.a[:, :].rearrange("p (h d) -> p h d", h=8)
        fw.op(dve, lambda e: e.tensor_tensor(Sbf[:], S[:], d33, ALU.add), reads=[TS] + D3.tt, writes=[TSbf])
        fw.op(act, lambda e: e.copy(dSs[:], d33), reads=D3.tt, writes=[TdSs])
        fw.op(pool, lambda e: e.tensor_tensor(S[:], S[:], dSs[:], ALU.add), reads=[TS, TdSs], writes=[TS])
        yield
        if full:
            o3 = D2.a[:, 0:512].rearrange(h3, h=8)
            fw.op(act, lambda e: e.activation(osq.a, o3, AF.Square), reads=[D2.t0], writes=[osq.t])
            E0 = Acc(3)
            fw.op(pe, lambda e: e.matmul(E0.a[:, 512:1024], ones_b[:, :], osq.a.rearrange("p h t -> p (h t)"), start=True, stop=True),
                  reads=[osq.t, Tc], writes=[E0.t1])
            yield
            rsqrt_big(rso.a, E0.a[:, 512:1024].rearrange(h3, h=8), 1.0 / DK, [E0.t1], [rso.t])
            fw.op(dve, lambda e: e.scalar_tensor_tensor(otmp.a, o3, gng, rso.a, ALU.mult, ALU.mult), reads=[D2.t0, Tp, rso.t], writes=[otmp.t])
            fw.op(pool, lambda e: e.tensor_tensor(yT[:, 8:16, c64], otmp.a, szT.a[:, :, c64], ALU.mult), reads=[otmp.t, szT.t], writes=[TyT])
            yield

    def gdn_chunks(chunks, full, tail_gen=None):
        fw.barrier()
        prev = None
        for ci, (t0, gci, sidx) in enumerate(chunks):
            pb = ci % 2
            cast_pump(3)
            nxt = (chunks[ci + 1][0], chunks[ci + 1][1]) if ci + 1 < len(chunks) else None
            extra = None
            if tail_gen is not None and ci == len(chunks) - 1:
                acc_pool[:] = [3]
                extra = tail_gen
            run_interleaved([gdn_stage1(t0, gci, full, pb, nxt), prev, extra])
            if extra is not None:
                acc_pool[:] = [0, 1, 2, 3]
            prev = gdn_stage2(t0, full, pb, ST[sidx])
        run_interleaved([prev])

    def gdn_heads(Nt, full, segs, pre_hook=None):
        cnt = [0]

        def bufs():
            i = cnt[0] % 2
            cnt[0] += 1
            return pre[i], cvb[i], qsb[i], sqbb[i]

        def proj(ub, tub, j):
            A = next_acc()
            mm_chunk(A, ub, tub, j, hT, ThT, Nt, KD)
            return A

        def conv_gen(A, ci, pr, cv):
            yield from conv_segs_gen(cv, pr, 4, wcg, ci, segs, lambda st: (st.hG, st.ThG),
                                     lambda dst, t0, n: fw.op(act, lambda e: e.copy(dst, A.a[:, t0:t0 + n]), reads=A.ts(Nt), writes=[pr.t]))

        def norm_gen(A, ci, pr, cv, qs, sq, out):
            yield from conv_gen(A, ci, pr, cv)
            fw.op(act, lambda e: e.activation(qs.a[:, :Nt], cv.a[:, :Nt], AF.Silu), reads=[cv.t], writes=[qs.t])
            fw.op(act, lambda e: e.activation(sq.a[:, :Nt], qs.a[:, :Nt], AF.Square), reads=[qs.t], writes=[sq.t])
            A2 = next_acc()
            for (o, n) in nsplit(Nt):
                fw.op(pe, lambda e: e.matmul(A2.a[:, o:o + n], ones_b[:, :], sq.a[:, o:o + n], start=True, stop=True),
                      reads=[sq.t, Tc], writes=[A2.t0 if o < 512 else A2.t1])
            out.append(A2)
            yield

        def norm_p2(A2, cv, qs, dst, h, sc):
            rsqrt_big(cv.a[:, :Nt], A2.a[:, :Nt], 1.0, A2.ts(Nt), [cv.t])
            fw.op(dve, lambda e: e.scalar_tensor_tensor(dst.a[:, h, :Nt], qs.a[:, :Nt], sc, cv.a[:, :Nt], ALU.mult, ALU.mult),
                  reads=[qs.t, cv.t], writes=[dst.t])

        seq = []
        for hp in range(4):
            if full:
                for j in range(2):
                    seq.append([("q", hp, j), ("k", hp, j)])
            else:
                seq.append([("k", hp, 0), ("k", hp, 1)])
            for j in range(2):
                seq.append([("v", hp, j)] + ([("z", hp, j)] if full else []))
        ubase = {"q": 12, "k": 16, "v": 20, "z": 24}
        units = {}

        def stage_a(grp):
            res = []
            for (kind, hp, j) in grp:
                if (kind, hp) not in units:
                    units[(kind, hp)] = load_unit(wi_b[:, (ubase[kind] + hp) * 256:(ubase[kind] + hp + 1) * 256], KD, Twi_rest if kind == "q" else Twi_kv)
                ub, tub = units[(kind, hp)]
                res.append(proj(ub, tub, j))
            return res

        nxt = stage_a(seq[0])
        if pre_hook is not None:
            pre_hook()
        for gi, grp in enumerate(seq):
            accs_ = nxt
            if grp[0][0] != "v" and gi + 1 < len(seq):
                pass
            if grp[0][0] in ("q", "k"):
                gens = []
                todo = []
                for (kind, hp, j), A in zip(grp, accs_):
                    h = hp * 2 + j
                    pr, cv, qs, sq = bufs()
                    cidx0, dst, sc = (0, qnT, DK ** -0.5) if kind == "q" else (8, knT, 1.0)
                    out = []
                    gens.append(norm_gen(A, cidx0 + h, pr, cv, qs, sq, out))
                    todo.append((out, cv, qs, dst, h, sc))
                run_interleaved(gens)
                if gi + 1 < len(seq):
                    nxt = stage_a(seq[gi + 1])
                for (out, cv, qs, dst, h, sc) in todo:
                    norm_p2(out[0], cv, qs, dst, h, sc)
            else:
                (kind, hp, j) = grp[0]
                h = hp * 2 + j
                pr, cv, qs, sq = bufs()
                A = accs_[0]
                run_interleaved([conv_gen(A, 16 + h, pr, cv)])
                fw.op(act, lambda e: e.activation(vT.a[:, h, :Nt], cv.a[:, :Nt], AF.Silu), reads=[cv.t], writes=[vT.t])
                if full:
                    Az = accs_[1]
                    fw.op(act, lambda e: e.activation(szT.a[:, h, :Nt], Az.a[:, :Nt], AF.Silu), reads=Az.ts(Nt), writes=[szT.t])
                if gi + 1 < len(seq):
                    nxt = stage_a(seq[gi + 1])

    def group_a(Nt, segs):
        fw.barrier()
        ssA = Acc(3)
        cnt = 0
        for cp in range(4):
            u_h, t_h = load_unit(wi_b[:, cp * 256:(cp + 1) * 256], KD, Twi_rest)
            u_c, t_c = load_unit(wi_b[:, (4 + cp) * 256:(5 + cp) * 256], KD, Twi_rest)
            u_b, t_b = load_unit(wi_b[:, (8 + cp) * 256:(9 + cp) * 256], KD, Twi_rest)
            for j in range(2):
                c = cp * 2 + j
                a_h, a_c, a_b = Acc(0), Acc(1), Acc(2)
                mm_chunk(a_h, u_h, t_h, j, hT, ThT, Nt, KD)
                mm_chunk(a_c, u_c, t_c, j, hT, ThT, Nt, KD)
                mm_chunk(a_b, u_b, t_b, j, hT, ThT, Nt, KD)
                pr = pre[cnt % 2]; cv = cvb[0]; cnt += 1
                fw.op(act, lambda e: e.copy(ahs.a[:, :Nt], a_h.a[:, :Nt]), reads=a_h.ts(Nt), writes=[ahs.t])
                conv_segs(cv, pr, 3, wca, c, segs, lambda st: (st.hA, st.ThA),
                          lambda dst, t0, n: fw.op(dve, lambda e: e.tensor_tensor(dst, a_c.a[:, t0:t0 + n], ahs.a[:, t0:t0 + n], ALU.mult),
                                                   reads=a_c.ts(Nt) + [ahs.t], writes=[pr.t]))
                fw.op(dve, lambda e: e.tensor_tensor(yraw.a[:, c, :Nt], a_b.a[:, :Nt], cv.a[:, :Nt], ALU.mult), reads=a_b.ts(Nt) + [cv.t], writes=[yraw.t])
                fw.op(act, lambda e: e.activation(sqb.a[:, :Nt], yraw.a[:, c, :Nt], AF.Square), reads=[yraw.t], writes=[sqb.t])
                for (o, n) in nsplit(Nt):
                    fw.op(pe, lambda e: e.matmul(ssA.a[:, o:o + n], ones_b[:, :], sqb.a[:, o:o + n], start=(c == 0), stop=(c == 7)),
                          reads=[sqb.t, Tc], writes=[ssA.t0 if o < 512 else ssA.t1])
        acc_ctr[0] = 0

        def tail():
            rsqrt_cols(ahs.a[:, :Nt], ssA.a[:, :Nt], 1.0 / CW, ssA.ts(Nt), [ahs.t])
            for c in range(8):
                fw.op(dve, lambda e: e.scalar_tensor_tensor(yT[:, c, :Nt], yraw.a[:, c, :Nt], gna[:, c:c + 1], ahs.a[:, :Nt], ALU.mult, ALU.mult),
                      reads=[yraw.t, ahs.t, Tp], writes=[TyT])
        return tail

    Tx1 = T("x1s")

    def out_and_ffn(Nt, segs):
        fw.barrier()
        blks = []
        for sg in segs:
            for (o, nb) in blocks_of(sg.n):
                blks.append((sg.t0 + o, nb))
        acc_pool[:] = [0, 1, 2]
        SSP = Acc(3)

        def proj_evac(A, m, gvec):
            sq = sqb2[m % 2]
            fw.op(act, lambda e: e.activation(sq.a[:, :Nt], A.a[:, :Nt], AF.Square), reads=A.ts(Nt), writes=[sq.t])
            fw.op(act, lambda e: e.activation(fT.a[:, m, :Nt], A.a[:, :Nt], AF.Copy, scale=gvec[:, m:m + 1]), reads=A.ts(Nt) + [Tp], writes=[fT.t])
            for bi, (o, nb) in enumerate(blks):
                if m == 0 and bi == 0:
                    fw.op(pe, lambda e: e.matmul(SSP.a[:nb, 512:517], sq.a[:, o:o + nb], e5[:, 0:5], start=True, stop=False),
                          reads=[sq.t, Tc], writes=[SSP.t1])
                else:
                    fw.op(pe, lambda e: e.matmul(SSP.a[:nb, 512 + bi:513 + bi], sq.a[:, o:o + nb], ones_b[:, 0:1], start=False,
                                                 stop=(m == KD - 1 and bi == len(blks) - 1)),
                          reads=[sq.t, Tc], writes=[SSP.t1])

        def token_major_resid(bi, o, nb, res=xin, tres=Txin):
            rsqrt_cols(t_rs.a[:nb], SSP.a[:nb, 512 + bi:513 + bi], 1.0 / D_MODEL, [SSP.t1], [t_rs.t])
            for g in range(4):
                A = next_acc()
                a3 = A.a[:, 0:512].rearrange("p (j t) -> p j t", j=4)
                for j in range(4):
                    m = g * 4 + j
                    fw.op(pe, lambda e: e.transpose(a3[:nb, j, :], fT.a[:, m, o:o + nb], ident_f[:, :]), reads=[fT.t, Tc], writes=[A.t0])
                fw.op(dve, lambda e: e.scalar_tensor_tensor(tok[:nb, g * 512:(g + 1) * 512], A.a[:nb, 0:512], t_rs.a[:nb],
                                                            res[:nb, g * 512:(g + 1) * 512], ALU.mult, ALU.add),
                      reads=[A.t0, t_rs.t, tres], writes=[Ttok])

        for up in range(8):
            ub, tub = load_unit(wo_b[:, up * 256:(up + 1) * 256], KD, Two)
            for j in range(2):
                m = up * 2 + j
                A = next_acc()
                mm_chunk(A, ub, tub, j, yT, TyT, Nt, KD)
                proj_evac(A, m, gqm)
        bsrc = []
        for sg in segs:
            for (o, nb) in blocks_of(sg.n):
                bsrc.append((sg, o))
        for bi, (o, nb) in enumerate(blks):
            sg, so = bsrc[bi]
            fw.dma(sp, d_x, xin[:nb], sg.xsrc[sg.row0 + so:sg.row0 + so + nb, :], writes=[Txin])
            token_major_resid(bi, o, nb)
            fw.dma(pool, d_x1, x1s[o:o + nb, :], tok[:nb], reads=[Ttok], writes=[Tx1])
            norm_block_to_T(tok, Ttok, nb, gpf, hT, ThT, o)
        cnt = 0
        for up in range(KF // 2):
            ug, tug = load_unit(wu_b[:, up * 256:(up + 1) * 256], KD, Twu)
            uv, tuv = load_unit(wu_b[:, D_FF + up * 256:D_FF + (up + 1) * 256], KD, Twu)
            for j in range(2):
                idx = up * 2 + j
                ag = next_acc(); av = next_acc()
                mm_chunk(ag, ug, tug, j, hT, ThT, Nt, KD)
                mm_chunk(av, uv, tuv, j, hT, ThT, Nt, KD)
                pr = pre2[0]; cv = cv2[0]; cnt += 1
                conv_segs(cv, pr, 3, wcf, idx, segs, lambda st: (st.hF, st.ThF),
                          lambda dst, t0, n: fw.op(act, lambda e: e.copy(dst, ag.a[:, t0:t0 + n]), reads=ag.ts(Nt), writes=[pr.t]))
                fw.op(act, lambda e: e.activation(cv.a[:, :Nt], cv.a[:, :Nt], AF.Silu), reads=[cv.t], writes=[cv.t])
                fw.op(dve, lambda e: e.tensor_tensor(actT.a[:, idx, :Nt], av.a[:, :Nt], cv.a[:, :Nt], ALU.mult), reads=av.ts(Nt) + [cv.t], writes=[actT.t])
        kparts = ((0, 16), (16, 16), (32, 12))
        for mp in range(8):
            a_m = [next_acc(), next_acc()]
            for pi, (k0, nk) in enumerate(kparts):
                ub, tub = load_unit(wd_b[k0 * 128:(k0 + nk) * 128, mp * 256:(mp + 1) * 256], nk, Twd)
                for j in range(2):
                    mm_chunk(a_m[j], ub, tub, j, actT.a, actT.t, Nt, nk, koff=k0, first=(pi == 0), last=(pi == 2))
            for j in range(2):
                proj_evac(a_m[j], mp * 2 + j, gqf)
        def final_gen():
            for bi, (o, nb) in enumerate(blks):
                fw.dma(sp, d_x1l, tok[:nb], x1s[o:o + nb, :], reads=[Tx1], writes=[Ttok])
                token_major_resid(bi, o, nb, tok, Ttok)
                sg, so = bsrc[bi]
                lo = max(0, sg.tok_lo - so)
                hi = min(nb, sg.tok_hi - so)
                if hi > lo:
                    r0 = sg.out_row0 + so + lo
                    fw.dma(pool, d_o, sg.out_ap[r0:r0 + hi - lo, :], tok[lo:hi], reads=[Ttok])
                yield
            acc_pool[:] = [0, 1, 2, 3]
        return final_gen()

    hT_prebuilt = [False]

    def tile_prefix(row0, Nt, gci0, next_segs):
        chk(1)
        segs = [Seg(0, Nt, Nt, 0, D["xpre"], row0)]
        if not hT_prebuilt[0]:
            build_hT(segs)
        hT_prebuilt[0] = False
        gdn_heads(Nt, False, segs)
        chk(3)
        gdn_chunks([(ci * 64, gci0 + ci, 0) for ci in range(Nt // 64)], False, build_hT_gen(next_segs))
        hT_prebuilt[0] = True
        chk(5)

    pending_final = [None]

    def tile_full(Nt, segs, chunks):
        chk(6)
        if hT_prebuilt[0]:
            hT_prebuilt[0] = False
        else:
            run_interleaved([pending_final[0], build_hT_gen(segs)])
        pending_final[0] = None
        ga_tail = group_a(Nt, segs)
        chk(7)
        gdn_heads(Nt, True, segs, ga_tail)
        chk(8)
        gdn_chunks(chunks, True)
        chk(11)
        pending_final[0] = out_and_ffn(Nt, segs)
        chk(10)

    stO = tok[:, 1024:1200]; TstO = Ttok
    stI = tok[:88, 1280:1536].rearrange("p (a c) -> p a c", a=2); TstI = Ttok

    def hist_specs(st):
        return ((st.hA, st.ThA, 2, 8, 0), (st.hG, st.ThG, 3, 24, 16), (st.hF, st.ThF, 2, KF, 88))

    def store_states(prefix, st):
        for (hb, th, K1, C, c0) in hist_specs(st):
            fw.op(pool, lambda e: e.tensor_copy(stO[:, c0:c0 + K1 * C].rearrange("p (t c) -> p c t", t=K1), hb[:]), reads=[th, TstO], writes=[TstO])
        A = next_acc()
        fw.op(pe, lambda e: e.transpose(A.a[:88, 0:128], stO[:, 0:88], ident_f[:, :]), reads=[TstO, Tc], writes=[A.t0])
        fw.op(pe, lambda e: e.transpose(A.a[:88, 128:256], stO[:, 88:176], ident_f[:, :]), reads=[TstO, Tc], writes=[A.t0])
        fw.op(dve, lambda e: e.tensor_copy(stI[:].rearrange("p a c -> p (a c)"), A.a[:88, 0:256]), reads=[A.t0, TstI], writes=[TstI])
        fw.dma(pool, d_st, D[prefix + "_nca"].rearrange("t (c p) -> (t c) p", p=128), stI[0:16, 0, :], reads=[TstI])
        fw.dma(pool, d_st, D[prefix + "_ngc"].rearrange("t (c p) -> (t c) p", p=128), stI[16:88, 0, :], reads=[TstI])
        fw.dma(pool, d_st, D[prefix + "_nfc"].rearrange("t (c p) -> (t c) p", p=128), stI[0:88, 1, :], reads=[TstI])
        fw.dma(pool, d_st, D[prefix + "_ngd"].rearrange("h d e -> d h e"), st.S[:], reads=[st.TS])
        fw.barrier()

    def load_states(st):
        d_ld = fw.dsem("ld")
        fw.dma(sp, d_ld, stI[0:16, 0, :], D["s_conv_a"].rearrange("t (c p) -> (t c) p", p=128), reads=[TstI], writes=[TstI])
        fw.dma(sp, d_ld, stI[16:88, 0, :], D["s_gdn_conv"].rearrange("t (c p) -> (t c) p", p=128), reads=[TstI], writes=[TstI])
        fw.dma(sp, d_ld, stI[0:88, 1, :], D["s_ffn_conv"].rearrange("t (c p) -> (t c) p", p=128), reads=[TstI], writes=[TstI])
        fw.dma(sp, fw.dsem("ldS"), st.S[:], D["s_gdn"].rearrange("h d e -> d h e"), reads=[st.TS], writes=[st.TS])
        A = next_acc()
        fw.op(pe, lambda e: e.transpose(A.a[:, 0:88], stI[:, 0, :], ident_f[:88, :88]), reads=[TstI, Tc], writes=[A.t0])
        fw.op(pe, lambda e: e.transpose(A.a[:, 88:176], stI[:, 1, :], ident_f[:88, :88]), reads=[TstI, Tc], writes=[A.t0])
        for (hb, th, K1, C, c0) in hist_specs(st):
            fw.op(dve, lambda e: e.tensor_copy(hb[:], A.a[:, c0:c0 + K1 * C].rearrange("p (t c) -> p c t", t=K1)), reads=[A.t0, th], writes=[th])
        fw.op(act, lambda e: e.copy(st.Sbf[:], st.S[:]), reads=[st.TS], writes=[st.TSbf])

    try:
        fw.dma(sp, fw.dsem("wl"), wl[:], wi_b[:, 7168:7184].rearrange("(k p) c -> p k c", p=128), reads=[Twi_kv], writes=[Twl])
        load_states(ST[1])
        p0 = ST[0]
        fw.op(pool, lambda e: e.memset(p0.hA[:], 0.0), writes=[p0.ThA])
        fw.op(pool, lambda e: e.memset(p0.hG[:], 0.0), writes=[p0.ThG])
        fw.op(pool, lambda e: e.memset(p0.hF[:], 0.0), writes=[p0.ThF])
        fw.op(pool, lambda e: e.memset(p0.S[:], 0.0), writes=[p0.TS])
        fw.op(pool, lambda e: e.memset(p0.Sbf[:], 0.0), writes=[p0.TSbf])
        gci = 0
        npre_t = NPRE // 512
        main0_segs = [Seg(0, 576, 576, 0, D["xmain"], 0, D["y"], -64, 64, 576)]
        for i in range(npre_t):
            nsegs = [Seg(0, 512, 512, 0, D["xpre"], (i + 1) * 512)] if i + 1 < npre_t else main0_segs
            tile_prefix(i * 512, 512, gci, nsegs)
            gci += 8
        cast_pump(10000)
        row = 0
        for ti, n_main in enumerate((576, 512, 512, 512)):
            segs = [Seg(0, n_main, n_main, 0, D["xmain"], row, D["y"], row - 64, max(0, 64 - row), n_main)]
            chunks = [(ci * 64, gci + ci, 0) for ci in range(n_main // 64)]
            Nt = n_main
            if ti == 3:
                segs.append(Seg(n_main, NSAMP, 16, 1, D["xsamp"], 0, D["ys"], 0, 0, 16))
                chunks.append((n_main, NCHUNK_ALL - 1, 1))
                Nt = n_main + NSAMP
            tile_full(Nt, segs, chunks)
            gci += n_main // 64
            row += n_main
        run_interleaved([pending_final[0]])
        fw.barrier()
        store_states("p", ST[0])
        store_states("s", ST[1])
    except _Stop:
        pass
    fw.finish()
    return nc, fw


_CACHE = {}


def kernel(x_prompt, x_sample, state_conv_a, state_gdn_conv, state_gdn, state_ffn_conv, meta_tokens,
           g_pre_mix, w_in, w_conv_a, g_norm_a, w_conv_gdn, a_log, dt_bias, g_norm_gdn, w_out, g_post_mix,
           g_pre_ffn, w_up, w_conv_ffn, w_down, g_post_ffn):
    f = lambda a: np.ascontiguousarray(np.asarray(a, dtype=np.float32))
    x_prompt, x_sample, meta = f(x_prompt), f(x_sample), f(meta_tokens)
    if "nc" not in _CACHE:
        _CACHE["nc"] = build_program()[0]
    nc = _CACHE["nc"]
    shared = {"g_pre_mix": f(g_pre_mix), "w_in": f(w_in[0]), "w_conv_a": f(w_conv_a[0]), "g_norm_a": f(g_norm_a),
              "w_conv_gdn": f(w_conv_gdn[0]), "a_log": f(a_log), "dt_bias": f(dt_bias), "g_norm_gdn": f(g_norm_gdn),
              "w_out": f(w_out[0]), "g_post_mix": f(g_post_mix), "g_pre_ffn": f(g_pre_ffn), "w_up": f(w_up[0]),
              "w_conv_ffn": f(w_conv_ffn[0]), "w_down": f(w_down[0]), "g_post_ffn": f(g_post_ffn)}
    in_maps = []
    zeros48 = np.zeros((48, D_MODEL), np.float32)
    for c in range(8):
        b, half = c // 2, c % 2
        full = np.concatenate([zeros48, meta, x_prompt[b]], axis=0)
        tmask = np.ones((NCHUNK_ALL, 64), np.float32)
        if half == 0:
            xpre = np.zeros((NPRE, D_MODEL), np.float32)
            xmain = full[0:NMAIN]
            tmask[32, :48] = 0.0
        else:
            xpre = full[0:NPRE]
            xmain = full[NPRE:NPRE + NMAIN]
            tmask[0, :48] = 0.0
        xsamp = np.concatenate([x_sample[c], np.zeros((48, D_MODEL), np.float32)], axis=0)
        tmask[65, 16:] = 0.0
        m = dict(shared)
        m.update({"xpre": np.ascontiguousarray(xpre), "xmain": np.ascontiguousarray(xmain), "xsamp": xsamp, "tmask": tmask,
                  "s_conv_a": f(state_conv_a[0, c]), "s_gdn_conv": f(state_gdn_conv[0, c]), "s_gdn": f(state_gdn[0, c]),
                  "s_ffn_conv": f(state_ffn_conv[0, c])})
        in_maps.append(m)
    res = run_bass_kernel_spmd(nc, in_maps, core_ids=list(range(8)))
    R = res.results
    y_prompt = np.stack([np.concatenate([R[2 * b]["y"], R[2 * b + 1]["y"]], axis=0) for b in range(4)])
    y_sample = np.stack([R[c]["ys"] for c in range(8)])

    def st(key, cores):
        return np.stack([R[c][key] for c in cores])[None]
    pc = [1, 3, 5, 7]
    sc = list(range(8))
    return (y_prompt, y_sample,
            st("p_nca", pc), st("p_ngc", pc), st("p_ngd", pc), st("p_nfc", pc),
            st("s_nca", sc), st("s_ngc", sc), st("s_ngd", sc), st("s_nfc", sc))
```

```python
import numpy as np
import concourse.bass as bass
import concourse.mybir as mybir
from concourse.bass_utils import run_bass_kernel_spmd

F32 = mybir.dt.float32
BF16 = mybir.dt.bfloat16
ALU = mybir.AluOpType
AF = mybir.ActivationFunctionType

D_MODEL = 2048
SEQ = 4096
N_META = 16
CW = 1024
NH = 8
DK = 128
QKV = 3072
D_FF = 5632
IN_COLS = 7184
EPS = 1e-6
NPRE = 2048
NMAIN = 2112
NSAMP = 64
NCHUNK_ALL = (NPRE + NMAIN + NSAMP) // 64
NTMAX = 576
KD = D_MODEL // 128
KF = D_FF // 128


class Eng:
    def __init__(self, name, h, sem, is_pe=False):
        self.name, self.h, self.sem = name, h, sem
        self.cnt = 0
        self.seen = {}
        self.is_pe = is_pe


class DSem:
    def __init__(self, sem, name):
        self.sem, self.name, self.cnt = sem, name, 0


class T:
    def __init__(self, name="", psum=False):
        self.name = name
        self.w = None
        self.r = []
        self.psum = psum


class FW:
    def __init__(self, nc):
        self.nc = nc
        self.stack = []
        self.pe = self._eng("pe", nc.tensor, True)
        self.dve = self._eng("dve", nc.vector)
        self.act = self._eng("act", nc.scalar)
        self.pool = self._eng("pool", nc.gpsimd)
        self.sp = self._eng("sp", nc.sync)
        self.engs = [self.pe, self.dve, self.act, self.pool, self.sp]
        self.dsems = []
        self.ninst = 0

    def sem(self, name):
        g = self.nc.semaphore(name)
        s = g.__enter__()
        self.stack.append(g)
        return s

    def _eng(self, name, h, is_pe=False):
        return Eng(name, h, self.sem("s_" + name), is_pe)

    def dsem(self, name):
        d = DSem(self.sem("d_" + name), name)
        self.dsems.append(d)
        return d

    def _wait(self, eng, e, c):
        if eng.seen.get(e, 0) < c:
            eng.h.wait_ge(e.sem, c)
            eng.seen[e] = c
            self.ninst += 1

    def _deps(self, eng, reads, writes):
        deps = {}

        def add(p):
            if p is None:
                return
            e, c = p
            if eng.is_pe and e is eng:
                return
            if deps.get(e, 0) < c:
                deps[e] = c
        for t in reads:
            add(t.w)
            if t.psum:
                for r in t.r:
                    if r[0] is not eng:
                        add(r)
        for t in writes:
            add(t.w)
            for r in t.r:
                add(r)
        for e, c in deps.items():
            self._wait(eng, e, c)

    def _mark(self, me, reads, writes):
        for t in reads:
            t.r.append(me)
            if len(t.r) > 24:
                best = {}
                for (e, c) in t.r:
                    if best.get(e, 0) < c:
                        best[e] = c
                t.r = list(best.items())
        for t in writes:
            t.w = me
            t.r = []

    def op(self, eng, emit, reads=(), writes=()):
        self._deps(eng, reads, writes)
        ins = emit(eng.h)
        eng.cnt += 1
        ins.then_inc(eng.sem, 1)
        self.ninst += 1
        self._mark((eng, eng.cnt), reads, writes)
        return ins

    def dma(self, q, dsem, out, in_, reads=(), writes=(), waw=True, **kw):
        self._deps(q, reads, writes if waw else ())
        ins = q.h.dma_start(out=out, in_=in_, **kw)
        dsem.cnt += 16
        ins.then_inc(dsem.sem, 16)
        self.ninst += 1
        self._mark((dsem, dsem.cnt), reads, writes)

    def barrier(self):
        for a in self.engs:
            for b in self.engs:
                if a is not b and b.cnt:
                    self._wait(a, b, b.cnt)
            for d in self.dsems:
                if d.cnt:
                    self._wait(a, d, d.cnt)

    def finish(self):
        self.barrier()


def nsplit(n):
    out = []
    o = 0
    while o < n:
        m = min(512, n - o)
        out.append((o, m))
        o += m
    return out


def blocks_of(n):
    out = []
    o = 0
    while o < n:
        m = min(128, n - o)
        out.append((o, m))
        o += m
    return out


class _Stop(Exception):
    pass


class Buf:
    def __init__(self, a, name="", psum=False):
        self.a = a
        self.t = T(name, psum)


def run_interleaved(gens):
    gens = [g for g in gens if g is not None]
    while gens:
        for g in list(gens):
            try:
                next(g)
            except StopIteration:
                gens.remove(g)


def build_program(stage=99):
    def chk(n):
        if stage == n:
            raise _Stop()

    nc = bass.Bass("TRN2", target_bir_lowering=False)
    D = {}

    def din(name, shape):
        D[name] = nc.dram_tensor(name, list(shape), F32, kind="ExternalInput").ap()

    def dout(name, shape):
        D[name] = nc.dram_tensor(name, list(shape), F32, kind="ExternalOutput").ap()

    din("xpre", (NPRE, D_MODEL)); din("xmain", (NMAIN, D_MODEL)); din("xsamp", (NSAMP, D_MODEL))
    din("tmask", (NCHUNK_ALL, 64))
    din("s_conv_a", (2, CW)); din("s_gdn_conv", (3, QKV)); din("s_gdn", (NH, DK, DK)); din("s_ffn_conv", (2, D_FF))
    din("g_pre_mix", (1, D_MODEL)); din("w_in", (D_MODEL, IN_COLS)); din("w_conv_a", (3, CW))
    din("g_norm_a", (1, CW)); din("w_conv_gdn", (4, QKV)); din("a_log", (1, NH)); din("dt_bias", (1, NH))
    din("g_norm_gdn", (1, DK)); din("w_out", (D_MODEL, D_MODEL)); din("g_post_mix", (1, D_MODEL))
    din("g_pre_ffn", (1, D_MODEL)); din("w_up", (D_MODEL, 2 * D_FF)); din("w_conv_ffn", (3, D_FF))
    din("w_down", (D_FF, D_MODEL)); din("g_post_ffn", (1, D_MODEL))
    dout("y", (NMAIN - 64, D_MODEL)); dout("ys", (16, D_MODEL))
    for p in ("p", "s"):
        dout(p + "_nca", (2, CW)); dout(p + "_ngc", (3, QKV)); dout(p + "_ngd", (NH, DK, DK)); dout(p + "_nfc", (2, D_FF))
    wi_b = nc.dram_tensor("wi_b", [D_MODEL, IN_COLS], BF16, kind="Internal").ap()
    wo_b = nc.dram_tensor("wo_b", [D_MODEL, D_MODEL], BF16, kind="Internal").ap()
    wu_b = nc.dram_tensor("wu_b", [D_MODEL, 2 * D_FF], BF16, kind="Internal").ap()
    wd_b = nc.dram_tensor("wd_b", [D_FF, D_MODEL], BF16, kind="Internal").ap()
    x1s = nc.dram_tensor("x1s", [640, D_MODEL], F32, kind="Internal").ap()

    fw = FW(nc)
    pe, dve, act, pool, sp = fw.pe, fw.dve, fw.act, fw.pool, fw.sp

    def sb(name, shape, dt=F32):
        return nc.alloc_sbuf_tensor(name, list(shape), dt)

    ACC = [nc.alloc_psum_tensor(f"acc{i}", [128, 1024], F32) for i in range(4)]
    TB = [T(f"bank{i}", psum=True) for i in range(8)]
    acc_ctr = [0]

    class Acc:
        def __init__(self, i):
            self.i = i
            self.a = ACC[i]
            self.t0 = TB[2 * i]
            self.t1 = TB[2 * i + 1]
            self.tt = [self.t0, self.t1]

        def ts(self, width):
            return [self.t0] if width <= 512 else self.tt

    acc_pool = [0, 1, 2, 3]

    def next_acc():
        i = acc_pool[acc_ctr[0] % len(acc_pool)]
        acc_ctr[0] += 1
        return Acc(i)

    ident_f = sb("ident_f", [128, 128]); ident_b = sb("ident_b", [128, 128], BF16)
    ones_b = sb("ones_b", [128, 128], BF16); ones_f = sb("ones_f", [64, 128])
    U8 = sb("U8", [64, 8, 64]); SU8 = sb("SU8", [64, 8, 64]); L8 = sb("L8", [64, 8, 64]); I8 = sb("I8", [64, 8, 64])
    eps_c = sb("eps_c", [128, 1]); one_c = sb("one_c", [128, 1])
    Tc = T("consts")
    fw.op(pool, lambda e: e.memset(ident_f[:], 0.0), writes=[Tc])
    fw.op(pool, lambda e: e.affine_select(out=ident_f[:], in_=ident_f[:], pattern=[[-1, 128]], compare_op=ALU.not_equal,
                                          fill=1.0, base=0, channel_multiplier=1), reads=[Tc], writes=[Tc])
    fw.op(pool, lambda e: e.tensor_copy(ident_b[:], ident_f[:]), reads=[Tc], writes=[Tc])
    fw.op(pool, lambda e: e.memset(ones_b[:], 1.0), writes=[Tc])
    fw.op(pool, lambda e: e.memset(ones_f[:], 1.0), writes=[Tc])
    fw.op(pool, lambda e: e.memset(eps_c[:], EPS), writes=[Tc])
    fw.op(pool, lambda e: e.memset(one_c[:], 1.0), writes=[Tc])
    mhalf_c = sb("mhalf_c", [128, 1])
    fw.op(pool, lambda e: e.memset(mhalf_c[:], -0.5), writes=[Tc])
    e5 = sb("e5", [128, 8], BF16)
    fw.op(pool, lambda e: e.memset(e5[:], 0.0), writes=[Tc])
    fw.op(pool, lambda e: e.memset(e5[:, 0:1], 1.0), reads=[Tc], writes=[Tc])
    for (m, pat, cm, cmp_) in ((U8, [[0, 8], [1, 64]], -1, ALU.is_ge), (SU8, [[0, 8], [1, 64]], -1, ALU.is_gt),
                               (L8, [[0, 8], [-1, 64]], 1, ALU.is_gt), (I8, [[0, 8], [-1, 64]], 1, ALU.is_equal)):
        fw.op(pool, lambda e: e.memset(m[:], 1.0), reads=[Tc], writes=[Tc])
        fw.op(pool, lambda e: e.affine_select(out=m[:], in_=m[:], pattern=pat, compare_op=cmp_, fill=0.0, base=0,
                                              channel_multiplier=cm), reads=[Tc], writes=[Tc])
    NEGU = sb("NEGU", [64, 8, 64], BF16); NEGL = sb("NEGL", [64, 8, 64], BF16)
    fw.op(dve, lambda e: e.tensor_scalar(NEGU[:], SU8[:], 30000.0, -30000.0, ALU.mult, ALU.add), reads=[Tc], writes=[Tc])
    fw.op(dve, lambda e: e.tensor_scalar(NEGL[:], L8[:], 30000.0, -30000.0, ALU.mult, ALU.add), reads=[Tc], writes=[Tc])
    Umat = U8[:, 0, :]
    Lmat = L8[:, 0, :]
    ULb = sb("ULb", [64, 2, 64], BF16)
    fw.op(pool, lambda e: e.tensor_copy(ULb[:, 0, :], Umat), reads=[Tc], writes=[Tc])
    fw.op(pool, lambda e: e.tensor_copy(ULb[:, 1, :], Lmat), reads=[Tc], writes=[Tc])
    Umat_b = ULb[:, 0, :]
    Lmat_b = ULb[:, 1, :]

    Twi_kv, Twi_rest, Two, Twu, Twd = T("wi_kv"), T("wi_rest"), T("wo"), T("wu"), T("wd")

    cast_sems = {}
    cast_q = []

    def cast(dst, src, tr, rows, c0, c1, rstep=256, defer=False):
        if tr.name not in cast_sems:
            cast_sems[tr.name] = fw.dsem("cast_" + tr.name)
        for r0 in range(0, rows, rstep):
            item = (dst[r0:r0 + rstep, c0:c1], src[r0:r0 + rstep, c0:c1], tr)
            if defer:
                cast_q.append(item)
            else:
                cast_emit(item)

    def cast_emit(item):
        d_, s_, tr = item
        fw.dma(pool, cast_sems[tr.name], d_, s_, writes=[tr], waw=False)

    def cast_pump(n):
        for _ in range(n):
            if cast_q:
                cast_emit(cast_q.pop(0))

    d_par = fw.dsem("par")
    Tp = T("params")
    xin = sb("xin", [128, D_MODEL]); Txin = T("xin")
    tok = sb("tok", [128, D_MODEL]); Ttok = T("tok")
    pst = xin[:, 0:512].rearrange("p (q c) -> p q c", q=4); Tpst = Txin
    stT = xin[:, 512:576]
    par = sb("par", [128, 4, 128])
    tm = sb("tm", [64, NCHUNK_ALL])
    alog_r = sb("alog_r", [128, 8]); dtb_r = sb("dtb_r", [128, 8]); nA_r = sb("nA_r", [128, 8])
    fw.op(pool, lambda e: e.memset(xin[:, 0:576], 0.0), writes=[Tpst])

    d_pst = fw.dsem("pst")

    def pl(dst, src):
        fw.dma(sp, d_pst, dst, src, reads=[Tpst], writes=[Tpst])
    r128 = lambda ap1d: ap1d.rearrange("(k p) -> k p", p=128)
    pl(pst[0:16, 0, :], r128(D["g_pre_mix"][0]))
    pl(pst[16:32, 0, :], r128(D["g_pre_ffn"][0]))
    pl(pst[32:40, 0, :], r128(D["g_norm_a"][0]))
    pl(pst[40:41, 0, :], r128(D["g_norm_gdn"][0]))
    pl(pst[41:65, 0, :], D["w_conv_a"].rearrange("j (c p) -> (j c) p", p=128))
    pl(pst[65:81, 0, :], r128(D["g_post_mix"][0]))
    pl(pst[81:97, 0, :], r128(D["g_post_ffn"][0]))
    pl(pst[0:96, 1, :], D["w_conv_gdn"].rearrange("j (c p) -> (j c) p", p=128))
    wcf_rows = D["w_conv_ffn"].rearrange("j (c p) -> (j c) p", p=128)
    pl(pst[0:128, 2, :], wcf_rows[0:128, :])
    pl(pst[0:4, 3, :], wcf_rows[128:132, :])
    pl(stT[0:NCHUNK_ALL, :], D["tmask"])
    fw.dma(sp, d_par, alog_r[:], D["a_log"][0].partition_broadcast(128), writes=[Tp])
    fw.dma(sp, d_par, dtb_r[:], D["dt_bias"][0].partition_broadcast(128), writes=[Tp])
    cast(wi_b, D["w_in"], Twi_kv, D_MODEL, 4096, IN_COLS)
    a_ = next_acc()
    for q in range(4):
        fw.op(pe, lambda e: e.transpose(a_.a[:, q * 128:(q + 1) * 128], pst[:, q, :], ident_f[:, :]), reads=[Tpst, Tc], writes=[a_.t0])
    fw.op(dve, lambda e: e.tensor_copy(par[:].rearrange("p q c -> p (q c)"), a_.a[:, 0:512]), reads=[a_.t0], writes=[Tp])
    fw.op(pe, lambda e: e.transpose(a_.a[:64, 512:512 + NCHUNK_ALL], stT[:NCHUNK_ALL, :], ident_f[:NCHUNK_ALL, :NCHUNK_ALL]),
          reads=[Tpst, Tc], writes=[a_.t1])
    fw.op(dve, lambda e: e.tensor_copy(tm[:], a_.a[:64, 512:512 + NCHUNK_ALL]), reads=[a_.t1], writes=[Tp])
    gpm = par[:, 0, 0:16]; gpf = par[:, 0, 16:32]; gna = par[:, 0, 32:40]; gng = par[:, 0, 40:41]
    gqm = par[:, 0, 65:81]; gqf = par[:, 0, 81:97]

    def wca(c, j):
        return par[:, 0, 41 + j * 8 + c:42 + j * 8 + c]

    def wcg(c, j):
        return par[:, 1, j * 24 + c:j * 24 + c + 1]

    def wcf(c, j):
        r = j * KF + c
        return par[:, 2, r:r + 1] if r < 128 else par[:, 3, r - 128:r - 127]
    fw.op(act, lambda e: e.activation(nA_r[:], alog_r[:], AF.Exp), reads=[Tp], writes=[Tp])
    fw.op(dve, lambda e: e.tensor_scalar(nA_r[:], nA_r[:], -1.0, None, ALU.mult), reads=[Tp], writes=[Tp])

    cast(wi_b, D["w_in"], Twi_rest, D_MODEL, 0, 4096, rstep=128, defer=True)
    cast(wo_b, D["w_out"], Two, D_MODEL, 0, D_MODEL, rstep=256, defer=True)
    cast(wu_b, D["w_up"], Twu, D_MODEL, 0, 2 * D_FF, rstep=64, defer=True)
    cast(wd_b, D["w_down"], Twd, D_FF, 0, D_MODEL, rstep=256, defer=True)

    xsb = sb("xsb", [128, D_MODEL], BF16); Txsb = T("xsb")
    junk = xsb; Tjunk = Txsb
    hT = sb("hT", [128, KD, NTMAX], BF16); ThT = T("hT")
    yT = sb("yT", [128, KD, NTMAX], BF16); TyT = T("yT")
    class St:
        pass

    def mk_state(sfx):
        st = St()
        st.S = sb("S" + sfx, [128, NH, DK]); st.TS = T("S" + sfx)
        st.Sbf = sb("Sbf" + sfx, [128, NH, DK], BF16); st.TSbf = T("Sbf" + sfx)
        st.hA = sb("histA" + sfx, [128, 8, 2]); st.ThA = T("hA" + sfx)
        st.hG = sb("histG" + sfx, [128, 24, 3]); st.ThG = T("hG" + sfx)
        st.hF = sb("histF" + sfx, [128, KF, 2]); st.ThF = T("hF" + sfx)
        return st
    ST = [mk_state(""), mk_state("_s")]
    dSs = tok[:, 0:1024].rearrange("p (h d) -> p h d", h=NH); TdSs = Ttok

    class Seg:
        def __init__(self, t0, n, nreal, sidx, xsrc, row0, out_ap=None, out_row0=0, tok_lo=0, tok_hi=0):
            self.t0, self.n, self.nreal, self.sidx, self.xsrc, self.row0 = t0, n, nreal, sidx, xsrc, row0
            self.out_ap, self.out_row0, self.tok_lo, self.tok_hi = out_ap, out_row0, tok_lo, tok_hi
    wl = sb("wl", [128, KD, 16], BF16); Twl = T("wl")
    smt = sb("smt", [128, 160])

    def small(c0, w, name, parts=128):
        return Buf(smt[:parts, c0:c0 + w], name)
    n_ss = small(0, 1, "n_ss"); n_rs = small(1, 1, "n_rs"); t_ssc = small(2, 4, "t_ssc"); t_st = small(6, 1, "t_st"); t_rs = small(7, 1, "t_rs")
    sm_beta = [small(8 + 64 * i, 8, f"beta{i}", 64) for i in range(2)]
    sm_gg = [small(16 + 64 * i, 8, f"gg{i}", 64) for i in range(2)]
    sm_t8 = [small(24 + 64 * i, 8, f"t8{i}", 64) for i in range(2)]
    sm_egc = [small(32 + 64 * i, 8, f"egc{i}", 64) for i in range(2)]
    sm_edl = [small(40 + 64 * i, 8, f"edl{i}", 64) for i in range(2)]
    sm_cb = [small(48 + 64 * i, 8, f"cb{i}", 64) for i in range(2)]
    sm_egl = [small(56 + 64 * i, 8, f"egl{i}", 128) for i in range(2)]
    sm_gc = [small(64 + 64 * i, 8, f"gc{i}", 64) for i in range(2)]
    smb = sb("smb", [64, 2, 2, 16], BF16)
    sm_r = sb("sm_r", [64, 2, 16])
    sm_split = [Buf(smb[:, i], f"split{i}") for i in range(2)]
    sm_res = [Buf(sm_r[:, i], f"res{i}") for i in range(2)]
    NU = 3
    wun = [sb(f"wu{i}", [128, 16, 256], BF16) for i in range(NU)]
    Twun = [T(f"wun{i}") for i in range(NU)]
    d_wun = [fw.dsem(f"wun{i}") for i in range(NU)]
    d_x = fw.dsem("x"); d_o = fw.dsem("o"); d_st = fw.dsem("st"); d_x1 = fw.dsem("x1"); d_x1l = fw.dsem("x1l")
    unit_ctr = [0]

    UN = 25200
    UNI = sb("UNI", [128, UN])
    upos = [0]

    def carve(nelem_f32):
        a = upos[0]
        upos[0] += nelem_f32
        assert upos[0] <= UN, upos[0]
        return UNI[:, a:a + nelem_f32]

    h3 = "p (h t) -> p h t"

    def cf(n, name, parts=128, h=None):
        a = carve(n)[:parts]
        if h:
            a = a.rearrange(h3, h=h)
        return Buf(a, name)

    def cb16(n_bf, name, parts=128, h=None):
        a = carve(n_bf // 2).bitcast(BF16)[:parts]
        if h:
            a = a.rearrange(h3, h=h)
        return Buf(a, name)

    qnT = cb16(NH * NTMAX, "qnT", h=NH); knT = cb16(NH * NTMAX, "knT", h=NH)
    vT = cb16(NH * NTMAX, "vT", h=NH); szT = cb16(NH * NTMAX, "szT", h=NH)
    pre = [cf(NTMAX + 8, f"pre{i}") for i in range(2)]
    cvb = [cf(NTMAX, f"cvb{i}") for i in range(2)]
    qsb = [cf(NTMAX, f"qs{i}") for i in range(2)]; ahs = cf(NTMAX, "ahs")
    sqbb = [cb16(NTMAX, f"sqb{i}") for i in range(2)]
    sqb = sqbb[0]
    alias0 = upos[0]
    yraw = cf(8 * NTMAX, "yraw", h=8)
    upos[0] = alias0
    def cf2(name):
        a = carve(512).bitcast(BF16)[:64].rearrange("p (s h t) -> p s h t", s=2, h=8)
        return Buf(a, name)
    rhsG = cf2("rhsG"); rhsL = cf2("rhsL"); rhsB = cf2("rhsB")
    EG = cb16(512, "EG", 128, 8)
    DTi = cb16(512, "DTi", 64, 8); DTs = cb16(512, "DTs", 64, 8); Dm = cb16(512, "Dm", 64, 8)
    otmp = cf(512, "otmp", 128, 8); rso = cf(512, "rso", 128, 8)
    kbT = cb16(512, "kbT", 128, 8)
    Pm = [cb16(512, f"P{i}", 64, 8) for i in range(2)]
    PTm = [cb16(512, f"PT{i}", 64, 8) for i in range(2)]
    TT = cb16(512, "TT", 64, 8); TTc = cb16(512, "TTc", 64, 8)
    vnew = cb16(1024, "vnew", 64, 8); vnews = cb16(1024, "vnews", 64, 8)
    osq = cb16(512, "osq", 128, 8)
    qgT = [cb16(512, f"qgT{i}", 128, 8) for i in range(2)]
    ktk = [cb16(1024, f"ktk{i}", 64, 8) for i in range(2)]
    vtk = [cb16(1024, f"vtk{i}", 64, 8) for i in range(2)]
    TTb = [cb16(512, f"TTb{i}", 64, 8) for i in range(2)]
    attnT = [cb16(512, f"attnT{i}", 64, 8) for i in range(2)]
    negwT = [cb16(512, f"negwT{i}", 128, 8) for i in range(2)]
    mix_end = upos[0]
    upos[0] = 0
    fT = Buf(carve(KD * NTMAX).rearrange("p (m t) -> p m t", m=KD), "fT")
    actT = Buf(carve(KF * NTMAX // 2).bitcast(BF16).rearrange("p (k t) -> p k t", k=KF), "actT")
    pre2 = [cf(NTMAX + 8, f"pre2{i}") for i in range(1)]
    cv2 = [cf(NTMAX, f"cv2{i}") for i in range(1)]
    sqb2 = [cb16(NTMAX, f"sqb2{i}") for i in range(2)]
    assert max(mix_end, upos[0]) <= UN, (mix_end, upos[0])

    def load_unit(src_ap, nk, tr_src):
        i = unit_ctr[0] % NU
        unit_ctr[0] += 1
        fw.dma(sp, d_wun[i], wun[i][:, 0:nk, :], src_ap.rearrange("(k p) c -> p k c", p=128), reads=[tr_src], writes=[Twun[i]])
        return wun[i], Twun[i]

    def mm_chunk(A, ub, tub, j, src, tsrc, Nt, nk, koff=0, first=True, last=True):
        for (o, n) in nsplit(Nt):
            for k in range(nk):
                fw.op(pe, lambda e: e.matmul(A.a[:, o:o + n], ub[:, k, j * 128:(j + 1) * 128], src[:, koff + k, o:o + n],
                                             start=(first and k == 0), stop=(last and k == nk - 1)),
                      reads=[tub, tsrc], writes=[A.t0 if o < 512 else A.t1])

    def rsqrt_cols(dst, src, scale, rd, wr):
        P = dst.shape[0]
        if len(dst.shape) == 2 and dst.shape[1] == 1:
            fw.op(act, lambda e: e.activation(dst, src, AF.Identity, bias=eps_c[:P], scale=scale), reads=rd + [Tc], writes=wr)
            fw.op(pool, lambda e: e.tensor_tensor(dst, dst, mhalf_c[:P], ALU.pow), reads=wr + [Tc], writes=wr)
            return
        fw.op(act, lambda e: e.activation(dst, src, AF.Sqrt, bias=eps_c[:P], scale=scale), reads=rd + [Tc], writes=wr)
        fw.op(dve, lambda e: e.reciprocal(dst, dst), reads=wr, writes=wr)

    def rsqrt_big(dst, src, scale, rd, wr):
        P = dst.shape[0]
        fw.op(act, lambda e: e.activation(dst, src, AF.Ln, bias=eps_c[:P], scale=scale), reads=rd + [Tc], writes=wr)
        fw.op(act, lambda e: e.activation(dst, dst, AF.Exp, scale=-0.5), reads=wr, writes=wr)

    def norm_block_to_T(src_tile, tsrc, nb, gvec, dstT, tdst, t0):
        ss = n_ss.a[:nb]
        rs = n_rs.a[:nb]
        fw.op(act, lambda e: e.activation(junk[:nb], src_tile[:nb], AF.Square, accum_out=ss), reads=[tsrc], writes=[Tjunk, n_ss.t])
        rsqrt_cols(rs, ss, 1.0 / D_MODEL, [n_ss.t], [n_rs.t])
        fw.op(act, lambda e: e.activation(xsb[:nb], src_tile[:nb], AF.Copy, scale=rs), reads=[tsrc, n_rs.t], writes=[Txsb])
        for g in range(4):
            A = next_acc()
            accb = A.a[:].bitcast(BF16)[:, 0:512].rearrange("p (j t) -> p j t", j=4)
            for j in range(4):
                k = g * 4 + j
                fw.op(pe, lambda e: e.transpose(accb[:, j, :nb], xsb[:nb, k * 128:(k + 1) * 128], ident_b[:nb, :nb]),
                      reads=[Txsb, Tc], writes=[A.t0])
            fw.op(dve, lambda e: e.tensor_tensor(dstT[:, g * 4:(g + 1) * 4, t0:t0 + nb], accb[:, :, :nb],
                                                 gvec[:, g * 4:(g + 1) * 4, None].broadcast_to([128, 4, nb]), ALU.mult),
                  reads=[A.t0, Tp], writes=[tdst])

    def build_hT_gen(segs):
        for sg in segs:
            for (o, nb) in blocks_of(sg.n):
                fw.dma(sp, d_x, xin[:nb], sg.xsrc[sg.row0 + o:sg.row0 + o + nb, :], writes=[Txin])
                norm_block_to_T(xin, Txin, nb, gpm, hT, ThT, sg.t0 + o)
                yield

    def build_hT(segs):
        for _ in build_hT_gen(segs):
            pass

    def conv_segs_gen(cv, pr, K, wfn, widx, segs, hist_of, fill):
        K1 = K - 1
        base = 0
        bases = []
        for sg in segs:
            hb, th = hist_of(ST[sg.sidx])
            fw.op(pool, lambda e: e.tensor_copy(pr.a[:, base:base + K1], hb[:, widx, :]), reads=[th], writes=[pr.t])
            fill(pr.a[:, base + K1:base + K1 + sg.n], sg.t0, sg.n)
            fw.op(pool, lambda e: e.tensor_copy(hb[:, widx, :], pr.a[:, base + sg.nreal:base + sg.nreal + K1]), reads=[pr.t], writes=[th])
            bases.append(base)
            base += K1 + sg.n
        yield
        for sg, base in zip(segs, bases):
            d = cv.a[:, sg.t0:sg.t0 + sg.n]
            fw.op(dve, lambda e: e.tensor_scalar(d, pr.a[:, base:base + sg.n], wfn(widx, 0), None, ALU.mult), reads=[pr.t, Tp], writes=[cv.t])
            for j in range(1, K):
                fw.op(dve, lambda e: e.scalar_tensor_tensor(d, pr.a[:, base + j:base + j + sg.n], wfn(widx, j), d, ALU.mult, ALU.add),
                      reads=[pr.t, cv.t, Tp], writes=[cv.t])
        yield

    def conv_segs(*args):
        for _ in conv_segs_gen(*args):
            pass

    def to_token_major(srcT, o, nb, dst_tile, tdst):
        for g in range(4):
            A = next_acc()
            a3 = A.a[:, 0:512].rearrange("p (j t) -> p j t", j=4)
            for j in range(4):
                m = g * 4 + j
                fw.op(pe, lambda e: e.transpose(a3[:nb, j, :], srcT.a[:, m, o:o + nb], ident_f[:, :]),
                      reads=[srcT.t, Tc], writes=[A.t0])
            fw.op(act, lambda e: e.activation(junk[:nb, 0:512], A.a[:nb, 0:512], AF.Square, accum_out=t_ssc.a[:nb, g:g + 1]),
                  reads=[A.t0], writes=[Tjunk, t_ssc.t])
            fw.op(dve, lambda e: e.tensor_copy(dst_tile[:nb, g * 512:(g + 1) * 512], A.a[:nb, 0:512]), reads=[A.t0], writes=[tdst])

    def bc(ap2, n):
        return ap2[:, :, None].broadcast_to([ap2.shape[0], ap2.shape[1], n])

    s1ctr = [0]

    def s1_acc():
        i = s1ctr[0] % 2
        s1ctr[0] += 1
        return Acc(i)

    front_done = set()
    WARM = 0

    def s1_front(t0, gci, pb):
        c64 = slice(t0, t0 + 64)
        beta, gg, t8 = sm_beta[pb], sm_gg[pb], sm_t8[pb]
        AL = s1_acc()
        for k in range(KD):
            fw.op(pe, lambda e: e.matmul(AL.a[:64, 0:16], hT[:, k, c64], wl[:, k, :], start=(k == 0), stop=(k == KD - 1)),
                  reads=[ThT, Twl], writes=[AL.t0])
        mcol = tm[:, gci:gci + 1]
        fw.op(act, lambda e: e.activation(beta.a, AL.a[:64, 0:8], AF.Exp, scale=-1.0), reads=[AL.t0], writes=[beta.t])
        fw.op(dve, lambda e: e.tensor_tensor(t8.a, AL.a[:64, 8:16], dtb_r[:64], ALU.add), reads=[AL.t0, Tp], writes=[t8.t])
        fw.op(act, lambda e: e.activation(t8.a, t8.a, AF.Exp), reads=[t8.t], writes=[t8.t])
        fw.op(dve, lambda e: e.tensor_scalar(beta.a, beta.a, 1.0, None, ALU.add), reads=[beta.t], writes=[beta.t])
        fw.op(act, lambda e: e.activation(t8.a, t8.a, AF.Ln, bias=one_c[:64]), reads=[t8.t, Tc], writes=[t8.t])
        fw.op(dve, lambda e: e.reciprocal(beta.a, beta.a), reads=[beta.t], writes=[beta.t])
        fw.op(dve, lambda e: e.tensor_scalar(beta.a, beta.a, mcol, None, ALU.mult), reads=[beta.t, Tp], writes=[beta.t])
        fw.op(dve, lambda e: e.scalar_tensor_tensor(gg.a, t8.a, mcol, nA_r[:64], ALU.mult, ALU.mult), reads=[t8.t, Tp], writes=[gg.t])
        sp_, rs_ = sm_split[pb], sm_res[pb]
        fw.op(dve, lambda e: e.tensor_copy(sp_.a[:, 0, 0:8], gg.a), reads=[gg.t], writes=[sp_.t])
        fw.op(dve, lambda e: e.tensor_copy(sp_.a[:, 0, 8:16], beta.a), reads=[beta.t, sp_.t], writes=[sp_.t])
        fw.op(dve, lambda e: e.tensor_tensor(rs_.a[:, 0:8], gg.a, sp_.a[:, 0, 0:8], ALU.subtract), reads=[gg.t, sp_.t], writes=[rs_.t])
        fw.op(dve, lambda e: e.tensor_tensor(rs_.a[:, 8:16], beta.a, sp_.a[:, 0, 8:16], ALU.subtract), reads=[beta.t, sp_.t, rs_.t], writes=[rs_.t])
        fw.op(dve, lambda e: e.tensor_copy(sp_.a[:, 1, :], rs_.a), reads=[rs_.t, sp_.t], writes=[sp_.t])
        front_done.add(gci)

    def gdn_stage1(t0, gci, full, pb, nxt):
        c64 = slice(t0, t0 + 64)
        beta, gg, t8, egc, edl, cb_, egl, gcs = sm_beta[pb], sm_gg[pb], sm_t8[pb], sm_egc[pb], sm_edl[pb], sm_cb[pb], sm_egl[pb], sm_gc[pb]
        if gci not in front_done:
            s1_front(t0, gci, pb)
        sp_ = sm_split[pb]
        A0 = s1_acc()
        g2 = sp_.a[:, :, 0:8]
        b2 = sp_.a[:, :, 8:16]

        def bc4(m8, v2):
            return (m8[:, None, :, :].broadcast_to([64, 2, 8, 64]), v2[:, :, :, None].broadcast_to([64, 2, 8, 64]))
        fw.op(dve, lambda e: e.tensor_tensor(rhsG.a, *bc4(U8, g2), ALU.mult), reads=[sp_.t, Tc], writes=[rhsG.t])
        fw.op(dve, lambda e: e.tensor_tensor(rhsL.a, *bc4(L8, g2), ALU.mult), reads=[sp_.t, Tc], writes=[rhsL.t])
        fw.op(dve, lambda e: e.tensor_tensor(rhsB.a, *bc4(I8, b2), ALU.mult), reads=[sp_.t, Tc], writes=[rhsB.t])
        yield
        A1 = s1_acc()
        fl = "p h t -> p (h t)"
        fl4 = "p h t -> p (h t)"

        def mm2(out, lhsT, rb, neg, tw):
            for part in range(2):
                fw.op(pe, lambda e: e.matmul(out, lhsT, rb.a[:, part].rearrange(fl4), start=(part == 0), stop=(part == 1 and neg is None)),
                      reads=[rb.t, Tc], writes=[tw])
            if neg is not None:
                fw.op(pe, lambda e: e.matmul(out, ident_b[:64, :64], neg[:].rearrange(fl4), start=False, stop=True), reads=[Tc], writes=[tw])
        fw.op(pe, lambda e: e.matmul(A0.a[:64, 16:24], Umat, gg.a, start=True, stop=True), reads=[gg.t, Tc], writes=[A0.t0])
        fw.op(pe, lambda e: e.matmul(A0.a[:, 24:32], ones_f[:, :], gg.a, start=True, stop=True), reads=[gg.t, Tc], writes=[A0.t0])
        mm2(A1.a[:64, 0:512], Lmat_b, rhsG, NEGU, A1.t0)
        mm2(A1.a[:64, 512:1024], Umat_b, rhsL, NEGL, A1.t1)
        mm2(A0.a[:, 512:1024], ones_b[:64, :], rhsB, None, A0.t1)
        yield
        fw.op(act, lambda e: e.activation(egc.a, A0.a[:64, 16:24], AF.Exp), reads=[A0.t0], writes=[egc.t])
        fw.op(act, lambda e: e.copy(gcs.a, A0.a[:64, 16:24]), reads=[A0.t0], writes=[gcs.t])
        fw.op(act, lambda e: e.activation(egl.a, A0.a[:, 24:32], AF.Exp), reads=[A0.t0], writes=[egl.t])
        fw.op(dve, lambda e: e.tensor_tensor(edl.a, A0.a[:64, 24:32], gcs.a, ALU.subtract), reads=[A0.t0, gcs.t], writes=[edl.t])
        fw.op(act, lambda e: e.activation(edl.a, edl.a, AF.Exp), reads=[edl.t], writes=[edl.t])
        fw.op(dve, lambda e: e.tensor_tensor(cb_.a, beta.a, egc.a, ALU.mult), reads=[beta.t, egc.t], writes=[cb_.t])
        if full:
            mm2(A0.a[:, 0:512], ones_b[:64, :], rhsG, None, A0.t0)
        fw.op(act, lambda e: e.activation(DTs.a, A1.a[:64, 0:512].rearrange(h3, h=8), AF.Exp), reads=[A1.t0], writes=[DTs.t])
        fw.op(dve, lambda e: e.tensor_tensor(kbT.a, knT.a[:, :, c64], A0.a[:, 512:1024].rearrange(h3, h=8), ALU.mult),
              reads=[knT.t, A0.t1], writes=[kbT.t])
        fw.op(act, lambda e: e.activation(Dm.a, A1.a[:64, 512:1024].rearrange(h3, h=8), AF.Exp), reads=[A1.t1], writes=[Dm.t])
        if full:
            fw.op(act, lambda e: e.activation(EG.a, A0.a[:, 0:512].rearrange(h3, h=8), AF.Exp), reads=[A0.t0], writes=[EG.t])
            fw.op(dve, lambda e: e.tensor_tensor(DTi.a, DTs.a, I8[:], ALU.add), reads=[DTs.t, Tc], writes=[DTi.t])
            fw.op(dve, lambda e: e.tensor_tensor(qgT[pb].a, qnT.a[:, :, c64], EG.a, ALU.mult), reads=[qnT.t, EG.t], writes=[qgT[pb].t])
        yield
        B2 = s1_acc()
        b2b = B2.a[:].bitcast(BF16)
        ktok = b2b[:64, 0:1024].rearrange("p (h d) -> p h d", h=8)
        vtok = b2b[:64, 1024:2048].rearrange("p (h d) -> p h d", h=8)
        for h in range(NH):
            fw.op(pe, lambda e: e.transpose(ktok[:, h, :], knT.a[:, h, c64], ident_b[:, :]), reads=[knT.t, Tc], writes=[B2.t0])
        for h in range(NH):
            fw.op(pe, lambda e: e.transpose(vtok[:, h, :], vT.a[:, h, c64], ident_b[:, :]), reads=[vT.t, Tc], writes=[B2.t1])
        fw.op(act, lambda e: e.copy(ktk[pb].a, ktok), reads=[B2.t0], writes=[ktk[pb].t])
        fw.op(act, lambda e: e.copy(vtk[pb].a, vtok), reads=[B2.t1], writes=[vtk[pb].t])
        yield
        B0 = s1_acc()
        for h in range(NH):
            hs = slice(h * 64, (h + 1) * 64)
            fw.op(pe, lambda e: e.matmul(B0.a[:64, hs], knT.a[:, h, c64], kbT.a[:, h, :], start=True, stop=True), reads=[knT.t, kbT.t], writes=[B0.t0])
        for h in range(NH):
            fw.op(pe, lambda e: e.matmul(B0.a[:64, 512 + h * 64:512 + (h + 1) * 64], kbT.a[:, h, :], knT.a[:, h, c64], start=True, stop=True),
                  reads=[knT.t, kbT.t], writes=[B0.t1])
        fw.op(dve, lambda e: e.tensor_tensor(PTm[0].a, B0.a[:64, 0:512].rearrange(h3, h=8), DTs.a, ALU.mult), reads=[B0.t0, DTs.t], writes=[PTm[0].t])
        fw.op(dve, lambda e: e.tensor_tensor(Pm[0].a, B0.a[:64, 512:1024].rearrange(h3, h=8), Dm.a, ALU.mult), reads=[B0.t1, Dm.t], writes=[Pm[0].t])
        fw.op(dve, lambda e: e.tensor_tensor(TT.a, I8[:], PTm[0].a, ALU.subtract), reads=[PTm[0].t, Tc], writes=[TT.t])
        yield
        if full:
            B1 = s1_acc()
            for h in range(NH):
                hs = slice(h * 64, (h + 1) * 64)
                fw.op(pe, lambda e: e.matmul(B1.a[:64, hs], knT.a[:, h, c64], qnT.a[:, h, c64], start=True, stop=True), reads=[knT.t, qnT.t], writes=[B1.t0])
            fw.op(dve, lambda e: e.tensor_tensor(attnT[pb].a, B1.a[:64, 0:512].rearrange(h3, h=8), DTi.a, ALU.mult),
                  reads=[B1.t0, DTi.t], writes=[attnT[pb].t])
        if nxt is not None and full:
            s1_front(nxt[0], nxt[1], 1 - pb)
        cur = 0
        for lvl in range(5):
            nxt = 1 - cur
            C0 = s1_acc()
            for h in range(NH):
                hs = slice(h * 64, (h + 1) * 64)
                fw.op(pe, lambda e: e.matmul(C0.a[:64, hs], PTm[cur].a[:, h, :], Pm[cur].a[:, h, :], start=True, stop=True),
                      reads=[PTm[cur].t, Pm[cur].t], writes=[C0.t0])
            if lvl < 4:
                for h in range(NH):
                    fw.op(pe, lambda e: e.matmul(C0.a[:64, 512 + h * 64:512 + (h + 1) * 64], Pm[cur].a[:, h, :], PTm[cur].a[:, h, :], start=True, stop=True),
                          reads=[PTm[cur].t, Pm[cur].t], writes=[C0.t1])
            C1 = s1_acc()
            for _ in range(WARM):
                fw.op(pe, lambda e: e.matmul(C1.a[:, 512:1024], ones_b[:, :], hT[:, 0, 0:512], start=True, stop=True), reads=[ThT, Tc], writes=[C1.t1])
            fw.op(act, lambda e: e.copy(Pm[nxt].a, C0.a[:64, 0:512].rearrange(h3, h=8)), reads=[C0.t0], writes=[Pm[nxt].t])
            if lvl < 4:
                fw.op(dve, lambda e: e.tensor_copy(PTm[nxt].a, C0.a[:64, 512:1024].rearrange(h3, h=8)), reads=[C0.t1], writes=[PTm[nxt].t])
            yield
            for h in range(NH):
                hs = slice(h * 64, (h + 1) * 64)
                fw.op(pe, lambda e: e.matmul(C1.a[:64, hs], Pm[nxt].a[:, h, :], TT.a[:, h, :], start=True, stop=True),
                      reads=[Pm[nxt].t, TT.t], writes=[C1.t0])
            fw.op(dve, lambda e: e.tensor_tensor(TT.a, TT.a, C1.a[:64, 0:512].rearrange(h3, h=8), ALU.add), reads=[C1.t0, TT.t], writes=[TT.t])
            cur = nxt
        yield
        fw.op(dve, lambda e: e.tensor_tensor(TTc.a, TT.a, bc(cb_.a, 64), ALU.mult), reads=[TT.t, cb_.t], writes=[TTc.t])
        fw.op(dve, lambda e: e.tensor_tensor(TTb[pb].a, TT.a, bc(beta.a, 64), ALU.mult), reads=[TT.t, beta.t], writes=[TTb[pb].t])
        D0 = s1_acc()
        for h in range(NH):
            hs = slice(h * 64, (h + 1) * 64)
            fw.op(pe, lambda e: e.matmul(D0.a[:, hs], ktk[pb].a[:, h, :], TTc.a[:, h, :], start=True, stop=True), reads=[ktk[pb].t, TTc.t], writes=[D0.t0])
        fw.op(act, lambda e: e.mul(negwT[pb].a, D0.a[:, 0:512].rearrange(h3, h=8), -1.0), reads=[D0.t0], writes=[negwT[pb].t])
        yield

    def gdn_stage2(t0, full, pb, st):
        c64 = slice(t0, t0 + 64)
        S, Sbf, TS, TSbf = st.S, st.Sbf, st.TS, st.TSbf
        edl, egl = sm_edl[pb], sm_egl[pb]
        D1 = Acc(2)
        for h in range(NH):
            es = slice(h * 128, (h + 1) * 128)
            tt_ = D1.t0 if h < 4 else D1.t1
            fw.op(pe, lambda e: e.matmul(D1.a[:64, es], TTb[pb].a[:, h, :], vtk[pb].a[:, h, :], start=True, stop=False), reads=[TTb[pb].t, vtk[pb].t], writes=[tt_])
            fw.op(pe, lambda e: e.matmul(D1.a[:64, es], negwT[pb].a[:, h, :], Sbf[:, h, :], start=False, stop=True), reads=[negwT[pb].t, TSbf], writes=[tt_])
        yield
        fw.op(pool, lambda e: e.tensor_tensor(S[:], S[:], bc(egl.a, 128), ALU.mult), reads=[egl.t, TS], writes=[TS])
        d13 = D1.a[:64, :].rearrange("p (h d) -> p h d", h=8)
        if full:
            fw.op(act, lambda e: e.copy(vnew.a, d13), reads=D1.tt, writes=[vnew.t])
        fw.op(dve, lambda e: e.tensor_tensor(vnews.a, d13, bc(edl.a, 128), ALU.mult), reads=D1.tt + [edl.t], writes=[vnews.t])
        yield
        D3 = Acc(2)
        for h in range(NH):
            es = slice(h * 128, (h + 1) * 128)
            fw.op(pe, lambda e: e.matmul(D3.a[:, es], ktk[pb].a[:, h, :], vnews.a[:, h, :], start=True, stop=True), reads=[ktk[pb].t, vnews.t],
                  writes=[D3.t0 if h < 4 else D3.t1])
        if full:
            D2 = Acc(3)
            for h in range(NH):
                hs = slice(h * 64, (h + 1) * 64)
                fw.op(pe, lambda e: e.matmul(D2.a[:, hs], Sbf[:, h, :], qgT[pb].a[:, h, :], start=True, stop=False), reads=[TSbf, qgT[pb].t], writes=[D2.t0])
                fw.op(pe, lambda e: e.matmul(D2.a[:, hs], vnew.a[:, h, :], attnT[pb].a[:, h, :], start=False, stop=True), reads=[vnew.t, attnT[pb].t], writes=[D2.t0])
        yield
        d33 = D3.a[:, :].rearrange("p (h d) -> p h d", h=8)
        fw.op(dve, lambda e: e.tensor_tensor(Sbf[:], S[:], d33, ALU.add), reads=[TS] + D3.tt, writes=[TSbf])
        fw.op(dve, lambda e: e.tensor_tensor(S[:], S[:], d33, ALU.add), reads=[TS] + D3.tt, writes=[TS])
        yield
        if full:
            o3 = D2.a[:, 0:512].rearrange(h3, h=8)
            fw.op(act, lambda e: e.activation(osq.a, o3, AF.Square), reads=[D2.t0], writes=[osq.t])
            E0 = Acc(3)
            fw.op(pe, lambda e: e.matmul(E0.a[:, 512:1024], ones_b[:, :], osq.a.rearrange("p h t -> p (h t)"), start=True, stop=True),
                  reads=[osq.t, Tc], writes=[E0.t1])
            yield
            rsqrt_big(rso.a, E0.a[:, 512:1024].rearrange(h3, h=8), 1.0 / DK, [E0.t1], [rso.t])
            fw.op(dve, lambda e: e.scalar_tensor_tensor(otmp.a, o3, gng, rso.a, ALU.mult, ALU.mult), reads=[D2.t0, Tp, rso.t], writes=[otmp.t])
            fw.op(pool, lambda e: e.tensor_tensor(yT[:, 8:16, c64], otmp.a, szT.a[:, :, c64], ALU.mult), reads=[otmp.t, szT.t], writes=[TyT])
            yield

    def gdn_chunks(chunks, full, tail_gen=None):
        fw.barrier()
        prev = None
        for ci, (t0, gci, sidx) in enumerate(chunks):
            pb = ci % 2
            cast_pump(3)
            nxt = (chunks[ci + 1][0], chunks[ci + 1][1]) if ci + 1 < len(chunks) else None
            extra = None
            if tail_gen is not None and ci == len(chunks) - 1:
                acc_pool[:] = [3]
                extra = tail_gen
            run_interleaved([gdn_stage1(t0, gci, full, pb, nxt), prev, extra])
            if extra is not None:
                acc_pool[:] = [0, 1, 2, 3]
            prev = gdn_stage2(t0, full, pb, ST[sidx])
        run_interleaved([prev])

    def gdn_heads(Nt, full, segs, pre_hook=None):
        cnt = [0]

        def bufs():
            i = cnt[0] % 2
            cnt[0] += 1
            return pre[i], cvb[i], qsb[i], sqbb[i]

        def proj(ub, tub, j):
            A = next_acc()
            mm_chunk(A, ub, tub, j, hT, ThT, Nt, KD)
            return A

        def conv_gen(A, ci, pr, cv):
            yield from conv_segs_gen(cv, pr, 4, wcg, ci, segs, lambda st: (st.hG, st.ThG),
                                     lambda dst, t0, n: fw.op(act, lambda e: e.copy(dst, A.a[:, t0:t0 + n]), reads=A.ts(Nt), writes=[pr.t]))

        def norm_gen(A, ci, pr, cv, qs, sq, out):
            yield from conv_gen(A, ci, pr, cv)
            fw.op(act, lambda e: e.activation(qs.a[:, :Nt], cv.a[:, :Nt], AF.Silu), reads=[cv.t], writes=[qs.t])
            fw.op(act, lambda e: e.activation(sq.a[:, :Nt], qs.a[:, :Nt], AF.Square), reads=[qs.t], writes=[sq.t])
            A2 = next_acc()
            for (o, n) in nsplit(Nt):
                fw.op(pe, lambda e: e.matmul(A2.a[:, o:o + n], ones_b[:, :], sq.a[:, o:o + n], start=True, stop=True),
                      reads=[sq.t, Tc], writes=[A2.t0 if o < 512 else A2.t1])
            out.append(A2)
            yield

        def norm_p2(A2, cv, qs, dst, h, sc):
            rsqrt_big(cv.a[:, :Nt], A2.a[:, :Nt], 1.0, A2.ts(Nt), [cv.t])
            fw.op(dve, lambda e: e.scalar_tensor_tensor(dst.a[:, h, :Nt], qs.a[:, :Nt], sc, cv.a[:, :Nt], ALU.mult, ALU.mult),
                  reads=[qs.t, cv.t], writes=[dst.t])

        seq = []
        for hp in range(4):
            if full:
                for j in range(2):
                    seq.append([("q", hp, j), ("k", hp, j)])
            else:
                seq.append([("k", hp, 0), ("k", hp, 1)])
            for j in range(2):
                seq.append([("v", hp, j)] + ([("z", hp, j)] if full else []))
        ubase = {"q": 12, "k": 16, "v": 20, "z": 24}
        units = {}

        def stage_a(grp):
            res = []
            for (kind, hp, j) in grp:
                if (kind, hp) not in units:
                    units[(kind, hp)] = load_unit(wi_b[:, (ubase[kind] + hp) * 256:(ubase[kind] + hp + 1) * 256], KD, Twi_rest if kind == "q" else Twi_kv)
                ub, tub = units[(kind, hp)]
                res.append(proj(ub, tub, j))
            return res

        nxt = stage_a(seq[0])
        if pre_hook is not None:
            pre_hook()
        for gi, grp in enumerate(seq):
            accs_ = nxt
            if grp[0][0] != "v" and gi + 1 < len(seq):
                pass
            if grp[0][0] in ("q", "k"):
                gens = []
                todo = []
                for (kind, hp, j), A in zip(grp, accs_):
                    h = hp * 2 + j
                    pr, cv, qs, sq = bufs()
                    cidx0, dst, sc = (0, qnT, DK ** -0.5) if kind == "q" else (8, knT, 1.0)
                    out = []
                    gens.append(norm_gen(A, cidx0 + h, pr, cv, qs, sq, out))
                    todo.append((out, cv, qs, dst, h, sc))
                run_interleaved(gens)
                if gi + 1 < len(seq):
                    nxt = stage_a(seq[gi + 1])
                for (out, cv, qs, dst, h, sc) in todo:
                    norm_p2(out[0], cv, qs, dst, h, sc)
            else:
                (kind, hp, j) = grp[0]
                h = hp * 2 + j
                pr, cv, qs, sq = bufs()
                A = accs_[0]
                run_interleaved([conv_gen(A, 16 + h, pr, cv)])
                fw.op(act, lambda e: e.activation(vT.a[:, h, :Nt], cv.a[:, :Nt], AF.Silu), reads=[cv.t], writes=[vT.t])
                if full:
                    Az = accs_[1]
                    fw.op(act, lambda e: e.activation(szT.a[:, h, :Nt], Az.a[:, :Nt], AF.Silu), reads=Az.ts(Nt), writes=[szT.t])
                if gi + 1 < len(seq):
                    nxt = stage_a(seq[gi + 1])

    def group_a(Nt, segs):
        fw.barrier()
        ssA = Acc(3)
        cnt = 0
        for cp in range(4):
            u_h, t_h = load_unit(wi_b[:, cp * 256:(cp + 1) * 256], KD, Twi_rest)
            u_c, t_c = load_unit(wi_b[:, (4 + cp) * 256:(5 + cp) * 256], KD, Twi_rest)
            u_b, t_b = load_unit(wi_b[:, (8 + cp) * 256:(9 + cp) * 256], KD, Twi_rest)
            for j in range(2):
                c = cp * 2 + j
                a_h, a_c, a_b = Acc(0), Acc(1), Acc(2)
                mm_chunk(a_h, u_h, t_h, j, hT, ThT, Nt, KD)
                mm_chunk(a_c, u_c, t_c, j, hT, ThT, Nt, KD)
                mm_chunk(a_b, u_b, t_b, j, hT, ThT, Nt, KD)
                pr = pre[cnt % 2]; cv = cvb[0]; cnt += 1
                fw.op(act, lambda e: e.copy(ahs.a[:, :Nt], a_h.a[:, :Nt]), reads=a_h.ts(Nt), writes=[ahs.t])
                conv_segs(cv, pr, 3, wca, c, segs, lambda st: (st.hA, st.ThA),
                          lambda dst, t0, n: fw.op(dve, lambda e: e.tensor_tensor(dst, a_c.a[:, t0:t0 + n], ahs.a[:, t0:t0 + n], ALU.mult),
                                                   reads=a_c.ts(Nt) + [ahs.t], writes=[pr.t]))
                fw.op(dve, lambda e: e.tensor_tensor(yraw.a[:, c, :Nt], a_b.a[:, :Nt], cv.a[:, :Nt], ALU.mult), reads=a_b.ts(Nt) + [cv.t], writes=[yraw.t])
                fw.op(act, lambda e: e.activation(sqb.a[:, :Nt], yraw.a[:, c, :Nt], AF.Square), reads=[yraw.t], writes=[sqb.t])
                for (o, n) in nsplit(Nt):
                    fw.op(pe, lambda e: e.matmul(ssA.a[:, o:o + n], ones_b[:, :], sqb.a[:, o:o + n], start=(c == 0), stop=(c == 7)),
                          reads=[sqb.t, Tc], writes=[ssA.t0 if o < 512 else ssA.t1])
        acc_ctr[0] = 0

        def tail():
            rsqrt_cols(ahs.a[:, :Nt], ssA.a[:, :Nt], 1.0 / CW, ssA.ts(Nt), [ahs.t])
            for c in range(8):
                fw.op(dve, lambda e: e.scalar_tensor_tensor(yT[:, c, :Nt], yraw.a[:, c, :Nt], gna[:, c:c + 1], ahs.a[:, :Nt], ALU.mult, ALU.mult),
                      reads=[yraw.t, ahs.t, Tp], writes=[TyT])
        return tail

    Tx1 = T("x1s")

    def out_and_ffn(Nt, segs):
        fw.barrier()
        blks = []
        for sg in segs:
            for (o, nb) in blocks_of(sg.n):
                blks.append((sg.t0 + o, nb))
        acc_pool[:] = [0, 1, 2]
        SSP = Acc(3)

        def proj_evac(A, m, gvec):
            sq = sqb2[m % 2]
            fw.op(act, lambda e: e.activation(sq.a[:, :Nt], A.a[:, :Nt], AF.Square), reads=A.ts(Nt), writes=[sq.t])
            fw.op(act, lambda e: e.activation(fT.a[:, m, :Nt], A.a[:, :Nt], AF.Copy, scale=gvec[:, m:m + 1]), reads=A.ts(Nt) + [Tp], writes=[fT.t])
            for bi, (o, nb) in enumerate(blks):
                if m == 0 and bi == 0:
                    fw.op(pe, lambda e: e.matmul(SSP.a[:nb, 512:517], sq.a[:, o:o + nb], e5[:, 0:5], start=True, stop=False),
                          reads=[sq.t, Tc], writes=[SSP.t1])
                else:
                    fw.op(pe, lambda e: e.matmul(SSP.a[:nb, 512 + bi:513 + bi], sq.a[:, o:o + nb], ones_b[:, 0:1], start=False,
                                                 stop=(m == KD - 1 and bi == len(blks) - 1)),
                          reads=[sq.t, Tc], writes=[SSP.t1])

        def token_major_resid(bi, o, nb, res=xin, tres=Txin):
            rsqrt_cols(t_rs.a[:nb], SSP.a[:nb, 512 + bi:513 + bi], 1.0 / D_MODEL, [SSP.t1], [t_rs.t])
            for g in range(4):
                A = next_acc()
                a3 = A.a[:, 0:512].rearrange("p (j t) -> p j t", j=4)
                for j in range(4):
                    m = g * 4 + j
                    fw.op(pe, lambda e: e.transpose(a3[:nb, j, :], fT.a[:, m, o:o + nb], ident_f[:, :]), reads=[fT.t, Tc], writes=[A.t0])
                fw.op(dve, lambda e: e.scalar_tensor_tensor(tok[:nb, g * 512:(g + 1) * 512], A.a[:nb, 0:512], t_rs.a[:nb],
                                                            res[:nb, g * 512:(g + 1) * 512], ALU.mult, ALU.add),
                      reads=[A.t0, t_rs.t, tres], writes=[Ttok])

        for up in range(8):
            ub, tub = load_unit(wo_b[:, up * 256:(up + 1) * 256], KD, Two)
            for j in range(2):
                m = up * 2 + j
                A = next_acc()
                mm_chunk(A, ub, tub, j, yT, TyT, Nt, KD)
                proj_evac(A, m, gqm)
        bsrc = []
        for sg in segs:
            for (o, nb) in blocks_of(sg.n):
                bsrc.append((sg, o))
        for bi, (o, nb) in enumerate(blks):
            sg, so = bsrc[bi]
            fw.dma(sp, d_x, xin[:nb], sg.xsrc[sg.row0 + so:sg.row0 + so + nb, :], writes=[Txin])
            token_major_resid(bi, o, nb)
            fw.dma(pool, d_x1, x1s[o:o + nb, :], tok[:nb], reads=[Ttok], writes=[Tx1])
            norm_block_to_T(tok, Ttok, nb, gpf, hT, ThT, o)
        cnt = 0
        for up in range(KF // 2):
            ug, tug = load_unit(wu_b[:, up * 256:(up + 1) * 256], KD, Twu)
            uv, tuv = load_unit(wu_b[:, D_FF + up * 256:D_FF + (up + 1) * 256], KD, Twu)
            for j in range(2):
                idx = up * 2 + j
                ag = next_acc(); av = next_acc()
                mm_chunk(ag, ug, tug, j, hT, ThT, Nt, KD)
                mm_chunk(av, uv, tuv, j, hT, ThT, Nt, KD)
                pr = pre2[0]; cv = cv2[0]; cnt += 1
                conv_segs(cv, pr, 3, wcf, idx, segs, lambda st: (st.hF, st.ThF),
                          lambda dst, t0, n: fw.op(act, lambda e: e.copy(dst, ag.a[:, t0:t0 + n]), reads=ag.ts(Nt), writes=[pr.t]))
                fw.op(act, lambda e: e.activation(cv.a[:, :Nt], cv.a[:, :Nt], AF.Silu), reads=[cv.t], writes=[cv.t])
                fw.op(dve, lambda e: e.tensor_tensor(actT.a[:, idx, :Nt], av.a[:, :Nt], cv.a[:, :Nt], ALU.mult), reads=av.ts(Nt) + [cv.t], writes=[actT.t])
        kparts = ((0, 16), (16, 16), (32, 12))
        for mp in range(8):
            a_m = [next_acc(), next_acc()]
            for pi, (k0, nk) in enumerate(kparts):
                ub, tub = load_unit(wd_b[k0 * 128:(k0 + nk) * 128, mp * 256:(mp + 1) * 256], nk, Twd)
                for j in range(2):
                    mm_chunk(a_m[j], ub, tub, j, actT.a, actT.t, Nt, nk, koff=k0, first=(pi == 0), last=(pi == 2))
            for j in range(2):
                proj_evac(a_m[j], mp * 2 + j, gqf)
        def final_gen():
            for bi, (o, nb) in enumerate(blks):
                fw.dma(sp, d_x1l, tok[:nb], x1s[o:o + nb, :], reads=[Tx1], writes=[Ttok])
                token_major_resid(bi, o, nb, tok, Ttok)
                sg, so = bsrc[bi]
                lo = max(0, sg.tok_lo - so)
                hi = min(nb, sg.tok_hi - so)
                if hi > lo:
                    r0 = sg.out_row0 + so + lo
                    fw.dma(pool, d_o, sg.out_ap[r0:r0 + hi - lo, :], tok[lo:hi], reads=[Ttok])
                yield
            acc_pool[:] = [0, 1, 2, 3]
        return final_gen()

    hT_prebuilt = [False]

    def tile_prefix(row0, Nt, gci0, next_segs):
        chk(1)
        segs = [Seg(0, Nt, Nt, 0, D["xpre"], row0)]
        if not hT_prebuilt[0]:
            build_hT(segs)
        hT_prebuilt[0] = False
        gdn_heads(Nt, False, segs)
        chk(3)
        gdn_chunks([(ci * 64, gci0 + ci, 0) for ci in range(Nt // 64)], False, build_hT_gen(next_segs))
        hT_prebuilt[0] = True
        chk(5)

    pending_final = [None]

    def tile_full(Nt, segs, chunks):
        chk(6)
        if hT_prebuilt[0]:
            hT_prebuilt[0] = False
        else:
            run_interleaved([pending_final[0], build_hT_gen(segs)])
        pending_final[0] = None
        ga_tail = group_a(Nt, segs)
        chk(7)
        gdn_heads(Nt, True, segs, ga_tail)
        chk(8)
        gdn_chunks(chunks, True)
        chk(11)
        pending_final[0] = out_and_ffn(Nt, segs)
        chk(10)

    stO = tok[:, 1024:1200]; TstO = Ttok
    stI = tok[:88, 1280:1536].rearrange("p (a c) -> p a c", a=2); TstI = Ttok

    def hist_specs(st):
        return ((st.hA, st.ThA, 2, 8, 0), (st.hG, st.ThG, 3, 24, 16), (st.hF, st.ThF, 2, KF, 88))

    def store_states(prefix, st):
        for (hb, th, K1, C, c0) in hist_specs(st):
            fw.op(pool, lambda e: e.tensor_copy(stO[:, c0:c0 + K1 * C].rearrange("p (t c) -> p c t", t=K1), hb[:]), reads=[th, TstO], writes=[TstO])
        A = next_acc()
        fw.op(pe, lambda e: e.transpose(A.a[:88, 0:128], stO[:, 0:88], ident_f[:, :]), reads=[TstO, Tc], writes=[A.t0])
        fw.op(pe, lambda e: e.transpose(A.a[:88, 128:256], stO[:, 88:176], ident_f[:, :]), reads=[TstO, Tc], writes=[A.t0])
        fw.op(dve, lambda e: e.tensor_copy(stI[:].rearrange("p a c -> p (a c)"), A.a[:88, 0:256]), reads=[A.t0, TstI], writes=[TstI])
        fw.dma(pool, d_st, D[prefix + "_nca"].rearrange("t (c p) -> (t c) p", p=128), stI[0:16, 0, :], reads=[TstI])
        fw.dma(pool, d_st, D[prefix + "_ngc"].rearrange("t (c p) -> (t c) p", p=128), stI[16:88, 0, :], reads=[TstI])
        fw.dma(pool, d_st, D[prefix + "_nfc"].rearrange("t (c p) -> (t c) p", p=128), stI[0:88, 1, :], reads=[TstI])
        fw.dma(pool, d_st, D[prefix + "_ngd"].rearrange("h d e -> d h e"), st.S[:], reads=[st.TS])
        fw.barrier()

    def load_states(st):
        d_ld = fw.dsem("ld")
        fw.dma(sp, d_ld, stI[0:16, 0, :], D["s_conv_a"].rearrange("t (c p) -> (t c) p", p=128), reads=[TstI], writes=[TstI])
        fw.dma(sp, d_ld, stI[16:88, 0, :], D["s_gdn_conv"].rearrange("t (c p) -> (t c) p", p=128), reads=[TstI], writes=[TstI])
        fw.dma(sp, d_ld, stI[0:88, 1, :], D["s_ffn_conv"].rearrange("t (c p) -> (t c) p", p=128), reads=[TstI], writes=[TstI])
        fw.dma(sp, fw.dsem("ldS"), st.S[:], D["s_gdn"].rearrange("h d e -> d h e"), reads=[st.TS], writes=[st.TS])
        A = next_acc()
        fw.op(pe, lambda e: e.transpose(A.a[:, 0:88], stI[:, 0, :], ident_f[:88, :88]), reads=[TstI, Tc], writes=[A.t0])
        fw.op(pe, lambda e: e.transpose(A.a[:, 88:176], stI[:, 1, :], ident_f[:88, :88]), reads=[TstI, Tc], writes=[A.t0])
        for (hb, th, K1, C, c0) in hist_specs(st):
            fw.op(dve, lambda e: e.tensor_copy(hb[:], A.a[:, c0:c0 + K1 * C].rearrange("p (t c) -> p c t", t=K1)), reads=[A.t0, th], writes=[th])
        fw.op(act, lambda e: e.copy(st.Sbf[:], st.S[:]), reads=[st.TS], writes=[st.TSbf])

    try:
        fw.dma(sp, fw.dsem("wl"), wl[:], wi_b[:, 7168:7184].rearrange("(k p) c -> p k c", p=128), reads=[Twi_kv], writes=[Twl])
        load_states(ST[1])
        p0 = ST[0]
        fw.op(pool, lambda e: e.memset(p0.hA[:], 0.0), writes=[p0.ThA])
        fw.op(pool, lambda e: e.memset(p0.hG[:], 0.0), writes=[p0.ThG])
        fw.op(pool, lambda e: e.memset(p0.hF[:], 0.0), writes=[p0.ThF])
        fw.op(pool, lambda e: e.memset(p0.S[:], 0.0), writes=[p0.TS])
        fw.op(pool, lambda e: e.memset(p0.Sbf[:], 0.0), writes=[p0.TSbf])
        gci = 0
        npre_t = NPRE // 512
        main0_segs = [Seg(0, 576, 576, 0, D["xmain"], 0, D["y"], -64, 64, 576)]
        for i in range(npre_t):
            nsegs = [Seg(0, 512, 512, 0, D["xpre"], (i + 1) * 512)] if i + 1 < npre_t else main0_segs
            tile_prefix(i * 512, 512, gci, nsegs)
            gci += 8
        cast_pump(10000)
        row = 0
        for ti, n_main in enumerate((576, 512, 512, 512)):
            segs = [Seg(0, n_main, n_main, 0, D["xmain"], row, D["y"], row - 64, max(0, 64 - row), n_main)]
            chunks = [(ci * 64, gci + ci, 0) for ci in range(n_main // 64)]
            Nt = n_main
            if ti == 3:
                segs.append(Seg(n_main, NSAMP, 16, 1, D["xsamp"], 0, D["ys"], 0, 0, 16))
                chunks.append((n_main, NCHUNK_ALL - 1, 1))
                Nt = n_main + NSAMP
            tile_full(Nt, segs, chunks)
            gci += n_main // 64
            row += n_main
        run_interleaved([pending_final[0]])
        fw.barrier()
        store_states("p", ST[0])
        store_states("s", ST[1])
    except _Stop:
        pass
    fw.finish()
    return nc, fw


_CACHE = {}


def kernel(x_prompt, x_sample, state_conv_a, state_gdn_conv, state_gdn, state_ffn_conv, meta_tokens,
           g_pre_mix, w_in, w_conv_a, g_norm_a, w_conv_gdn, a_log, dt_bias, g_norm_gdn, w_out, g_post_mix,
           g_pre_ffn, w_up, w_conv_ffn, w_down, g_post_ffn):
    f = lambda a: np.ascontiguousarray(np.asarray(a, dtype=np.float32))
    x_prompt, x_sample, meta = f(x_prompt), f(x_sample), f(meta_tokens)
    if "nc" not in _CACHE:
        _CACHE["nc"] = build_program()[0]
    nc = _CACHE["nc"]
    shared = {"g_pre_mix": f(g_pre_mix), "w_in": f(w_in[0]), "w_conv_a": f(w_conv_a[0]), "g_norm_a": f(g_norm_a),
              "w_conv_gdn": f(w_conv_gdn[0]), "a_log": f(a_log), "dt_bias": f(dt_bias), "g_norm_gdn": f(g_norm_gdn),
              "w_out": f(w_out[0]), "g_post_mix": f(g_post_mix), "g_pre_ffn": f(g_pre_ffn), "w_up": f(w_up[0]),
              "w_conv_ffn": f(w_conv_ffn[0]), "w_down": f(w_down[0]), "g_post_ffn": f(g_post_ffn)}
    in_maps = []
    zeros48 = np.zeros((48, D_MODEL), np.float32)
    for c in range(8):
        b, half = c // 2, c % 2
        full = np.concatenate([zeros48, meta, x_prompt[b]], axis=0)
        tmask = np.ones((NCHUNK_ALL, 64), np.float32)
        if half == 0:
            xpre = np.zeros((NPRE, D_MODEL), np.float32)
            xmain = full[0:NMAIN]
            tmask[32, :48] = 0.0
        else:
            xpre = full[0:NPRE]
            xmain = full[NPRE:NPRE + NMAIN]
            tmask[0, :48] = 0.0
        xsamp = np.concatenate([x_sample[c], np.zeros((48, D_MODEL), np.float32)], axis=0)
        tmask[65, 16:] = 0.0
        m = dict(shared)
        m.update({"xpre": np.ascontiguousarray(xpre), "xmain": np.ascontiguousarray(xmain), "xsamp": xsamp, "tmask": tmask,
                  "s_conv_a": f(state_conv_a[0, c]), "s_gdn_conv": f(state_gdn_conv[0, c]), "s_gdn": f(state_gdn[0, c]),
                  "s_ffn_conv": f(state_ffn_conv[0, c])})
        in_maps.append(m)
    res = run_bass_kernel_spmd(nc, in_maps, core_ids=list(range(8)))
    R = res.results
    y_prompt = np.stack([np.concatenate([R[2 * b]["y"], R[2 * b + 1]["y"]], axis=0) for b in range(4)])
    y_sample = np.stack([R[c]["ys"] for c in range(8)])

    def st(key, cores):
        return np.stack([R[c][key] for c in cores])[None]
    pc = [1, 3, 5, 7]
    sc = list(range(8))
    return (y_prompt, y_sample,
            st("p_nca", pc), st("p_ngc", pc), st("p_ngd", pc), st("p_nfc", pc),
            st("s_nca", sc), st("s_ngc", sc), st("s_ngd", sc), st("s_nfc", sc))
```

```python
import numpy as np
import concourse.bass as bass
import concourse.mybir as mybir
from concourse.bass_utils import run_bass_kernel_spmd

F32 = mybir.dt.float32
BF16 = mybir.dt.bfloat16
ALU = mybir.AluOpType
AF = mybir.ActivationFunctionType

D_MODEL = 2048
SEQ = 4096
N_META = 16
CW = 1024
NH = 8
DK = 128
QKV = 3072
D_FF = 5632
IN_COLS = 7184
EPS = 1e-6
NPRE = 2048
NMAIN = 2112
NSAMP = 64
NCHUNK_ALL = (NPRE + NMAIN + NSAMP) // 64
NTMAX = 576
KD = D_MODEL // 128
KF = D_FF // 128


class Eng:
    def __init__(self, name, h, sem, is_pe=False):
        self.name, self.h, self.sem = name, h, sem
        self.cnt = 0
        self.seen = {}
        self.is_pe = is_pe


class DSem:
    def __init__(self, sem, name):
        self.sem, self.name, self.cnt = sem, name, 0


class T:
    def __init__(self, name="", psum=False):
        self.name = name
        self.w = None
        self.r = []
        self.psum = psum


class FW:
    def __init__(self, nc):
        self.nc = nc
        self.stack = []
        self.pe = self._eng("pe", nc.tensor, True)
        self.dve = self._eng("dve", nc.vector)
        self.act = self._eng("act", nc.scalar)
        self.pool = self._eng("pool", nc.gpsimd)
        self.sp = self._eng("sp", nc.sync)
        self.engs = [self.pe, self.dve, self.act, self.pool, self.sp]
        self.dsems = []
        self.ninst = 0

    def sem(self, name):
        g = self.nc.semaphore(name)
        s = g.__enter__()
        self.stack.append(g)
        return s

    def _eng(self, name, h, is_pe=False):
        return Eng(name, h, self.sem("s_" + name), is_pe)

    def dsem(self, name):
        d = DSem(self.sem("d_" + name), name)
        self.dsems.append(d)
        return d

    def _wait(self, eng, e, c):
        if eng.seen.get(e, 0) < c:
            eng.h.wait_ge(e.sem, c)
            eng.seen[e] = c
            self.ninst += 1

    def _deps(self, eng, reads, writes):
        deps = {}

        def add(p):
            if p is None:
                return
            e, c = p
            if eng.is_pe and e is eng:
                return
            if deps.get(e, 0) < c:
                deps[e] = c
        for t in reads:
            add(t.w)
            if t.psum:
                for r in t.r:
                    if r[0] is not eng:
                        add(r)
        for t in writes:
            add(t.w)
            for r in t.r:
                add(r)
        for e, c in deps.items():
            self._wait(eng, e, c)

    def _mark(self, me, reads, writes):
        for t in reads:
            t.r.append(me)
            if len(t.r) > 24:
                best = {}
                for (e, c) in t.r:
                    if best.get(e, 0) < c:
                        best[e] = c
                t.r = list(best.items())
        for t in writes:
            t.w = me
            t.r = []

    def op(self, eng, emit, reads=(), writes=()):
        self._deps(eng, reads, writes)
        ins = emit(eng.h)
        eng.cnt += 1
        ins.then_inc(eng.sem, 1)
        self.ninst += 1
        self._mark((eng, eng.cnt), reads, writes)
        return ins

    def dma(self, q, dsem, out, in_, reads=(), writes=(), waw=True, **kw):
        self._deps(q, reads, writes if waw else ())
        ins = q.h.dma_start(out=out, in_=in_, **kw)
        dsem.cnt += 16
        ins.then_inc(dsem.sem, 16)
        self.ninst += 1
        self._mark((dsem, dsem.cnt), reads, writes)

    def barrier(self):
        for a in self.engs:
            for b in self.engs:
                if a is not b and b.cnt:
                    self._wait(a, b, b.cnt)
            for d in self.dsems:
                if d.cnt:
                    self._wait(a, d, d.cnt)

    def finish(self):
        self.barrier()


def nsplit(n):
    out = []
    o = 0
    while o < n:
        m = min(512, n - o)
        out.append((o, m))
        o += m
    return out


def blocks_of(n):
    out = []
    o = 0
    while o < n:
        m = min(128, n - o)
        out.append((o, m))
        o += m
    return out


class _Stop(Exception):
    pass


class Buf:
    def __init__(self, a, name="", psum=False):
        self.a = a
        self.t = T(name, psum)


def run_interleaved(gens):
    gens = [g for g in gens if g is not None]
    while gens:
        for g in list(gens):
            try:
                next(g)
            except StopIteration:
                gens.remove(g)


def build_program(stage=99):
    def chk(n):
        if stage == n:
            raise _Stop()

    nc = bass.Bass("TRN2", target_bir_lowering=False)
    D = {}

    def din(name, shape):
        D[name] = nc.dram_tensor(name, list(shape), F32, kind="ExternalInput").ap()

    def dout(name, shape):
        D[name] = nc.dram_tensor(name, list(shape), F32, kind="ExternalOutput").ap()

    din("xpre", (NPRE, D_MODEL)); din("xmain", (NMAIN, D_MODEL)); din("xsamp", (NSAMP, D_MODEL))
    din("tmask", (NCHUNK_ALL, 64))
    din("s_conv_a", (2, CW)); din("s_gdn_conv", (3, QKV)); din("s_gdn", (NH, DK, DK)); din("s_ffn_conv", (2, D_FF))
    din("g_pre_mix", (1, D_MODEL)); din("w_in", (D_MODEL, IN_COLS)); din("w_conv_a", (3, CW))
    din("g_norm_a", (1, CW)); din("w_conv_gdn", (4, QKV)); din("a_log", (1, NH)); din("dt_bias", (1, NH))
    din("g_norm_gdn", (1, DK)); din("w_out", (D_MODEL, D_MODEL)); din("g_post_mix", (1, D_MODEL))
    din("g_pre_ffn", (1, D_MODEL)); din("w_up", (D_MODEL, 2 * D_FF)); din("w_conv_ffn", (3, D_FF))
    din("w_down", (D_FF, D_MODEL)); din("g_post_ffn", (1, D_MODEL))
    dout("y", (NMAIN - 64, D_MODEL)); dout("ys", (16, D_MODEL))
    for p in ("p", "s"):
        dout(p + "_nca", (2, CW)); dout(p + "_ngc", (3, QKV)); dout(p + "_ngd", (NH, DK, DK)); dout(p + "_nfc", (2, D_FF))
    wi_b = nc.dram_tensor("wi_b", [D_MODEL, IN_COLS], BF16, kind="Internal").ap()
    wo_b = nc.dram_tensor("wo_b", [D_MODEL, D_MODEL], BF16, kind="Internal").ap()
    wu_b = nc.dram_tensor("wu_b", [D_MODEL, 2 * D_FF], BF16, kind="Internal").ap()
    wd_b = nc.dram_tensor("wd_b", [D_FF, D_MODEL], BF16, kind="Internal").ap()
    x1s = nc.dram_tensor("x1s", [640, D_MODEL], F32, kind="Internal").ap()

    fw = FW(nc)
    pe, dve, act, pool, sp = fw.pe, fw.dve, fw.act, fw.pool, fw.sp

    def sb(name, shape, dt=F32):
        return nc.alloc_sbuf_tensor(name, list(shape), dt)

    ACC = [nc.alloc_psum_tensor(f"acc{i}", [128, 1024], F32) for i in range(4)]
    TB = [T(f"bank{i}", psum=True) for i in range(8)]
    acc_ctr = [0]

    class Acc:
        def __init__(self, i):
            self.i = i
            self.a = ACC[i]
            self.t0 = TB[2 * i]
            self.t1 = TB[2 * i + 1]
            self.tt = [self.t0, self.t1]

        def ts(self, width):
            return [self.t0] if width <= 512 else self.tt

    acc_pool = [0, 1, 2, 3]

    def next_acc():
        i = acc_pool[acc_ctr[0] % len(acc_pool)]
        acc_ctr[0] += 1
        return Acc(i)

    ident_f = sb("ident_f", [128, 128]); ident_b = sb("ident_b", [128, 128], BF16)
    ones_b = sb("ones_b", [128, 128], BF16); ones_f = sb("ones_f", [64, 128])
    U8 = sb("U8", [64, 8, 64]); SU8 = sb("SU8", [64, 8, 64]); L8 = sb("L8", [64, 8, 64]); I8 = sb("I8", [64, 8, 64])
    eps_c = sb("eps_c", [128, 1]); one_c = sb("one_c", [128, 1])
    Tc = T("consts")
    fw.op(pool, lambda e: e.memset(ident_f[:], 0.0), writes=[Tc])
    fw.op(pool, lambda e: e.affine_select(out=ident_f[:], in_=ident_f[:], pattern=[[-1, 128]], compare_op=ALU.not_equal,
                                          fill=1.0, base=0, channel_multiplier=1), reads=[Tc], writes=[Tc])
    fw.op(pool, lambda e: e.tensor_copy(ident_b[:], ident_f[:]), reads=[Tc], writes=[Tc])
    fw.op(pool, lambda e: e.memset(ones_b[:], 1.0), writes=[Tc])
    fw.op(pool, lambda e: e.memset(ones_f[:], 1.0), writes=[Tc])
    fw.op(pool, lambda e: e.memset(eps_c[:], EPS), writes=[Tc])
    fw.op(pool, lambda e: e.memset(one_c[:], 1.0), writes=[Tc])
    mhalf_c = sb("mhalf_c", [128, 1])
    fw.op(pool, lambda e: e.memset(mhalf_c[:], -0.5), writes=[Tc])
    e5 = sb("e5", [128, 8], BF16)
    fw.op(pool, lambda e: e.memset(e5[:], 0.0), writes=[Tc])
    fw.op(pool, lambda e: e.memset(e5[:, 0:1], 1.0), reads=[Tc], writes=[Tc])
    for (m, pat, cm, cmp_) in ((U8, [[0, 8], [1, 64]], -1, ALU.is_ge), (SU8, [[0, 8], [1, 64]], -1, ALU.is_gt),
                               (L8, [[0, 8], [-1, 64]], 1, ALU.is_gt), (I8, [[0, 8], [-1, 64]], 1, ALU.is_equal)):
        fw.op(pool, lambda e: e.memset(m[:], 1.0), reads=[Tc], writes=[Tc])
        fw.op(pool, lambda e: e.affine_select(out=m[:], in_=m[:], pattern=pat, compare_op=cmp_, fill=0.0, base=0,
                                              channel_multiplier=cm), reads=[Tc], writes=[Tc])
    NEGU = sb("NEGU", [64, 8, 64], BF16); NEGL = sb("NEGL", [64, 8, 64], BF16)
    fw.op(dve, lambda e: e.tensor_scalar(NEGU[:], SU8[:], 30000.0, -30000.0, ALU.mult, ALU.add), reads=[Tc], writes=[Tc])
    fw.op(dve, lambda e: e.tensor_scalar(NEGL[:], L8[:], 30000.0, -30000.0, ALU.mult, ALU.add), reads=[Tc], writes=[Tc])
    Umat = U8[:, 0, :]
    Lmat = L8[:, 0, :]
    ULb = sb("ULb", [64, 2, 64], BF16)
    fw.op(pool, lambda e: e.tensor_copy(ULb[:, 0, :], Umat), reads=[Tc], writes=[Tc])
    fw.op(pool, lambda e: e.tensor_copy(ULb[:, 1, :], Lmat), reads=[Tc], writes=[Tc])
    Umat_b = ULb[:, 0, :]
    Lmat_b = ULb[:, 1, :]

    Twi_kv, Twi_rest, Two, Twu, Twd = T("wi_kv"), T("wi_rest"), T("wo"), T("wu"), T("wd")

    cast_sems = {}
    cast_q = []

    def cast(dst, src, tr, rows, c0, c1, rstep=256, defer=False):
        if tr.name not in cast_sems:
            cast_sems[tr.name] = fw.dsem("cast_" + tr.name)
        for r0 in range(0, rows, rstep):
            item = (dst[r0:r0 + rstep, c0:c1], src[r0:r0 + rstep, c0:c1], tr)
            if defer:
                cast_q.append(item)
            else:
                cast_emit(item)

    def cast_emit(item):
        d_, s_, tr = item
        fw.dma(pool, cast_sems[tr.name], d_, s_, writes=[tr], waw=False)

    def cast_pump(n):
        for _ in range(n):
            if cast_q:
                cast_emit(cast_q.pop(0))

    d_par = fw.dsem("par")
    Tp = T("params")
    xin = sb("xin", [128, D_MODEL]); Txin = T("xin")
    tok = sb("tok", [128, D_MODEL]); Ttok = T("tok")
    pst = xin[:, 0:512].rearrange("p (q c) -> p q c", q=4); Tpst = Txin
    stT = xin[:, 512:576]
    par = sb("par", [128, 4, 128])
    tm = sb("tm", [64, NCHUNK_ALL])
    alog_r = sb("alog_r", [128, 8]); dtb_r = sb("dtb_r", [128, 8]); nA_r = sb("nA_r", [128, 8])
    fw.op(pool, lambda e: e.memset(xin[:, 0:576], 0.0), writes=[Tpst])

    d_pst = fw.dsem("pst")

    def pl(dst, src):
        fw.dma(sp, d_pst, dst, src, reads=[Tpst], writes=[Tpst])
    r128 = lambda ap1d: ap1d.rearrange("(k p) -> k p", p=128)
    pl(pst[0:16, 0, :], r128(D["g_pre_mix"][0]))
    pl(pst[16:32, 0, :], r128(D["g_pre_ffn"][0]))
    pl(pst[32:40, 0, :], r128(D["g_norm_a"][0]))
    pl(pst[40:41, 0, :], r128(D["g_norm_gdn"][0]))
    pl(pst[41:65, 0, :], D["w_conv_a"].rearrange("j (c p) -> (j c) p", p=128))
    pl(pst[65:81, 0, :], r128(D["g_post_mix"][0]))
    pl(pst[81:97, 0, :], r128(D["g_post_ffn"][0]))
    pl(pst[0:96, 1, :], D["w_conv_gdn"].rearrange("j (c p) -> (j c) p", p=128))
    wcf_rows = D["w_conv_ffn"].rearrange("j (c p) -> (j c) p", p=128)
    pl(pst[0:128, 2, :], wcf_rows[0:128, :])
    pl(pst[0:4, 3, :], wcf_rows[128:132, :])
    pl(stT[0:NCHUNK_ALL, :], D["tmask"])
    fw.dma(sp, d_par, alog_r[:], D["a_log"][0].partition_broadcast(128), writes=[Tp])
    fw.dma(sp, d_par, dtb_r[:], D["dt_bias"][0].partition_broadcast(128), writes=[Tp])
    cast(wi_b, D["w_in"], Twi_kv, D_MODEL, 4096, IN_COLS)
    a_ = next_acc()
    for q in range(4):
        fw.op(pe, lambda e: e.transpose(a_.a[:, q * 128:(q + 1) * 128], pst[:, q, :], ident_f[:, :]), reads=[Tpst, Tc], writes=[a_.t0])
    fw.op(dve, lambda e: e.tensor_copy(par[:].rearrange("p q c -> p (q c)"), a_.a[:, 0:512]), reads=[a_.t0], writes=[Tp])
    fw.op(pe, lambda e: e.transpose(a_.a[:64, 512:512 + NCHUNK_ALL], stT[:NCHUNK_ALL, :], ident_f[:NCHUNK_ALL, :NCHUNK_ALL]),
          reads=[Tpst, Tc], writes=[a_.t1])
    fw.op(dve, lambda e: e.tensor_copy(tm[:], a_.a[:64, 512:512 + NCHUNK_ALL]), reads=[a_.t1], writes=[Tp])
    gpm = par[:, 0, 0:16]; gpf = par[:, 0, 16:32]; gna = par[:, 0, 32:40]; gng = par[:, 0, 40:41]
    gqm = par[:, 0, 65:81]; gqf = par[:, 0, 81:97]

    def wca(c, j):
        return par[:, 0, 41 + j * 8 + c:42 + j * 8 + c]

    def wcg(c, j):
        return par[:, 1, j * 24 + c:j * 24 + c + 1]

    def wcf(c, j):
        r = j * KF + c
        return par[:, 2, r:r + 1] if r < 128 else par[:, 3, r - 128:r - 127]
    fw.op(act, lambda e: e.activation(nA_r[:], alog_r[:], AF.Exp), reads=[Tp], writes=[Tp])
    fw.op(dve, lambda e: e.tensor_scalar(nA_r[:], nA_r[:], -1.0, None, ALU.mult), reads=[Tp], writes=[Tp])

    cast(wi_b, D["w_in"], Twi_rest, D_MODEL, 0, 4096, rstep=128, defer=True)
    cast(wo_b, D["w_out"], Two, D_MODEL, 0, D_MODEL, rstep=256, defer=True)
    cast(wu_b, D["w_up"], Twu, D_MODEL, 0, 2 * D_FF, rstep=64, defer=True)
    cast(wd_b, D["w_down"], Twd, D_FF, 0, D_MODEL, rstep=256, defer=True)

    xsb = sb("xsb", [128, D_MODEL], BF16); Txsb = T("xsb")
    junk = xsb; Tjunk = Txsb
    hT = sb("hT", [128, KD, NTMAX], BF16); ThT = T("hT")
    yT = sb("yT", [128, KD, NTMAX], BF16); TyT = T("yT")
    class St:
        pass

    def mk_state(sfx):
        st = St()
        st.S = sb("S" + sfx, [128, NH, DK]); st.TS = T("S" + sfx)
        st.Sbf = sb("Sbf" + sfx, [128, NH, DK], BF16); st.TSbf = T("Sbf" + sfx)
        st.hA = sb("histA" + sfx, [128, 8, 2]); st.ThA = T("hA" + sfx)
        st.hG = sb("histG" + sfx, [128, 24, 3]); st.ThG = T("hG" + sfx)
        st.hF = sb("histF" + sfx, [128, KF, 2]); st.ThF = T("hF" + sfx)
        return st
    ST = [mk_state(""), mk_state("_s")]
    dSs = tok[:, 0:1024].rearrange("p (h d) -> p h d", h=NH); TdSs = Ttok

    class Seg:
        def __init__(self, t0, n, nreal, sidx, xsrc, row0, out_ap=None, out_row0=0, tok_lo=0, tok_hi=0):
            self.t0, self.n, self.nreal, self.sidx, self.xsrc, self.row0 = t0, n, nreal, sidx, xsrc, row0
            self.out_ap, self.out_row0, self.tok_lo, self.tok_hi = out_ap, out_row0, tok_lo, tok_hi
    wl = sb("wl", [128, KD, 16], BF16); Twl = T("wl")
    smt = sb("smt", [128, 160])

    def small(c0, w, name, parts=128):
        return Buf(smt[:parts, c0:c0 + w], name)
    n_ss = small(0, 1, "n_ss"); n_rs = small(1, 1, "n_rs"); t_ssc = small(2, 4, "t_ssc"); t_st = small(6, 1, "t_st"); t_rs = small(7, 1, "t_rs")
    sm_beta = [small(8 + 64 * i, 8, f"beta{i}", 64) for i in range(2)]
    sm_gg = [small(16 + 64 * i, 8, f"gg{i}", 64) for i in range(2)]
    sm_t8 = [small(24 + 64 * i, 8, f"t8{i}", 64) for i in range(2)]
    sm_egc = [small(32 + 64 * i, 8, f"egc{i}", 64) for i in range(2)]
    sm_edl = [small(40 + 64 * i, 8, f"edl{i}", 64) for i in range(2)]
    sm_cb = [small(48 + 64 * i, 8, f"cb{i}", 64) for i in range(2)]
    sm_egl = [small(56 + 64 * i, 8, f"egl{i}", 128) for i in range(2)]
    sm_gc = [small(64 + 64 * i, 8, f"gc{i}", 64) for i in range(2)]
    smb = sb("smb", [64, 2, 2, 16], BF16)
    sm_r = sb("sm_r", [64, 2, 16])
    sm_split = [Buf(smb[:, i], f"split{i}") for i in range(2)]
    sm_res = [Buf(sm_r[:, i], f"res{i}") for i in range(2)]
    NU = 3
    wun = [sb(f"wu{i}", [128, 16, 256], BF16) for i in range(NU)]
    Twun = [T(f"wun{i}") for i in range(NU)]
    d_wun = [fw.dsem(f"wun{i}") for i in range(NU)]
    d_x = fw.dsem("x"); d_o = fw.dsem("o"); d_st = fw.dsem("st"); d_x1 = fw.dsem("x1"); d_x1l = fw.dsem("x1l")
    unit_ctr = [0]

    UN = 25200
    UNI = sb("UNI", [128, UN])
    upos = [0]

    def carve(nelem_f32):
        a = upos[0]
        upos[0] += nelem_f32
        assert upos[0] <= UN, upos[0]
        return UNI[:, a:a + nelem_f32]

    h3 = "p (h t) -> p h t"

    def cf(n, name, parts=128, h=None):
        a = carve(n)[:parts]
        if h:
            a = a.rearrange(h3, h=h)
        return Buf(a, name)

    def cb16(n_bf, name, parts=128, h=None):
        a = carve(n_bf // 2).bitcast(BF16)[:parts]
        if h:
            a = a.rearrange(h3, h=h)
        return Buf(a, name)

    qnT = cb16(NH * NTMAX, "qnT", h=NH); knT = cb16(NH * NTMAX, "knT", h=NH)
    vT = cb16(NH * NTMAX, "vT", h=NH); szT = cb16(NH * NTMAX, "szT", h=NH)
    pre = [cf(NTMAX + 8, f"pre{i}") for i in range(2)]
    cvb = [cf(NTMAX, f"cvb{i}") for i in range(2)]
    qsb = [cf(NTMAX, f"qs{i}") for i in range(2)]; ahs = cf(NTMAX, "ahs")
    sqbb = [cb16(NTMAX, f"sqb{i}") for i in range(2)]
    sqb = sqbb[0]
    alias0 = upos[0]
    yraw = cf(8 * NTMAX, "yraw", h=8)
    upos[0] = alias0
    def cf2(name):
        a = carve(512).bitcast(BF16)[:64].rearrange("p (s h t) -> p s h t", s=2, h=8)
        return Buf(a, name)
    rhsG = cf2("rhsG"); rhsL = cf2("rhsL"); rhsB = cf2("rhsB")
    EG = cb16(512, "EG", 128, 8)
    DTi = cb16(512, "DTi", 64, 8); DTs = cb16(512, "DTs", 64, 8); Dm = cb16(512, "Dm", 64, 8)
    otmp = cf(512, "otmp", 128, 8); rso = cf(512, "rso", 128, 8)
    kbT = cb16(512, "kbT", 128, 8)
    Pm = [cb16(512, f"P{i}", 64, 8) for i in range(2)]
    PTm = [cb16(512, f"PT{i}", 64, 8) for i in range(2)]
    TT = cb16(512, "TT", 64, 8); TTc = cb16(512, "TTc", 64, 8)
    vnew = cb16(1024, "vnew", 64, 8); vnews = cb16(1024, "vnews", 64, 8)
    osq = cb16(512, "osq", 128, 8)
    qgT = [cb16(512, f"qgT{i}", 128, 8) for i in range(2)]
    ktk = [cb16(1024, f"ktk{i}", 64, 8) for i in range(2)]
    vtk = [cb16(1024, f"vtk{i}", 64, 8) for i in range(2)]
    TTb = [cb16(512, f"TTb{i}", 64, 8) for i in range(2)]
    attnT = [cb16(512, f"attnT{i}", 64, 8) for i in range(2)]
    negwT = [cb16(512, f"negwT{i}", 128, 8) for i in range(2)]
    mix_end = upos[0]
    upos[0] = 0
    fT = Buf(carve(KD * NTMAX).rearrange("p (m t) -> p m t", m=KD), "fT")
    actT = Buf(carve(KF * NTMAX // 2).bitcast(BF16).rearrange("p (k t) -> p k t", k=KF), "actT")
    pre2 = [cf(NTMAX + 8, f"pre2{i}") for i in range(1)]
    cv2 = [cf(NTMAX, f"cv2{i}") for i in range(1)]
    sqb2 = [cb16(NTMAX, f"sqb2{i}") for i in range(2)]
    assert max(mix_end, upos[0]) <= UN, (mix_end, upos[0])

    def load_unit(src_ap, nk, tr_src):
        i = unit_ctr[0] % NU
        unit_ctr[0] += 1
        fw.dma(sp, d_wun[i], wun[i][:, 0:nk, :], src_ap.rearrange("(k p) c -> p k c", p=128), reads=[tr_src], writes=[Twun[i]])
        return wun[i], Twun[i]

    def mm_chunk(A, ub, tub, j, src, tsrc, Nt, nk, koff=0, first=True, last=True):
        for (o, n) in nsplit(Nt):
            for k in range(nk):
                fw.op(pe, lambda e: e.matmul(A.a[:, o:o + n], ub[:, k, j * 128:(j + 1) * 128], src[:, koff + k, o:o + n],
                                             start=(first and k == 0), stop=(last and k == nk - 1)),
                      reads=[tub, tsrc], writes=[A.t0 if o < 512 else A.t1])

    def rsqrt_cols(dst, src, scale, rd, wr):
        P = dst.shape[0]
        if len(dst.shape) == 2 and dst.shape[1] == 1:
            fw.op(act, lambda e: e.activation(dst, src, AF.Identity, bias=eps_c[:P], scale=scale), reads=rd + [Tc], writes=wr)
            fw.op(pool, lambda e: e.tensor_tensor(dst, dst, mhalf_c[:P], ALU.pow), reads=wr + [Tc], writes=wr)
            return
        fw.op(act, lambda e: e.activation(dst, src, AF.Sqrt, bias=eps_c[:P], scale=scale), reads=rd + [Tc], writes=wr)
        fw.op(dve, lambda e: e.reciprocal(dst, dst), reads=wr, writes=wr)

    def rsqrt_big(dst, src, scale, rd, wr):
        P = dst.shape[0]
        fw.op(act, lambda e: e.activation(dst, src, AF.Ln, bias=eps_c[:P], scale=scale), reads=rd + [Tc], writes=wr)
        fw.op(act, lambda e: e.activation(dst, dst, AF.Exp, scale=-0.5), reads=wr, writes=wr)

    def norm_block_to_T(src_tile, tsrc, nb, gvec, dstT, tdst, t0):
        ss = n_ss.a[:nb]
        rs = n_rs.a[:nb]
        fw.op(act, lambda e: e.activation(junk[:nb], src_tile[:nb], AF.Square, accum_out=ss), reads=[tsrc], writes=[Tjunk, n_ss.t])
        rsqrt_cols(rs, ss, 1.0 / D_MODEL, [n_ss.t], [n_rs.t])
        fw.op(act, lambda e: e.activation(xsb[:nb], src_tile[:nb], AF.Copy, scale=rs), reads=[tsrc, n_rs.t], writes=[Txsb])
        for g in range(4):
            A = next_acc()
            accb = A.a[:].bitcast(BF16)[:, 0:512].rearrange("p (j t) -> p j t", j=4)
            for j in range(4):
                k = g * 4 + j
                fw.op(pe, lambda e: e.transpose(accb[:, j, :nb], xsb[:nb, k * 128:(k + 1) * 128], ident_b[:nb, :nb]),
                      reads=[Txsb, Tc], writes=[A.t0])
            fw.op(dve, lambda e: e.tensor_tensor(dstT[:, g * 4:(g + 1) * 4, t0:t0 + nb], accb[:, :, :nb],
                                                 gvec[:, g * 4:(g + 1) * 4, None].broadcast_to([128, 4, nb]), ALU.mult),
                  reads=[A.t0, Tp], writes=[tdst])

    def build_hT_gen(segs):
        for sg in segs:
            for (o, nb) in blocks_of(sg.n):
                fw.dma(sp, d_x, xin[:nb], sg.xsrc[sg.row0 + o:sg.row0 + o + nb, :], writes=[Txin])
                norm_block_to_T(xin, Txin, nb, gpm, hT, ThT, sg.t0 + o)
                yield

    def build_hT(segs):
        for _ in build_hT_gen(segs):
            pass

    def conv_segs_gen(cv, pr, K, wfn, widx, segs, hist_of, fill):
        K1 = K - 1
        base = 0
        bases = []
        for sg in segs:
            hb, th = hist_of(ST[sg.sidx])
            fw.op(pool, lambda e: e.tensor_copy(pr.a[:, base:base + K1], hb[:, widx, :]), reads=[th], writes=[pr.t])
            fill(pr.a[:, base + K1:base + K1 + sg.n], sg.t0, sg.n)
            fw.op(pool, lambda e: e.tensor_copy(hb[:, widx, :], pr.a[:, base + sg.nreal:base + sg.nreal + K1]), reads=[pr.t], writes=[th])
            bases.append(base)
            base += K1 + sg.n
        yield
        for sg, base in zip(segs, bases):
            d = cv.a[:, sg.t0:sg.t0 + sg.n]
            fw.op(dve, lambda e: e.tensor_scalar(d, pr.a[:, base:base + sg.n], wfn(widx, 0), None, ALU.mult), reads=[pr.t, Tp], writes=[cv.t])
            for j in range(1, K):
                fw.op(dve, lambda e: e.scalar_tensor_tensor(d, pr.a[:, base + j:base + j + sg.n], wfn(widx, j), d, ALU.mult, ALU.add),
                      reads=[pr.t, cv.t, Tp], writes=[cv.t])
        yield

    def conv_segs(*args):
        for _ in conv_segs_gen(*args):
            pass

    def to_token_major(srcT, o, nb, dst_tile, tdst):
        for g in range(4):
            A = next_acc()
            a3 = A.a[:, 0:512].rearrange("p (j t) -> p j t", j=4)
            for j in range(4):
                m = g * 4 + j
                fw.op(pe, lambda e: e.transpose(a3[:nb, j, :], srcT.a[:, m, o:o + nb], ident_f[:, :]),
                      reads=[srcT.t, Tc], writes=[A.t0])
            fw.op(act, lambda e: e.activation(junk[:nb, 0:512], A.a[:nb, 0:512], AF.Square, accum_out=t_ssc.a[:nb, g:g + 1]),
                  reads=[A.t0], writes=[Tjunk, t_ssc.t])
            fw.op(dve, lambda e: e.tensor_copy(dst_tile[:nb, g * 512:(g + 1) * 512], A.a[:nb, 0:512]), reads=[A.t0], writes=[tdst])

    def bc(ap2, n):
        return ap2[:, :, None].broadcast_to([ap2.shape[0], ap2.shape[1], n])

    s1ctr = [0]

    def s1_acc():
        i = s1ctr[0] % 2
        s1ctr[0] += 1
        return Acc(i)

    front_done = set()
    WARM = 0

    def s1_front(t0, gci, pb):
        c64 = slice(t0, t0 + 64)
        beta, gg, t8 = sm_beta[pb], sm_gg[pb], sm_t8[pb]
        AL = s1_acc()
        for k in range(KD):
            fw.op(pe, lambda e: e.matmul(AL.a[:64, 0:16], hT[:, k, c64], wl[:, k, :], start=(k == 0), stop=(k == KD - 1)),
                  reads=[ThT, Twl], writes=[AL.t0])
        mcol = tm[:, gci:gci + 1]
        fw.op(act, lambda e: e.activation(beta.a, AL.a[:64, 0:8], AF.Exp, scale=-1.0), reads=[AL.t0], writes=[beta.t])
        fw.op(dve, lambda e: e.tensor_tensor(t8.a, AL.a[:64, 8:16], dtb_r[:64], ALU.add), reads=[AL.t0, Tp], writes=[t8.t])
        fw.op(act, lambda e: e.activation(t8.a, t8.a, AF.Exp), reads=[t8.t], writes=[t8.t])
        fw.op(dve, lambda e: e.tensor_scalar(beta.a, beta.a, 1.0, None, ALU.add), reads=[beta.t], writes=[beta.t])
        fw.op(act, lambda e: e.activation(t8.a, t8.a, AF.Ln, bias=one_c[:64]), reads=[t8.t, Tc], writes=[t8.t])
        fw.op(dve, lambda e: e.reciprocal(beta.a, beta.a), reads=[beta.t], writes=[beta.t])
        fw.op(dve, lambda e: e.tensor_scalar(beta.a, beta.a, mcol, None, ALU.mult), reads=[beta.t, Tp], writes=[beta.t])
        fw.op(dve, lambda e: e.scalar_tensor_tensor(gg.a, t8.a, mcol, nA_r[:64], ALU.mult, ALU.mult), reads=[t8.t, Tp], writes=[gg.t])
        sp_, rs_ = sm_split[pb], sm_res[pb]
        fw.op(dve, lambda e: e.tensor_copy(sp_.a[:, 0, 0:8], gg.a), reads=[gg.t], writes=[sp_.t])
        fw.op(dve, lambda e: e.tensor_copy(sp_.a[:, 0, 8:16], beta.a), reads=[beta.t, sp_.t], writes=[sp_.t])
        fw.op(dve, lambda e: e.tensor_tensor(rs_.a[:, 0:8], gg.a, sp_.a[:, 0, 0:8], ALU.subtract), reads=[gg.t, sp_.t], writes=[rs_.t])
        fw.op(dve, lambda e: e.tensor_tensor(rs_.a[:, 8:16], beta.a, sp_.a[:, 0, 8:16], ALU.subtract), reads=[beta.t, sp_.t, rs_.t], writes=[rs_.t])
        fw.op(dve, lambda e: e.tensor_copy(sp_.a[:, 1, :], rs_.a), reads=[rs_.t, sp_.t], writes=[sp_.t])
        front_done.add(gci)

    def gdn_stage1(t0, gci, full, pb, nxt):
        c64 = slice(t0, t0 + 64)
        beta, gg, t8, egc, edl, cb_, egl, gcs = sm_beta[pb], sm_gg[pb], sm_t8[pb], sm_egc[pb], sm_edl[pb], sm_cb[pb], sm_egl[pb], sm_gc[pb]
        if gci not in front_done:
            s1_front(t0, gci, pb)
        sp_ = sm_split[pb]
        A0 = s1_acc()
        g2 = sp_.a[:, :, 0:8]
        b2 = sp_.a[:, :, 8:16]

        def bc4(m8, v2):
            return (m8[:, None, :, :].broadcast_to([64, 2, 8, 64]), v2[:, :, :, None].broadcast_to([64, 2, 8, 64]))
        fw.op(dve, lambda e: e.tensor_tensor(rhsG.a, *bc4(U8, g2), ALU.mult), reads=[sp_.t, Tc], writes=[rhsG.t])
        fw.op(dve, lambda e: e.tensor_tensor(rhsL.a, *bc4(L8, g2), ALU.mult), reads=[sp_.t, Tc], writes=[rhsL.t])
        fw.op(dve, lambda e: e.tensor_tensor(rhsB.a, *bc4(I8, b2), ALU.mult), reads=[sp_.t, Tc], writes=[rhsB.t])
        yield
        A1 = s1_acc()
        fl = "p h t -> p (h t)"
        fl4 = "p h t -> p (h t)"

        def mm2(out, lhsT, rb, neg, tw):
            for part in range(2):
                fw.op(pe, lambda e: e.matmul(out, lhsT, rb.a[:, part].rearrange(fl4), start=(part == 0), stop=(part == 1 and neg is None)),
                      reads=[rb.t, Tc], writes=[tw])
            if neg is not None:
                fw.op(pe, lambda e: e.matmul(out, ident_b[:64, :64], neg[:].rearrange(fl4), start=False, stop=True), reads=[Tc], writes=[tw])
        fw.op(pe, lambda e: e.matmul(A0.a[:64, 16:24], Umat, gg.a, start=True, stop=True), reads=[gg.t, Tc], writes=[A0.t0])
        fw.op(pe, lambda e: e.matmul(A0.a[:, 24:32], ones_f[:, :], gg.a, start=True, stop=True), reads=[gg.t, Tc], writes=[A0.t0])
        mm2(A1.a[:64, 0:512], Lmat_b, rhsG, NEGU, A1.t0)
        mm2(A1.a[:64, 512:1024], Umat_b, rhsL, NEGL, A1.t1)
        mm2(A0.a[:, 512:1024], ones_b[:64, :], rhsB, None, A0.t1)
        yield
        fw.op(act, lambda e: e.activation(egc.a, A0.a[:64, 16:24], AF.Exp), reads=[A0.t0], writes=[egc.t])
        fw.op(act, lambda e: e.copy(gcs.a, A0.a[:64, 16:24]), reads=[A0.t0], writes=[gcs.t])
        fw.op(act, lambda e: e.activation(egl.a, A0.a[:, 24:32], AF.Exp), reads=[A0.t0], writes=[egl.t])
        fw.op(dve, lambda e: e.tensor_tensor(edl.a, A0.a[:64, 24:32], gcs.a, ALU.subtract), reads=[A0.t0, gcs.t], writes=[edl.t])
        fw.op(act, lambda e: e.activation(edl.a, edl.a, AF.Exp), reads=[edl.t], writes=[edl.t])
        fw.op(dve, lambda e: e.tensor_tensor(cb_.a, beta.a, egc.a, ALU.mult), reads=[beta.t, egc.t], writes=[cb_.t])
        if full:
            mm2(A0.a[:, 0:512], ones_b[:64, :], rhsG, None, A0.t0)
        fw.op(act, lambda e: e.activation(DTs.a, A1.a[:64, 0:512].rearrange(h3, h=8), AF.Exp), reads=[A1.t0], writes=[DTs.t])
        fw.op(dve, lambda e: e.tensor_tensor(kbT.a, knT.a[:, :, c64], A0.a[:, 512:1024].rearrange(h3, h=8), ALU.mult),
              reads=[knT.t, A0.t1], writes=[kbT.t])
        fw.op(act, lambda e: e.activation(Dm.a, A1.a[:64, 512:1024].rearrange(h3, h=8), AF.Exp), reads=[A1.t1], writes=[Dm.t])
        if full:
            fw.op(act, lambda e: e.activation(EG.a, A0.a[:, 0:512].rearrange(h3, h=8), AF.Exp), reads=[A0.t0], writes=[EG.t])
            fw.op(dve, lambda e: e.tensor_tensor(DTi.a, DTs.a, I8[:], ALU.add), reads=[DTs.t, Tc], writes=[DTi.t])
            fw.op(dve, lambda e: e.tensor_tensor(qgT[pb].a, qnT.a[:, :, c64], EG.a, ALU.mult), reads=[qnT.t, EG.t], writes=[qgT[pb].t])
        yield
        B2 = s1_acc()
        b2b = B2.a[:].bitcast(BF16)
        ktok = b2b[:64, 0:1024].rearrange("p (h d) -> p h d", h=8)
        vtok = b2b[:64, 1024:2048].rearrange("p (h d) -> p h d", h=8)
        for h in range(NH):
            fw.op(pe, lambda e: e.transpose(ktok[:, h, :], knT.a[:, h, c64], ident_b[:, :]), reads=[knT.t, Tc], writes=[B2.t0])
        for h in range(NH):
            fw.op(pe, lambda e: e.transpose(vtok[:, h, :], vT.a[:, h, c64], ident_b[:, :]), reads=[vT.t, Tc], writes=[B2.t1])
        fw.op(act, lambda e: e.copy(ktk[pb].a, ktok), reads=[B2.t0], writes=[ktk[pb].t])
        fw.op(act, lambda e: e.copy(vtk[pb].a, vtok), reads=[B2.t1], writes=[vtk[pb].t])
        yield
        B0 = s1_acc()
        for h in range(NH):
            hs = slice(h * 64, (h + 1) * 64)
            fw.op(pe, lambda e: e.matmul(B0.a[:64, hs], knT.a[:, h, c64], kbT.a[:, h, :], start=True, stop=True), reads=[knT.t, kbT.t], writes=[B0.t0])
        for h in range(NH):
            fw.op(pe, lambda e: e.matmul(B0.a[:64, 512 + h * 64:512 + (h + 1) * 64], kbT.a[:, h, :], knT.a[:, h, c64], start=True, stop=True),
                  reads=[knT.t, kbT.t], writes=[B0.t1])
        fw.op(dve, lambda e: e.tensor_tensor(PTm[0].a, B0.a[:64, 0:512].rearrange(h3, h=8), DTs.a, ALU.mult), reads=[B0.t0, DTs.t], writes=[PTm[0].t])
        fw.op(dve, lambda e: e.tensor_tensor(Pm[0].a, B0.a[:64, 512:1024].rearrange(h3, h=8), Dm.a, ALU.mult), reads=[B0.t1, Dm.t], writes=[Pm[0].t])
        fw.op(dve, lambda e: e.tensor_tensor(TT.a, I8[:], PTm[0].a, ALU.subtract), reads=[PTm[0].t, Tc], writes=[TT.t])
        yield
        if full:
            B1 = s1_acc()
            for h in range(NH):
                hs = slice(h * 64, (h + 1) * 64)
                fw.op(pe, lambda e: e.matmul(B1.a[:64, hs], knT.a[:, h, c64], qnT.a[:, h, c64], start=True, stop=True), reads=[knT.t, qnT.t], writes=[B1.t0])
            fw.op(dve, lambda e: e.tensor_tensor(attnT[pb].a, B1.a[:64, 0:512].rearrange(h3, h=8), DTi.a, ALU.mult),
                  reads=[B1.t0, DTi.t], writes=[attnT[pb].t])
        if nxt is not None and full:
            s1_front(nxt[0], nxt[1], 1 - pb)
        cur = 0
        for lvl in range(5):
            nxt = 1 - cur
            C0 = s1_acc()
            for h in range(NH):
                hs = slice(h * 64, (h + 1) * 64)
                fw.op(pe, lambda e: e.matmul(C0.a[:64, hs], PTm[cur].a[:, h, :], Pm[cur].a[:, h, :], start=True, stop=True),
                      reads=[PTm[cur].t, Pm[cur].t], writes=[C0.t0])
            if lvl < 4:
                for h in range(NH):
                    fw.op(pe, lambda e: e.matmul(C0.a[:64, 512 + h * 64:512 + (h + 1) * 64], Pm[cur].a[:, h, :], PTm[cur].a[:, h, :], start=True, stop=True),
                          reads=[PTm[cur].t, Pm[cur].t], writes=[C0.t1])
            C1 = s1_acc()
            for _ in range(WARM):
                fw.op(pe, lambda e: e.matmul(C1.a[:, 512:1024], ones_b[:, :], hT[:, 0, 0:512], start=True, stop=True), reads=[ThT, Tc], writes=[C1.t1])
            fw.op(act, lambda e: e.copy(Pm[nxt].a, C0.a[:64, 0:512].rearrange(h3, h=8)), reads=[C0.t0], writes=[Pm[nxt].t])
            if lvl < 4:
                fw.op(dve, lambda e: e.tensor_copy(PTm[nxt].a, C0.a[:64, 512:1024].rearrange(h3, h=8)), reads=[C0.t1], writes=[PTm[nxt].t])
            yield
            for h in range(NH):
                hs = slice(h * 64, (h + 1) * 64)
                fw.op(pe, lambda e: e.matmul(C1.a[:64, hs], Pm[nxt].a[:, h, :], TT.a[:, h, :], start=True, stop=True),
                      reads=[Pm[nxt].t, TT.t], writes=[C1.t0])
            fw.op(dve, lambda e: e.tensor_tensor(TT.a, TT.a, C1.a[:64, 0:512].rearrange(h3, h=8), ALU.add), reads=[C1.t0, TT.t], writes=[TT.t])
            cur = nxt
        yield
        fw.op(dve, lambda e: e.tensor_tensor(TTc.a, TT.a, bc(cb_.a, 64), ALU.mult), reads=[TT.t, cb_.t], writes=[TTc.t])
        fw.op(dve, lambda e: e.tensor_tensor(TTb[pb].a, TT.a, bc(beta.a, 64), ALU.mult), reads=[TT.t, beta.t], writes=[TTb[pb].t])
        D0 = s1_acc()
        for h in range(NH):
            hs = slice(h * 64, (h + 1) * 64)
            fw.op(pe, lambda e: e.matmul(D0.a[:, hs], ktk[pb].a[:, h, :], TTc.a[:, h, :], start=True, stop=True), reads=[ktk[pb].t, TTc.t], writes=[D0.t0])
        fw.op(act, lambda e: e.mul(negwT[pb].a, D0.a[:, 0:512].rearrange(h3, h=8), -1.0), reads=[D0.t0], writes=[negwT[pb].t])
        yield

    def gdn_stage2(t0, full, pb, st):
        c64 = slice(t0, t0 + 64)
        S, Sbf, TS, TSbf = st.S, st.Sbf, st.TS, st.TSbf
        edl, egl = sm_edl[pb], sm_egl[pb]
        D1 = Acc(2)
        for ha in range(4):
            pair = (ha, ha + 4)
            for h in pair:
                es = slice(h * 128, (h + 1) * 128)
                tt_ = D1.t0 if h < 4 else D1.t1
                fw.op(pe, lambda e: e.matmul(D1.a[:64, es], TTb[pb].a[:, h, :], vtk[pb].a[:, h, :], start=True, stop=False), reads=[TTb[pb].t, vtk[pb].t], writes=[tt_])
            for h in pair:
                es = slice(h * 128, (h + 1) * 128)
                tt_ = D1.t0 if h < 4 else D1.t1
                fw.op(pe, lambda e: e.matmul(D1.a[:64, es], negwT[pb].a[:, h, :], Sbf[:, h, :], start=False, stop=True), reads=[negwT[pb].t, TSbf], writes=[tt_])
        yield
        fw.op(pool, lambda e: e.tensor_tensor(S[:], S[:], bc(egl.a, 128), ALU.mult), reads=[egl.t, TS], writes=[TS])
        d13 = D1.a[:64, :].rearrange("p (h d) -> p h d", h=8)
        if full:
            fw.op(act, lambda e: e.copy(vnew.a, d13), reads=D1.tt, writes=[vnew.t])
        fw.op(dve, lambda e: e.tensor_tensor(vnews.a, d13, bc(edl.a, 128), ALU.mult), reads=D1.tt + [edl.t], writes=[vnews.t])
        yield
        D3 = Acc(2)
        for h in range(NH):
            es = slice(h * 128, (h + 1) * 128)
            fw.op(pe, lambda e: e.matmul(D3.a[:, es], ktk[pb].a[:, h, :], vnews.a[:, h, :], start=True, stop=True), reads=[ktk[pb].t, vnews.t],
                  writes=[D3.t0 if h < 4 else D3.t1])
        if full:
            D2 = Acc(3)
            for h in range(NH):
                hs = slice(h * 64, (h + 1) * 64)
                fw.op(pe, lambda e: e.matmul(D2.a[:, hs], Sbf[:, h, :], qgT[pb].a[:, h, :], start=True, stop=False), reads=[TSbf, qgT[pb].t], writes=[D2.t0])
                fw.op(pe, lambda e: e.matmul(D2.a[:, hs], vnew.a[:, h, :], attnT[pb].a[:, h, :], start=False, stop=True), reads=[vnew.t, attnT[pb].t], writes=[D2.t0])
        yield
        d33 = D3.a[:, :].rearrange("p (h d) -> p h d", h=8)
        fw.op(dve, lambda e: e.tensor_tensor(Sbf[:], S[:], d33, ALU.add), reads=[TS] + D3.tt, writes=[TSbf])
        fw.op(dve, lambda e: e.tensor_tensor(S[:], S[:], d33, ALU.add), reads=[TS] + D3.tt, writes=[TS])
        yield
        if full:
            o3 = D2.a[:, 0:512].rearrange(h3, h=8)
            fw.op(act, lambda e: e.activation(osq.a, o3, AF.Square), reads=[D2.t0], writes=[osq.t])
            E0 = Acc(3)
            fw.op(pe, lambda e: e.matmul(E0.a[:, 512:1024], ones_b[:, :], osq.a.rearrange("p h t -> p (h t)"), start=True, stop=True),
                  reads=[osq.t, Tc], writes=[E0.t1])
            yield
            rsqrt_big(rso.a, E0.a[:, 512:1024].rearrange(h3, h=8), 1.0 / DK, [E0.t1], [rso.t])
            fw.op(dve, lambda e: e.scalar_tensor_tensor(otmp.a, o3, gng, rso.a, ALU.mult, ALU.mult), reads=[D2.t0, Tp, rso.t], writes=[otmp.t])
            fw.op(pool, lambda e: e.tensor_tensor(yT[:, 8:16, c64], otmp.a, szT.a[:, :, c64], ALU.mult), reads=[otmp.t, szT.t], writes=[TyT])
            yield

    def gdn_chunks(chunks, full, tail_gen=None):
        fw.barrier()
        prev = None
        for ci, (t0, gci, sidx) in enumerate(chunks):
            pb = ci % 2
            cast_pump(3)
            nxt = (chunks[ci + 1][0], chunks[ci + 1][1]) if ci + 1 < len(chunks) else None
            extra = None
            if tail_gen is not None and ci == len(chunks) - 1:
                acc_pool[:] = [3]
                extra = tail_gen
            run_interleaved([gdn_stage1(t0, gci, full, pb, nxt), prev, extra])
            if extra is not None:
                acc_pool[:] = [0, 1, 2, 3]
            prev = gdn_stage2(t0, full, pb, ST[sidx])
        run_interleaved([prev])

    def gdn_heads(Nt, full, segs, pre_hook=None):
        cnt = [0]

        def bufs():
            i = cnt[0] % 2
            cnt[0] += 1
            return pre[i], cvb[i], qsb[i], sqbb[i]

        def proj(ub, tub, j):
            A = next_acc()
            mm_chunk(A, ub, tub, j, hT, ThT, Nt, KD)
            return A

        def conv_gen(A, ci, pr, cv):
            yield from conv_segs_gen(cv, pr, 4, wcg, ci, segs, lambda st: (st.hG, st.ThG),
                                     lambda dst, t0, n: fw.op(act, lambda e: e.copy(dst, A.a[:, t0:t0 + n]), reads=A.ts(Nt), writes=[pr.t]))

        def norm_gen(A, ci, pr, cv, qs, sq, out):
            yield from conv_gen(A, ci, pr, cv)
            fw.op(act, lambda e: e.activation(qs.a[:, :Nt], cv.a[:, :Nt], AF.Silu), reads=[cv.t], writes=[qs.t])
            fw.op(act, lambda e: e.activation(sq.a[:, :Nt], qs.a[:, :Nt], AF.Square), reads=[qs.t], writes=[sq.t])
            A2 = next_acc()
            for (o, n) in nsplit(Nt):
                fw.op(pe, lambda e: e.matmul(A2.a[:, o:o + n], ones_b[:, :], sq.a[:, o:o + n], start=True, stop=True),
                      reads=[sq.t, Tc], writes=[A2.t0 if o < 512 else A2.t1])
            out.append(A2)
            yield

        def norm_p2(A2, cv, qs, dst, h, sc):
            rsqrt_big(cv.a[:, :Nt], A2.a[:, :Nt], 1.0, A2.ts(Nt), [cv.t])
            fw.op(dve, lambda e: e.scalar_tensor_tensor(dst.a[:, h, :Nt], qs.a[:, :Nt], sc, cv.a[:, :Nt], ALU.mult, ALU.mult),
                  reads=[qs.t, cv.t], writes=[dst.t])

        seq = []
        for hp in range(4):
            if full:
                for j in range(2):
                    seq.append([("q", hp, j), ("k", hp, j)])
            else:
                seq.append([("k", hp, 0), ("k", hp, 1)])
            for j in range(2):
                seq.append([("v", hp, j)] + ([("z", hp, j)] if full else []))
        ubase = {"q": 12, "k": 16, "v": 20, "z": 24}
        units = {}

        def stage_a(grp):
            res = []
            for (kind, hp, j) in grp:
                if (kind, hp) not in units:
                    units[(kind, hp)] = load_unit(wi_b[:, (ubase[kind] + hp) * 256:(ubase[kind] + hp + 1) * 256], KD, Twi_rest if kind == "q" else Twi_kv)
                ub, tub = units[(kind, hp)]
                res.append(proj(ub, tub, j))
            return res

        nxt = stage_a(seq[0])
        if pre_hook is not None:
            pre_hook()
        for gi, grp in enumerate(seq):
            accs_ = nxt
            if grp[0][0] != "v" and gi + 1 < len(seq):
                pass
            if grp[0][0] in ("q", "k"):
                gens = []
                todo = []
                for (kind, hp, j), A in zip(grp, accs_):
                    h = hp * 2 + j
                    pr, cv, qs, sq = bufs()
                    cidx0, dst, sc = (0, qnT, DK ** -0.5) if kind == "q" else (8, knT, 1.0)
                    out = []
                    gens.append(norm_gen(A, cidx0 + h, pr, cv, qs, sq, out))
                    todo.append((out, cv, qs, dst, h, sc))
                run_interleaved(gens)
                if gi + 1 < len(seq):
                    nxt = stage_a(seq[gi + 1])
                for (out, cv, qs, dst, h, sc) in todo:
                    norm_p2(out[0], cv, qs, dst, h, sc)
            else:
                (kind, hp, j) = grp[0]
                h = hp * 2 + j
                pr, cv, qs, sq = bufs()
                A = accs_[0]
                run_interleaved([conv_gen(A, 16 + h, pr, cv)])
                fw.op(act, lambda e: e.activation(vT.a[:, h, :Nt], cv.a[:, :Nt], AF.Silu), reads=[cv.t], writes=[vT.t])
                if full:
                    Az = accs_[1]
                    fw.op(act, lambda e: e.activation(szT.a[:, h, :Nt], Az.a[:, :Nt], AF.Silu), reads=Az.ts(Nt), writes=[szT.t])
                if gi + 1 < len(seq):
                    nxt = stage_a(seq[gi + 1])

    def group_a(Nt, segs):
        fw.barrier()
        ssA = Acc(3)
        cnt = 0
        for cp in range(4):
            u_h, t_h = load_unit(wi_b[:, cp * 256:(cp + 1) * 256], KD, Twi_rest)
            u_c, t_c = load_unit(wi_b[:, (4 + cp) * 256:(5 + cp) * 256], KD, Twi_rest)
            u_b, t_b = load_unit(wi_b[:, (8 + cp) * 256:(9 + cp) * 256], KD, Twi_rest)
            for j in range(2):
                c = cp * 2 + j
                a_h, a_c, a_b = Acc(0), Acc(1), Acc(2)
                mm_chunk(a_h, u_h, t_h, j, hT, ThT, Nt, KD)
                mm_chunk(a_c, u_c, t_c, j, hT, ThT, Nt, KD)
                mm_chunk(a_b, u_b, t_b, j, hT, ThT, Nt, KD)
                pr = pre[cnt % 2]; cv = cvb[0]; cnt += 1
                fw.op(act, lambda e: e.copy(ahs.a[:, :Nt], a_h.a[:, :Nt]), reads=a_h.ts(Nt), writes=[ahs.t])
                conv_segs(cv, pr, 3, wca, c, segs, lambda st: (st.hA, st.ThA),
                          lambda dst, t0, n: fw.op(dve, lambda e: e.tensor_tensor(dst, a_c.a[:, t0:t0 + n], ahs.a[:, t0:t0 + n], ALU.mult),
                                                   reads=a_c.ts(Nt) + [ahs.t], writes=[pr.t]))
                fw.op(dve, lambda e: e.tensor_tensor(yraw.a[:, c, :Nt], a_b.a[:, :Nt], cv.a[:, :Nt], ALU.mult), reads=a_b.ts(Nt) + [cv.t], writes=[yraw.t])
                fw.op(act, lambda e: e.activation(sqb.a[:, :Nt], yraw.a[:, c, :Nt], AF.Square), reads=[yraw.t], writes=[sqb.t])
                for (o, n) in nsplit(Nt):
                    fw.op(pe, lambda e: e.matmul(ssA.a[:, o:o + n], ones_b[:, :], sqb.a[:, o:o + n], start=(c == 0), stop=(c == 7)),
                          reads=[sqb.t, Tc], writes=[ssA.t0 if o < 512 else ssA.t1])
        acc_ctr[0] = 0

        def tail():
            rsqrt_cols(ahs.a[:, :Nt], ssA.a[:, :Nt], 1.0 / CW, ssA.ts(Nt), [ahs.t])
            for c in range(8):
                fw.op(dve, lambda e: e.scalar_tensor_tensor(yT[:, c, :Nt], yraw.a[:, c, :Nt], gna[:, c:c + 1], ahs.a[:, :Nt], ALU.mult, ALU.mult),
                      reads=[yraw.t, ahs.t, Tp], writes=[TyT])
        return tail

    Tx1 = T("x1s")

    def out_and_ffn(Nt, segs):
        fw.barrier()
        blks = []
        for sg in segs:
            for (o, nb) in blocks_of(sg.n):
                blks.append((sg.t0 + o, nb))
        acc_pool[:] = [0, 1, 2]
        SSP = Acc(3)

        def proj_evac(A, m, gvec):
            sq = sqb2[m % 2]
            fw.op(act, lambda e: e.activation(sq.a[:, :Nt], A.a[:, :Nt], AF.Square), reads=A.ts(Nt), writes=[sq.t])
            fw.op(act, lambda e: e.activation(fT.a[:, m, :Nt], A.a[:, :Nt], AF.Copy, scale=gvec[:, m:m + 1]), reads=A.ts(Nt) + [Tp], writes=[fT.t])
            for bi, (o, nb) in enumerate(blks):
                if m == 0 and bi == 0:
                    fw.op(pe, lambda e: e.matmul(SSP.a[:nb, 512:517], sq.a[:, o:o + nb], e5[:, 0:5], start=True, stop=False),
                          reads=[sq.t, Tc], writes=[SSP.t1])
                else:
                    fw.op(pe, lambda e: e.matmul(SSP.a[:nb, 512 + bi:513 + bi], sq.a[:, o:o + nb], ones_b[:, 0:1], start=False,
                                                 stop=(m == KD - 1 and bi == len(blks) - 1)),
                          reads=[sq.t, Tc], writes=[SSP.t1])

        def token_major_resid(bi, o, nb, res=xin, tres=Txin):
            rsqrt_cols(t_rs.a[:nb], SSP.a[:nb, 512 + bi:513 + bi], 1.0 / D_MODEL, [SSP.t1], [t_rs.t])
            for g in range(4):
                A = next_acc()
                a3 = A.a[:, 0:512].rearrange("p (j t) -> p j t", j=4)
                for j in range(4):
                    m = g * 4 + j
                    fw.op(pe, lambda e: e.transpose(a3[:nb, j, :], fT.a[:, m, o:o + nb], ident_f[:, :]), reads=[fT.t, Tc], writes=[A.t0])
                fw.op(dve, lambda e: e.scalar_tensor_tensor(tok[:nb, g * 512:(g + 1) * 512], A.a[:nb, 0:512], t_rs.a[:nb],
                                                            res[:nb, g * 512:(g + 1) * 512], ALU.mult, ALU.add),
                      reads=[A.t0, t_rs.t, tres], writes=[Ttok])

        for up in range(8):
            ub, tub = load_unit(wo_b[:, up * 256:(up + 1) * 256], KD, Two)
            for j in range(2):
                m = up * 2 + j
                A = next_acc()
                mm_chunk(A, ub, tub, j, yT, TyT, Nt, KD)
                proj_evac(A, m, gqm)
        bsrc = []
        for sg in segs:
            for (o, nb) in blocks_of(sg.n):
                bsrc.append((sg, o))
        for bi, (o, nb) in enumerate(blks):
            sg, so = bsrc[bi]
            fw.dma(sp, d_x, xin[:nb], sg.xsrc[sg.row0 + so:sg.row0 + so + nb, :], writes=[Txin])
            token_major_resid(bi, o, nb)
            fw.dma(pool, d_x1, x1s[o:o + nb, :], tok[:nb], reads=[Ttok], writes=[Tx1])
            norm_block_to_T(tok, Ttok, nb, gpf, hT, ThT, o)
        cnt = 0
        for up in range(KF // 2):
            ug, tug = load_unit(wu_b[:, up * 256:(up + 1) * 256], KD, Twu)
            uv, tuv = load_unit(wu_b[:, D_FF + up * 256:D_FF + (up + 1) * 256], KD, Twu)
            for j in range(2):
                idx = up * 2 + j
                ag = next_acc(); av = next_acc()
                mm_chunk(ag, ug, tug, j, hT, ThT, Nt, KD)
                mm_chunk(av, uv, tuv, j, hT, ThT, Nt, KD)
                pr = pre2[0]; cv = cv2[0]; cnt += 1
                conv_segs(cv, pr, 3, wcf, idx, segs, lambda st: (st.hF, st.ThF),
                          lambda dst, t0, n: fw.op(act, lambda e: e.copy(dst, ag.a[:, t0:t0 + n]), reads=ag.ts(Nt), writes=[pr.t]))
                fw.op(act, lambda e: e.activation(cv.a[:, :Nt], cv.a[:, :Nt], AF.Silu), reads=[cv.t], writes=[cv.t])
                fw.op(dve, lambda e: e.tensor_tensor(actT.a[:, idx, :Nt], av.a[:, :Nt], cv.a[:, :Nt], ALU.mult), reads=av.ts(Nt) + [cv.t], writes=[actT.t])
        kparts = ((0, 16), (16, 16), (32, 12))
        for mp in range(8):
            a_m = [next_acc(), next_acc()]
            for pi, (k0, nk) in enumerate(kparts):
                ub, tub = load_unit(wd_b[k0 * 128:(k0 + nk) * 128, mp * 256:(mp + 1) * 256], nk, Twd)
                for j in range(2):
                    mm_chunk(a_m[j], ub, tub, j, actT.a, actT.t, Nt, nk, koff=k0, first=(pi == 0), last=(pi == 2))
            for j in range(2):
                proj_evac(a_m[j], mp * 2 + j, gqf)
        def final_gen():
            for bi, (o, nb) in enumerate(blks):
                fw.dma(sp, d_x1l, tok[:nb], x1s[o:o + nb, :], reads=[Tx1], writes=[Ttok])
                token_major_resid(bi, o, nb, tok, Ttok)
                sg, so = bsrc[bi]
                lo = max(0, sg.tok_lo - so)
                hi = min(nb, sg.tok_hi - so)
                if hi > lo:
                    r0 = sg.out_row0 + so + lo
                    fw.dma(pool, d_o, sg.out_ap[r0:r0 + hi - lo, :], tok[lo:hi], reads=[Ttok])
                yield
            acc_pool[:] = [0, 1, 2, 3]
        return final_gen()

    hT_prebuilt = [False]

    def tile_prefix(row0, Nt, gci0, next_segs):
        chk(1)
        segs = [Seg(0, Nt, Nt, 0, D["xpre"], row0)]
        if not hT_prebuilt[0]:
            build_hT(segs)
        hT_prebuilt[0] = False
        gdn_heads(Nt, False, segs)
        chk(3)
        gdn_chunks([(ci * 64, gci0 + ci, 0) for ci in range(Nt // 64)], False, build_hT_gen(next_segs))
        hT_prebuilt[0] = True
        chk(5)

    pending_final = [None]

    def tile_full(Nt, segs, chunks):
        chk(6)
        if hT_prebuilt[0]:
            hT_prebuilt[0] = False
        else:
            run_interleaved([pending_final[0], build_hT_gen(segs)])
        pending_final[0] = None
        ga_tail = group_a(Nt, segs)
        chk(7)
        gdn_heads(Nt, True, segs, ga_tail)
        chk(8)
        gdn_chunks(chunks, True)
        chk(11)
        pending_final[0] = out_and_ffn(Nt, segs)
        chk(10)

    stO = tok[:, 1024:1200]; TstO = Ttok
    stI = tok[:88, 1280:1536].rearrange("p (a c) -> p a c", a=2); TstI = Ttok

    def hist_specs(st):
        return ((st.hA, st.ThA, 2, 8, 0), (st.hG, st.ThG, 3, 24, 16), (st.hF, st.ThF, 2, KF, 88))

    def store_states(prefix, st):
        for (hb, th, K1, C, c0) in hist_specs(st):
            fw.op(pool, lambda e: e.tensor_copy(stO[:, c0:c0 + K1 * C].rearrange("p (t c) -> p c t", t=K1), hb[:]), reads=[th, TstO], writes=[TstO])
        A = next_acc()
        fw.op(pe, lambda e: e.transpose(A.a[:88, 0:128], stO[:, 0:88], ident_f[:, :]), reads=[TstO, Tc], writes=[A.t0])
        fw.op(pe, lambda e: e.transpose(A.a[:88, 128:256], stO[:, 88:176], ident_f[:, :]), reads=[TstO, Tc], writes=[A.t0])
        fw.op(dve, lambda e: e.tensor_copy(stI[:].rearrange("p a c -> p (a c)"), A.a[:88, 0:256]), reads=[A.t0, TstI], writes=[TstI])
        fw.dma(pool, d_st, D[prefix + "_nca"].rearrange("t (c p) -> (t c) p", p=128), stI[0:16, 0, :], reads=[TstI])
        fw.dma(pool, d_st, D[prefix + "_ngc"].rearrange("t (c p) -> (t c) p", p=128), stI[16:88, 0, :], reads=[TstI])
        fw.dma(pool, d_st, D[prefix + "_nfc"].rearrange("t (c p) -> (t c) p", p=128), stI[0:88, 1, :], reads=[TstI])
        fw.dma(pool, d_st, D[prefix + "_ngd"].rearrange("h d e -> d h e"), st.S[:], reads=[st.TS])
        fw.barrier()

    def load_states(st):
        d_ld = fw.dsem("ld")
        fw.dma(sp, d_ld, stI[0:16, 0, :], D["s_conv_a"].rearrange("t (c p) -> (t c) p", p=128), reads=[TstI], writes=[TstI])
        fw.dma(sp, d_ld, stI[16:88, 0, :], D["s_gdn_conv"].rearrange("t (c p) -> (t c) p", p=128), reads=[TstI], writes=[TstI])
        fw.dma(sp, d_ld, stI[0:88, 1, :], D["s_ffn_conv"].rearrange("t (c p) -> (t c) p", p=128), reads=[TstI], writes=[TstI])
        fw.dma(sp, fw.dsem("ldS"), st.S[:], D["s_gdn"].rearrange("h d e -> d h e"), reads=[st.TS], writes=[st.TS])
        A = next_acc()
        fw.op(pe, lambda e: e.transpose(A.a[:, 0:88], stI[:, 0, :], ident_f[:88, :88]), reads=[TstI, Tc], writes=[A.t0])
        fw.op(pe, lambda e: e.transpose(A.a[:, 88:176], stI[:, 1, :], ident_f[:88, :88]), reads=[TstI, Tc], writes=[A.t0])
        for (hb, th, K1, C, c0) in hist_specs(st):
            fw.op(dve, lambda e: e.tensor_copy(hb[:], A.a[:, c0:c0 + K1 * C].rearrange("p (t c) -> p c t", t=K1)), reads=[A.t0, th], writes=[th])
        fw.op(act, lambda e: e.copy(st.Sbf[:], st.S[:]), reads=[st.TS], writes=[st.TSbf])

    try:
        fw.dma(sp, fw.dsem("wl"), wl[:], wi_b[:, 7168:7184].rearrange("(k p) c -> p k c", p=128), reads=[Twi_kv], writes=[Twl])
        load_states(ST[1])
        p0 = ST[0]
        fw.op(pool, lambda e: e.memset(p0.hA[:], 0.0), writes=[p0.ThA])
        fw.op(pool, lambda e: e.memset(p0.hG[:], 0.0), writes=[p0.ThG])
        fw.op(pool, lambda e: e.memset(p0.hF[:], 0.0), writes=[p0.ThF])
        fw.op(pool, lambda e: e.memset(p0.S[:], 0.0), writes=[p0.TS])
        fw.op(pool, lambda e: e.memset(p0.Sbf[:], 0.0), writes=[p0.TSbf])
        gci = 0
        npre_t = NPRE // 512
        main0_segs = [Seg(0, 576, 576, 0, D["xmain"], 0, D["y"], -64, 64, 576)]
        for i in range(npre_t):
            nsegs = [Seg(0, 512, 512, 0, D["xpre"], (i + 1) * 512)] if i + 1 < npre_t else main0_segs
            tile_prefix(i * 512, 512, gci, nsegs)
            gci += 8
        cast_pump(10000)
        row = 0
        for ti, n_main in enumerate((576, 512, 512, 512)):
            segs = [Seg(0, n_main, n_main, 0, D["xmain"], row, D["y"], row - 64, max(0, 64 - row), n_main)]
            chunks = [(ci * 64, gci + ci, 0) for ci in range(n_main // 64)]
            Nt = n_main
            if ti == 3:
                segs.append(Seg(n_main, NSAMP, 16, 1, D["xsamp"], 0, D["ys"], 0, 0, 16))
                chunks.append((n_main, NCHUNK_ALL - 1, 1))
                Nt = n_main + NSAMP
            tile_full(Nt, segs, chunks)
            gci += n_main // 64
            row += n_main
        run_interleaved([pending_final[0]])
        fw.barrier()
        store_states("p", ST[0])
        store_states("s", ST[1])
    except _Stop:
        pass
    fw.finish()
    return nc, fw


_CACHE = {}


def kernel(x_prompt, x_sample, state_conv_a, state_gdn_conv, state_gdn, state_ffn_conv, meta_tokens,
           g_pre_mix, w_in, w_conv_a, g_norm_a, w_conv_gdn, a_log, dt_bias, g_norm_gdn, w_out, g_post_mix,
           g_pre_ffn, w_up, w_conv_ffn, w_down, g_post_ffn):
    f = lambda a: np.ascontiguousarray(np.asarray(a, dtype=np.float32))
    x_prompt, x_sample, meta = f(x_prompt), f(x_sample), f(meta_tokens)
    if "nc" not in _CACHE:
        _CACHE["nc"] = build_program()[0]
    nc = _CACHE["nc"]
    shared = {"g_pre_mix": f(g_pre_mix), "w_in": f(w_in[0]), "w_conv_a": f(w_conv_a[0]), "g_norm_a": f(g_norm_a),
              "w_conv_gdn": f(w_conv_gdn[0]), "a_log": f(a_log), "dt_bias": f(dt_bias), "g_norm_gdn": f(g_norm_gdn),
              "w_out": f(w_out[0]), "g_post_mix": f(g_post_mix), "g_pre_ffn": f(g_pre_ffn), "w_up": f(w_up[0]),
              "w_conv_ffn": f(w_conv_ffn[0]), "w_down": f(w_down[0]), "g_post_ffn": f(g_post_ffn)}
    in_maps = []
    zeros48 = np.zeros((48, D_MODEL), np.float32)
    for c in range(8):
        b, half = c // 2, c % 2
        full = np.concatenate([zeros48, meta, x_prompt[b]], axis=0)
        tmask = np.ones((NCHUNK_ALL, 64), np.float32)
        if half == 0:
            xpre = np.zeros((NPRE, D_MODEL), np.float32)
            xmain = full[0:NMAIN]
            tmask[32, :48] = 0.0
        else:
            xpre = full[0:NPRE]
            xmain = full[NPRE:NPRE + NMAIN]
            tmask[0, :48] = 0.0
        xsamp = np.concatenate([x_sample[c], np.zeros((48, D_MODEL), np.float32)], axis=0)
        tmask[65, 16:] = 0.0
        m = dict(shared)
        m.update({"xpre": np.ascontiguousarray(xpre), "xmain": np.ascontiguousarray(xmain), "xsamp": xsamp, "tmask": tmask,
                  "s_conv_a": f(state_conv_a[0, c]), "s_gdn_conv": f(state_gdn_conv[0, c]), "s_gdn": f(state_gdn[0, c]),
                  "s_ffn_conv": f(state_ffn_conv[0, c])})
        in_maps.append(m)
    res = run_bass_kernel_spmd(nc, in_maps, core_ids=list(range(8)))
    R = res.results
    y_prompt = np.stack([np.concatenate([R[2 * b]["y"], R[2 * b + 1]["y"]], axis=0) for b in range(4)])
    y_sample = np.stack([R[c]["ys"] for c in range(8)])

    def st(key, cores):
        return np.stack([R[c][key] for c in cores])[None]
    pc = [1, 3, 5, 7]
    sc = list(range(8))
    return (y_prompt, y_sample,
            st("p_nca", pc), st("p_ngc", pc), st("p_ngd", pc), st("p_nfc", pc),
            st("s_nca", sc), st("s_ngc", sc), st("s_ngd", sc), st("s_nfc", sc))
```
